# Optimizing a Trainium2 kernel written in Bass

```python
import jax, jax.numpy as jnp
from jax import lax
import numpy as np

D_MODEL = 1024
BATCH = 8
SEQ = 4096
DEPTH = 1

N_HEADS = 8
HEAD_DIM = 64
N_KV = 2
GROUP = N_HEADS // N_KV
ROT_DIM = HEAD_DIM // 4
ROPE_THETA = 500000.0
CMP_BLOCK = 32
CMP_STRIDE = 16
CMP_HIDDEN = 4 * HEAD_DIM
SEL_BLOCK = 64
SEL_TOPK = 16
WINDOW = 512
Q_BLOCK = 128
N_NSA_BRANCHES = 3
ATTN_WIDTH = N_HEADS * HEAD_DIM
KV_WIDTH = N_KV * HEAD_DIM
POOL_WIDTH = 512
POOL_WINDOWS = (2, 4, 8, 16)
POOL_GROUP = POOL_WIDTH // len(POOL_WINDOWS)
D_FF = 4 * D_MODEL
IN_SIZES = (ATTN_WIDTH, 6 * KV_WIDTH, N_NSA_BRANCHES * N_HEADS, POOL_WIDTH, 2 * D_MODEL)
IN_WIDTH = sum(IN_SIZES)
EPS = 1e-6
NEG_INF = -1e30
FORCE_SCORE = 1e4

kernel_name = "nsa_pool_gated_hybrid_block"


def rms_norm(x, g):
    xf = x.astype(jnp.float32)
    y = xf * lax.rsqrt(jnp.mean(xf * xf, axis=-1, keepdims=True) + EPS)
    return (y * g.astype(jnp.float32)).astype(x.dtype)


def rope(x, pos):
    inv = ROPE_THETA ** (-jnp.arange(0, ROT_DIM, 2, dtype=jnp.float32) / ROT_DIM)
    ang = pos.astype(jnp.float32)[:, None] * inv
    cos, sin = jnp.cos(ang), jnp.sin(ang)
    xr = x[..., :ROT_DIM].astype(jnp.float32)
    x1, x2 = xr[..., :ROT_DIM // 2], xr[..., ROT_DIM // 2:]
    rot = jnp.concatenate([x1 * cos - x2 * sin, x2 * cos + x1 * sin], axis=-1)
    return jnp.concatenate([rot.astype(x.dtype), x[..., ROT_DIM:]], axis=-1)


def masked_softmax(s, mask):
    s = jnp.where(mask, s.astype(jnp.float32), NEG_INF)
    e = jnp.exp(s - jnp.max(s, axis=-1, keepdims=True)) * mask
    return e / jnp.maximum(jnp.sum(e, axis=-1, keepdims=True), 1e-30)


def cmp_to_sel_matrix(S):
    n_cmp = (S - CMP_BLOCK) // CMP_STRIDE + 1
    n_sel = S // SEL_BLOCK
    cs = np.arange(n_cmp)[:, None] * CMP_STRIDE
    js = np.arange(n_sel)[None, :] * SEL_BLOCK
    ov = np.clip(np.minimum(cs + CMP_BLOCK, js + SEL_BLOCK) - np.maximum(cs, js), 0, None)
    return jnp.asarray((ov / CMP_STRIDE).astype(np.float32))


def compress_blocks(k, pe, w1, w2):
    B, G, S, dh = k.shape
    ratio = CMP_BLOCK // CMP_STRIDE
    chunks = k.reshape(B, G, S // CMP_STRIDE, CMP_STRIDE, dh)
    n_cmp = S // CMP_STRIDE - ratio + 1
    blocks = jnp.concatenate([chunks[:, :, j:j + n_cmp] for j in range(ratio)], axis=3)
    flat = (blocks + pe).reshape(B, G, n_cmp, CMP_BLOCK * dh)
    return jax.nn.gelu(flat @ w1) @ w2


def nsa_attention(q_flat, kv_flat, gate_logits, pe_k, pe_v, ck_w1, ck_w2, cv_w1, cv_w2):
    B, S, _ = q_flat.shape
    dt = q_flat.dtype
    scale = HEAD_DIM ** -0.5
    t = jnp.arange(S)
    q = rope(q_flat.reshape(B, S, N_KV, GROUP, HEAD_DIM).transpose(0, 2, 3, 1, 4), t)

    def kv_heads(a):
        return a.reshape(B, S, N_KV, HEAD_DIM).transpose(0, 2, 1, 3)

    kc, vc, ks, vs, kw, vw = [kv_heads(a) for a in jnp.split(kv_flat, 6, axis=-1)]
    ks = rope(ks, t)
    kw = rope(kw, t)

    n_cmp = (S - CMP_BLOCK) // CMP_STRIDE + 1
    pos_c = jnp.arange(n_cmp) * CMP_STRIDE + CMP_BLOCK - 1
    k_cmp = rope(compress_blocks(kc, pe_k, ck_w1, ck_w2), pos_c)
    v_cmp = compress_blocks(vc, pe_v, cv_w1, cv_w2)
    s = jnp.einsum('bgrsd,bgcd->bgrsc', q, k_cmp) * scale
    p_cmp = masked_softmax(s, pos_c[None, :] <= t[:, None])
    o_cmp = jnp.einsum('bgrsc,bgcd->bgrsd', p_cmp.astype(dt), v_cmp)

    n_sel = S // SEL_BLOCK
    top = min(SEL_TOPK, n_sel)
    imp = jnp.einsum('bgrsc,cj->bgsj', p_cmp, cmp_to_sel_matrix(S))
    blk = jnp.arange(n_sel)[None, :]
    cur = (t // SEL_BLOCK)[:, None]
    imp = jnp.where(blk > cur, -FORCE_SCORE, imp)
    imp = jnp.where((blk == 0) | (blk == cur) | (blk == cur - 1), FORCE_SCORE, imp)
    _, sel_idx = lax.top_k(imp, top)

    nq = S // Q_BLOCK
    q_blocks = q.reshape(B, N_KV, GROUP, nq, Q_BLOCK, HEAD_DIM).transpose(3, 0, 1, 2, 4, 5)
    idx_blocks = sel_idx.reshape(B, N_KV, nq, Q_BLOCK, top).transpose(2, 0, 1, 3, 4)
    ks_blk = ks.reshape(B, N_KV, n_sel, SEL_BLOCK, HEAD_DIM)
    vs_blk = vs.reshape(B, N_KV, n_sel, SEL_BLOCK, HEAD_DIM)
    pad = ((0, 0), (0, 0), (WINDOW, 0), (0, 0))
    kw_pad = jnp.pad(kw, pad)
    vw_pad = jnp.pad(vw, pad)
    bi = jnp.arange(B)[:, None, None, None]
    gi = jnp.arange(N_KV)[None, :, None, None]
    offs = jnp.arange(SEL_BLOCK)
    band = WINDOW + Q_BLOCK

    def step(args):
        i, qb, ib = args
        qpos = i * Q_BLOCK + jnp.arange(Q_BLOCK)
        k_sel = ks_blk[bi, gi, ib].reshape(B, N_KV, Q_BLOCK, top * SEL_BLOCK, HEAD_DIM)
        v_sel = vs_blk[bi, gi, ib].reshape(B, N_KV, Q_BLOCK, top * SEL_BLOCK, HEAD_DIM)
        kpos = (ib[..., None] * SEL_BLOCK + offs).reshape(B, N_KV, Q_BLOCK, top * SEL_BLOCK)
        s_sel = jnp.einsum('bgrqd,bgqkd->bgrqk', qb, k_sel) * scale
        p_sel = masked_softmax(s_sel, (kpos <= qpos[:, None])[:, :, None])
        o_sel = jnp.einsum('bgrqk,bgqkd->bgrqd', p_sel.astype(dt), v_sel)
        k_win = lax.dynamic_slice_in_dim(kw_pad, i * Q_BLOCK, band, axis=2)
        v_win = lax.dynamic_slice_in_dim(vw_pad, i * Q_BLOCK, band, axis=2)
        wpos = i * Q_BLOCK - WINDOW + jnp.arange(band)
        dist = qpos[:, None] - wpos[None, :]
        wmask = (dist >= 0) & (dist < WINDOW) & (wpos >= 0)[None, :]
        s_win = jnp.einsum('bgrqd,bgkd->bgrqk', qb, k_win) * scale
        p_win = masked_softmax(s_win, wmask)
        o_win = jnp.einsum('bgrqk,bgkd->bgrqd', p_win.astype(dt), v_win)
        return o_sel, o_win

    o_sel, o_win = lax.map(step, (jnp.arange(nq), q_blocks, idx_blocks))
    o_sel = o_sel.transpose(1, 2, 3, 0, 4, 5).reshape(B, N_KV, GROUP, S, HEAD_DIM)
    o_win = o_win.transpose(1, 2, 3, 0, 4, 5).reshape(B, N_KV, GROUP, S, HEAD_DIM)

    g = jax.nn.sigmoid(gate_logits).reshape(B, S, N_NSA_BRANCHES, N_KV, GROUP).transpose(2, 0, 3, 4, 1)[..., None]
    o = g[0] * o_cmp + g[1] * o_sel + g[2] * o_win
    return o.transpose(0, 3, 1, 2, 4).reshape(B, S, ATTN_WIDTH)


def pool_mixer(u, w_pool, pool_scale):
    B, S, _ = u.shape
    uf = u.astype(jnp.float32)
    csp = jnp.concatenate([jnp.zeros((B, 1, POOL_WIDTH), jnp.float32), jnp.cumsum(uf, axis=1)], axis=1)
    t = jnp.arange(S)
    outs = []
    for gidx, w in enumerate(POOL_WINDOWS):
        lo, hi = gidx * POOL_GROUP, (gidx + 1) * POOL_GROUP
        c = csp[..., lo:hi]
        lower = jnp.concatenate([jnp.zeros((B, w - 1, POOL_GROUP), jnp.float32), c[:, :S - w + 1]], axis=1)
        cnt = jnp.minimum(t + 1, w).astype(jnp.float32)[:, None]
        outs.append((c[:, 1:] - lower) / cnt - uf[..., lo:hi])
    pooled = jnp.stack(outs, axis=2).astype(u.dtype)
    mixed = jnp.einsum('bsgc,gcd->bsgd', pooled, w_pool).reshape(B, S, POOL_WIDTH)
    return mixed * pool_scale


def hybrid_layer(x, norm_mix, w_in, cmp_pe_k, cmp_pe_v, cmp_k_w1, cmp_k_w2, cmp_v_w1, cmp_v_w2,
                 w_branch_attn, pool_w, pool_scale, w_branch_pool, w_out, norm_mlp, w_ff1, w_ff2):
    h = rms_norm(x, norm_mix)
    proj = h @ w_in
    q_flat, kv_flat, g_attn, u_pool, g_merge = jnp.split(proj, np.cumsum(IN_SIZES)[:-1].tolist(), axis=-1)
    a = nsa_attention(q_flat, kv_flat, g_attn, cmp_pe_k, cmp_pe_v, cmp_k_w1, cmp_k_w2, cmp_v_w1, cmp_v_w2) @ w_branch_attn
    b = pool_mixer(u_pool, pool_w, pool_scale) @ w_branch_pool
    ga, gb = jnp.split(jax.nn.sigmoid(g_merge), 2, axis=-1)
    x = x + (ga * a + gb * b) @ w_out
    h = rms_norm(x, norm_mlp)
    return x + jnp.square(jax.nn.relu(h @ w_ff1)) @ w_ff2


def setup_inputs(seed: int = 0) -> dict:
    key = jax.random.key(seed)
    ks = jax.random.split(key, 20)
    f32 = jnp.float32

    def nrm(k, shape, fan_in):
        return jax.random.normal(k, shape, f32) * (fan_in ** -0.5)

    def gain(k, shape):
        return 1.0 + 0.05 * jax.random.normal(k, shape, f32)

    L = DEPTH
    return {
        "x": jax.random.normal(ks[0], (BATCH, SEQ, D_MODEL), f32),
        "norm_mix": gain(ks[1], (L, D_MODEL)),
        "w_in": nrm(ks[2], (L, D_MODEL, IN_WIDTH), D_MODEL),
        "cmp_pe_k": 0.02 * jax.random.normal(ks[3], (L, CMP_BLOCK, HEAD_DIM), f32),
        "cmp_pe_v": 0.02 * jax.random.normal(ks[4], (L, CMP_BLOCK, HEAD_DIM), f32),
        "cmp_k_w1": nrm(ks[5], (L, CMP_BLOCK * HEAD_DIM, CMP_HIDDEN), CMP_BLOCK * HEAD_DIM),
        "cmp_k_w2": nrm(ks[6], (L, CMP_HIDDEN, HEAD_DIM), CMP_HIDDEN),
        "cmp_v_w1": nrm(ks[7], (L, CMP_BLOCK * HEAD_DIM, CMP_HIDDEN), CMP_BLOCK * HEAD_DIM),
        "cmp_v_w2": nrm(ks[8], (L, CMP_HIDDEN, HEAD_DIM), CMP_HIDDEN),
        "w_branch_attn": nrm(ks[9], (L, ATTN_WIDTH, D_MODEL), ATTN_WIDTH),
        "pool_w": nrm(ks[10], (L, len(POOL_WINDOWS), POOL_GROUP, POOL_GROUP), POOL_GROUP),
        "pool_scale": 1.0 + 0.1 * jax.random.normal(ks[11], (L, POOL_WIDTH), f32),
        "w_branch_pool": nrm(ks[12], (L, POOL_WIDTH, D_MODEL), POOL_WIDTH),
        "w_out": nrm(ks[13], (L, D_MODEL, D_MODEL), D_MODEL),
        "norm_mlp": gain(ks[14], (L, D_MODEL)),
        "w_ff1": nrm(ks[15], (L, D_MODEL, D_FF), D_MODEL),
        "w_ff2": nrm(ks[16], (L, D_FF, D_MODEL), D_FF),
        "norm_final": gain(ks[17], (D_MODEL,)),
    }


def reference(x, norm_mix, w_in, cmp_pe_k, cmp_pe_v, cmp_k_w1, cmp_k_w2, cmp_v_w1, cmp_v_w2,
              w_branch_attn, pool_w, pool_scale, w_branch_pool, w_out, norm_mlp, w_ff1, w_ff2,
              norm_final):
    for l in range(DEPTH):
        x = hybrid_layer(x, norm_mix[l], w_in[l], cmp_pe_k[l], cmp_pe_v[l], cmp_k_w1[l], cmp_k_w2[l],
                         cmp_v_w1[l], cmp_v_w2[l], w_branch_attn[l], pool_w[l], pool_scale[l],
                         w_branch_pool[l], w_out[l], norm_mlp[l], w_ff1[l], w_ff2[l])
    return rms_norm(x, norm_final)
```

```python
import os
import numpy as np
import concourse.bass as bass
import concourse.mybir as mybir
from concourse.bass_utils import run_bass_kernel_spmd

F32 = mybir.dt.float32
BF16 = mybir.dt.bfloat16
I32 = mybir.dt.int32
ALU = mybir.AluOpType
AF = mybir.ActivationFunctionType

PE, ACT, DVE, POOL, SP, PQ = "pe", "act", "dve", "pool", "sp", "pq"
COMPUTE = (PE, ACT, DVE, POOL)
N_DMA_SEMS = 16
N_PQ_SEMS = 8
N_ALL_DSEMS = N_DMA_SEMS + N_PQ_SEMS

S_LEN = 4096
D = 1024
NT = 32
NB = 8
DFF = 4096
NEGB = -3000.0
INV_FREQ = [float(np.float32(500000.0) ** (-np.float32(2 * i) / np.float32(16))) for i in range(8)]
TWO_PI = float(2 * np.pi)


class Op:
    __slots__ = ("eng", "emit", "deps", "idx", "dsem", "dval", "dprev")

    def __init__(self, eng, emit):
        self.eng = eng
        self.emit = emit
        self.deps = set()
        self.idx = 0
        self.dsem = None
        self.dval = 0
        self.dprev = 0


class Sched:
    def __init__(self):
        self.streams = {e: [] for e in COMPUTE + (SP,)}
        self.nreal = {e: 0 for e in COMPUTE + (SP,)}
        self.lastreal = {e: None for e in COMPUTE}
        self.last_w = {}
        self.readers = {}
        self.dma_rr = 0
        self.pq_rr = 0
        self.dma_cnt = [0] * N_ALL_DSEMS
        self.dma_last = [None] * N_ALL_DSEMS

    def add(self, eng, emit, reads=(), writes=()):
        op = Op(eng, emit)
        for k in reads:
            w = self.last_w.get(k)
            if w is not None:
                op.deps.add(w)
        for k in writes:
            w = self.last_w.get(k)
            if w is not None:
                op.deps.add(w)
            for r in self.readers.get(k, ()):
                op.deps.add(r)
        for k in writes:
            self.last_w[k] = op
            self.readers[k] = []
        ws = set(writes)
        for k in reads:
            if k not in ws:
                self.readers.setdefault(k, []).append(op)
        op.deps.discard(op)
        self.streams[POOL if eng == PQ else eng].append(op)
        if eng == SP or eng == PQ:
            if eng == SP:
                s = self.dma_rr
                self.dma_rr = (self.dma_rr + 1) % N_DMA_SEMS
            else:
                s = N_DMA_SEMS + self.pq_rr
                self.pq_rr = (self.pq_rr + 1) % N_PQ_SEMS
            op.dsem = s
            op.dprev = self.dma_cnt[s] * 16
            self.dma_cnt[s] += 1
            op.dval = self.dma_cnt[s] * 16
            self.dma_last[s] = op
        else:
            self.nreal[eng] += 1
            op.idx = self.nreal[eng]
            self.lastreal[eng] = op
        return op

    def barrier(self):
        deps = set()
        for e in COMPUTE:
            if self.lastreal[e] is not None:
                deps.add(self.lastreal[e])
        for s in range(N_ALL_DSEMS):
            if self.dma_last[s] is not None:
                deps.add(self.dma_last[s])
        for e in COMPUTE + (SP,):
            op = Op(e, None)
            op.deps = set(deps)
            self.streams[e].append(op)
        self.last_w = {}
        self.readers = {}

    def emit_all(self, block, sems, dsems):
        engmap = {PE: "tensor", ACT: "scalar", DVE: "vector", POOL: "gpsimd", SP: "sync"}
        for ename in (SP, PE, ACT, DVE, POOL):
            ops = self.streams[ename]

            def body(eng, ops=ops, ename=ename):
                waited = {}
                for op in ops:
                    need = {}
                    for d in op.deps:
                        if d.eng == SP or d.eng == PQ:
                            key = ("d", d.dsem)
                            val = d.dval
                        else:
                            if d.eng == ename and ename == PE:
                                continue
                            key = ("c", d.eng)
                            val = d.idx
                        if need.get(key, 0) < val:
                            need[key] = val
                    if (op.eng == SP or op.eng == PQ) and op.emit is not None and op.dprev > 0:
                        key = ("d", op.dsem)
                        if need.get(key, 0) < op.dprev:
                            need[key] = op.dprev
                    for key, val in need.items():
                        if waited.get(key, 0) >= val:
                            continue
                        waited[key] = val
                        sem = dsems[key[1]] if key[0] == "d" else sems[key[1]]
                        eng.wait_ge(sem, val)
                    if op.emit is None:
                        continue
                    ins = op.emit(eng)
                    if op.eng == SP or op.eng == PQ:
                        ins.then_inc(dsems[op.dsem], 16)
                    else:
                        ins.then_inc(sems[op.eng], 1)
                if ename == SP:
                    for s in range(N_DMA_SEMS):
                        if self.dma_cnt[s] > 0 and waited.get(("d", s), 0) < self.dma_cnt[s] * 16:
                            eng.wait_ge(dsems[s], self.dma_cnt[s] * 16)

            getattr(block, engmap[ename])(body)


class Alloc:
    def __init__(self, nc, base=16512, top=229344):
        self.nc = nc
        self.cur = base
        self.top = top
        self.n = 0

    def mark(self):
        return self.cur

    def release(self, m):
        self.cur = m

    def __call__(self, shape, dt):
        nbytes = int(np.prod(shape[1:])) * (2 if dt == BF16 else 4)
        nbytes = (nbytes + 31) // 32 * 32
        off = self.cur
        assert off + nbytes <= self.top, ("SBUF overflow", off, nbytes, self.top)
        self.cur += nbytes
        self.n += 1
        return self.nc.alloc_sbuf_tensor_at("t%d" % self.n, list(shape), dt, offset=off)


def build_nc(stop_after=99):
    nc = bass.Bass("TRN2", target_bir_lowering=False)

    def din(name, shape):
        return nc.dram_tensor(name, list(shape), F32, kind="ExternalInput").ap()

    x_d = din("x", [S_LEN, D])
    norm_mix_d = din("norm_mix", [1, D])
    w_in_d = din("w_in", [D, 3864])
    pe_k_d = din("cmp_pe_k", [32, 64])
    pe_v_d = din("cmp_pe_v", [32, 64])
    w1k_d = din("cmp_k_w1", [2048, 256])
    w2k_d = din("cmp_k_w2", [256, 64])
    w1v_d = din("cmp_v_w1", [2048, 256])
    w2v_d = din("cmp_v_w2", [256, 64])
    wba_d = din("w_branch_attn", [512, D])
    poolw_d = din("pool_w", [512, 128])
    pscale_d = din("pool_scale", [512, 1])
    wbp_d = din("w_branch_pool", [512, D])
    wout_d = din("w_out", [D, D])
    norm_mlp_d = din("norm_mlp", [1, D])
    wff1_d = din("w_ff1", [D, DFF])
    wff2_d = din("w_ff2", [DFF, D])
    norm_fin_d = din("norm_final", [1, D])
    out_d = nc.dram_tensor("out", [S_LEN, D], F32, kind="ExternalOutput").ap()
    x1_d = nc.dram_tensor("x1_scratch", [S_LEN, D], F32).ap()

    S = Sched()
    A = Alloc(nc)

    class _Rec:
        def __getattr__(self, name):
            def f(*a, **kw):
                return (name, a, kw)
            return f

    _REC = _Rec()

    def add(eng, fn, reads=(), writes=()):
        call = fn(_REC)
        op = S.add(eng, lambda e: getattr(e, call[0])(*call[1], **call[2]), reads, writes)
        if os.environ.get("MK_DUMP"):
            S.dbg = getattr(S, "dbg", [])
            S.dbg.append((eng, op.idx, call[0], [(d.eng, d.idx, d.dval) for d in op.deps], list(reads), list(writes)))
        return op


    def finish():
        from contextlib import ExitStack
        with ExitStack() as es:
            sems = {e: es.enter_context(nc.semaphore("sem_" + e)) for e in COMPUTE}
            dsems = [es.enter_context(nc.semaphore("dsem%d" % i)) for i in range(N_ALL_DSEMS)]
            block = es.enter_context(nc.Block())
            S.emit_all(block, sems, dsems)
        return nc

    psf = [nc.alloc_psum_tensor("psb%d" % i, [128, 512], F32) for i in range(8)]
    psb = [p[:, :].bitcast(BF16) for p in psf]
    PK = [("ps", i) for i in range(8)]
    rr = [0]

    def nb():
        i = rr[0]
        rr[0] = (i + 1) % 8
        return i

    def mm(out, lhsT, rhs, start, stop, r, w):
        add(PE, lambda e: e.matmul(out, lhsT=lhsT, rhs=rhs, start=start, stop=stop, skip_group_check=True), r, w)

    identb = A([128, 128], BF16)
    onesf = A([128, 128], F32)
    zerosb = A([128, 128], BF16)
    CBt = A([128, 128], BF16)
    WBt = A([128, 128], BF16)
    gmix = A([128, D], F32)
    gmlp = A([128, D], F32)
    gfin = A([128, D], F32)
    epst = A([128, 1], F32)
    notHalf = A([128, 1], F32)
    negHalf = A([128, 1], F32)
    junk = A([128, D], BF16)

    add(POOL, lambda e: e.memset(onesf[:], 1.0), [], ["onesf"])
    add(POOL, lambda e: e.affine_select(out=notHalf[:], in_=onesf[:, 0:1], pattern=[[0, 1]], compare_op=ALU.is_ge,
                                        fill=0.0, base=-64, channel_multiplier=1), ["onesf"], ["half"])
    add(POOL, lambda e: e.tensor_scalar(out=negHalf[:], in0=notHalf[:], scalar1=-1.0, scalar2=None, op0=ALU.add), ["half"], ["half"])
    add(POOL, lambda e: e.memset(zerosb[:], 0.0), [], ["zerosb"])
    add(POOL, lambda e: e.memset(epst[:], 1e-6), [], ["eps"])
    add(POOL, lambda e: e.affine_select(out=identb[:], in_=onesf[:], pattern=[[-1, 128]], compare_op=ALU.is_equal,
                                        fill=0.0, base=0, channel_multiplier=1), ["onesf"], ["identb"])
    add(POOL, lambda e: e.affine_select(out=CBt[:], in_=zerosb[:], pattern=[[1, 128]], compare_op=ALU.is_ge,
                                        fill=NEGB, base=0, channel_multiplier=-1), ["zerosb"], ["CB"])
    add(POOL, lambda e: e.affine_select(out=WBt[:], in_=zerosb[:], pattern=[[-1, 128]], compare_op=ALU.is_ge,
                                        fill=NEGB, base=-1, channel_multiplier=1), ["zerosb"], ["WB"])
    add(SP, lambda e: e.dma_start(out=gmix[:], in_=norm_mix_d.partition_broadcast(128)), [], ["gmix"])
    add(SP, lambda e: e.dma_start(out=gmlp[:], in_=norm_mlp_d.partition_broadcast(128)), [], ["gmlp"])
    add(SP, lambda e: e.dma_start(out=gfin[:], in_=norm_fin_d.partition_broadcast(128)), [], ["gfin"])

    def rms_stats(xt_ap, kx, ss, rstd, kst):
        add(ACT, lambda e: e.activation(out=junk[:], in_=xt_ap, func=AF.Square, accum_out=ss), [kx], ["junk", kst])
        add(ACT, lambda e: e.activation(out=rstd, in_=ss, func=AF.Sqrt, bias=epst[:, 0:1], scale=1.0 / D),
            [kst, "eps"], [kst])
        add(DVE, lambda e: e.reciprocal(out=rstd, in_=rstd), [kst], [kst])

    m_l1 = A.mark()
    OT = A([128, 4, S_LEN], BF16)
    m_l2 = A.mark()
    QT = A([128, 4, S_LEN], BF16)
    KSA = [A([128, S_LEN], BF16), A([128, S_LEN], BF16)]
    KWT = A([128, S_LEN], BF16)
    V2 = A([128, NT, 2, 2, 66], BF16)
    G = A([128, NT, 24], F32)
    KCMPT = A([128, 256], BF16)
    VCA = A([128, 2, 2, 130], BF16)
    CS = A([128, NT, 16], F32)
    CSC = A([128, 2, 16], F32)
    m_p1 = A.mark()
    KCT = A([128, S_LEN], BF16)
    VCT = A([128, S_LEN], BF16)
    m_p1a = A.mark()

    posi = A([128, NT], I32)
    posf = A([128, NT], F32)
    ang = A([128, NT * 16], F32)
    angn = A([128, NT * 16], F32)
    angi = A([128, NT * 16], I32)

    def build_cs(dst, ntile, base, cm, step):
        n = ntile * 16
        add(POOL, lambda e: e.iota(posi[:, 0:ntile], pattern=[[step, ntile]], base=base, channel_multiplier=cm), [], ["posi"])
        add(POOL, lambda e: e.tensor_copy(out=posf[:, 0:ntile], in_=posi[:, 0:ntile]), ["posi"], ["posf"])
        a3 = ang[:, 0:n].rearrange("p (t f) -> p t f", f=16)
        for f in range(8):
            add(DVE, lambda e, f=f: e.tensor_scalar(out=a3[:, :, 8 + f], in0=posf[:, 0:ntile], scalar1=INV_FREQ[f],
                                                    scalar2=None, op0=ALU.mult), ["posf"], ["ang"])
        add(DVE, lambda e: e.tensor_scalar(out=a3[:, :, 0:8], in0=a3[:, :, 8:16], scalar1=float(np.pi / 2), scalar2=None,
                                           op0=ALU.add), ["ang"], ["ang"])
        av, nv, iv = ang[:, 0:n], angn[:, 0:n], angi[:, 0:n]
        add(DVE, lambda e: e.tensor_scalar(out=nv, in0=av, scalar1=1.0 / TWO_PI, scalar2=None, op0=ALU.mult), ["ang"], ["angn"])
        add(DVE, lambda e: e.tensor_copy(out=iv, in_=nv), ["angn"], ["angi"])
        add(DVE, lambda e: e.tensor_copy(out=nv, in_=iv), ["angi"], ["angn"])
        add(DVE, lambda e: e.scalar_tensor_tensor(out=av, in0=nv, scalar=-6.28125, in1=av, op0=ALU.mult, op1=ALU.add),
            ["ang", "angn"], ["ang"])
        add(DVE, lambda e: e.scalar_tensor_tensor(out=av, in0=nv, scalar=-(TWO_PI - 6.28125), in1=av, op0=ALU.mult,
                                                  op1=ALU.add), ["ang", "angn"], ["ang"])
        add(DVE, lambda e: e.tensor_scalar(out=nv, in0=av, scalar1=float(np.pi), scalar2=-TWO_PI, op0=ALU.is_gt,
                                           op1=ALU.mult), ["ang"], ["angn"])
        add(DVE, lambda e: e.tensor_tensor(out=av, in0=av, in1=nv, op=ALU.add), ["ang", "angn"], ["ang"])
        add(DVE, lambda e: e.tensor_scalar(out=nv, in0=av, scalar1=-float(np.pi), scalar2=TWO_PI, op0=ALU.is_lt,
                                           op1=ALU.mult), ["ang"], ["angn"])
        add(DVE, lambda e: e.tensor_tensor(out=av, in0=av, in1=nv, op=ALU.add), ["ang", "angn"], ["ang"])
        add(DVE, lambda e: e.tensor_scalar(out=av, in0=av, scalar1=3.141592, scalar2=-3.141592, op0=ALU.min,
                                           op1=ALU.max), ["ang"], ["ang"])
        add(ACT, lambda e: e.activation(out=dst, in_=av, func=AF.Sin), ["ang"], ["cs"])

    build_cs(CS[:, :, :].rearrange("p t f -> p (t f)"), NT, 0, 1, 128)
    build_cs(CSC[:, :, :].rearrange("p t f -> p (t f)"), 2, 31, 16, 2048)

    for g_, prt_ in ((0, slice(64, 128)), (1, slice(0, 64))):
        ev = KSA[g_][prt_, :].rearrange("p (j c) -> p j c", c=64)
        add(POOL, lambda e: e.memset(KSA[g_][prt_, :], 1.0), [], [("KSAE", g_)])
        add(POOL, lambda e: e.affine_select(out=ev, in_=ev, pattern=[[-1, 64], [0, 64]], compare_op=ALU.is_equal,
                                            fill=0.0, base=0, channel_multiplier=1), [("KSAE", g_)], [("KSAE", g_)])
    v2keys = [("V2", t) for t in range(NT)]
    add(POOL, lambda e: e.memset(V2[:], 1.0), [], v2keys)
    add(POOL, lambda e: e.memset(VCA[:], 1.0), [], ["VCA"])
    mtmp = A([128, 2, 64], F32)
    mtmp2 = A([128, 2, 64], F32)
    for ct in range(2):
        add(POOL, lambda e, ct=ct: e.affine_select(out=mtmp[:, ct, :], in_=onesf[:, 0:64], pattern=[[-4, 64]],
                                                   compare_op=ALU.is_ge, fill=0.0, base=128 * ct + 1, channel_multiplier=1),
            ["onesf"], ["mtmp"])
        add(POOL, lambda e, ct=ct: e.affine_select(out=mtmp[:, ct, :], in_=mtmp[:, ct, :], pattern=[[4, 64]],
                                                   compare_op=ALU.is_ge, fill=0.0, base=3 - 128 * ct, channel_multiplier=-1),
            ["mtmp"], ["mtmp"])
        add(POOL, lambda e, ct=ct: e.affine_select(out=mtmp2[:, ct, :], in_=onesf[:, 0:64], pattern=[[-4, 64]],
                                                   compare_op=ALU.is_ge, fill=0.0, base=128 * ct, channel_multiplier=1),
            ["onesf"], ["mtmp2"])
        add(POOL, lambda e, ct=ct: e.affine_select(out=mtmp2[:, ct, :], in_=mtmp2[:, ct, :], pattern=[[4, 64]],
                                                   compare_op=ALU.is_ge, fill=0.0, base=2 - 128 * ct, channel_multiplier=-1),
            ["mtmp2"], ["mtmp2"])
    add(POOL, lambda e: e.tensor_tensor(out=mtmp[:], in0=mtmp[:], in1=mtmp2[:], op=ALU.add), ["mtmp", "mtmp2"], ["mtmp"])
    for g in range(2):
        add(POOL, lambda e, g=g: e.tensor_copy(out=VCA[:, :, g, 65:129], in_=mtmp[:]), ["mtmp"], ["VCA"])

    if stop_after < 1:
        return finish()
    WA = A([128, 8, 1304], BF16)
    xt = [A([128, D], F32), A([128, D], F32)]
    hbf = [A([128, D], BF16), A([128, D], BF16)]
    hT = [A([128, 8, 512], BF16), A([128, 8, 512], BF16)]
    qr = [A([128, 768], BF16), A([128, 768], BF16)]
    rt = A([128, 4, 64], F32)
    st = [A([128, 2], F32), A([128, 2], F32)]

    w_in_v = w_in_d.rearrange("(k p) n -> p k n", p=128)
    cast_engs = [POOL, ACT, DVE]
    for k in range(8):
        add(PQ, lambda e: e.dma_start(out=WA[:, k, :], in_=w_in_v[:, k, 0:1304]), [], [("WA", k)])
    WAK = [("WA", k) for k in range(8)]

    def stage_norm(tt):
        b = tt % 2
        add(SP, lambda e: e.dma_start(out=xt[b][:], in_=x_d[tt * 128:(tt + 1) * 128, :]), [], [("xt", b)])
        rms_stats(xt[b][:], ("xt", b), st[b][:, 0:1], st[b][:, 1:2], ("st", b))
        add(DVE, lambda e: e.scalar_tensor_tensor(out=hbf[b][:], in0=xt[b][:], scalar=st[b][:, 1:2], in1=gmix[:],
                                                  op0=ALU.mult, op1=ALU.mult), [("xt", b), ("st", b), "gmix"], [("hbf", b)])

    def stage_proj(tt):
        b = tt % 2
        I, tl = tt // 4, tt % 4
        bI = I % 2
        tb = nb()
        for k in range(8):
            add(PE, lambda e, k=k: e.transpose(out=psb[tb][:, k * 128:(k + 1) * 128], in_=hbf[b][:, k * 128:(k + 1) * 128],
                                               identity=identb[:]), [("hbf", b), "identb"], [PK[tb]])
        add(ACT, lambda e: e.copy(out=hT[bI][:, :, tl * 128:(tl + 1) * 128],
                                  in_=psb[tb][:, :].rearrange("p (k t) -> p k t", k=8)), [PK[tb]], [("hT", bI, tl)])
        pa, pb, pc = nb(), nb(), nb()
        for k in range(8):
            mm(psf[pa][:, 0:512], hT[bI][:, k, tl * 128:(tl + 1) * 128], WA[:, k, 0:512], k == 0, k == 7,
               [("hT", bI, tl), ("WA", k)], [PK[pa]])
        for k in range(8):
            mm(psf[pb][:, 0:512], hT[bI][:, k, tl * 128:(tl + 1) * 128], WA[:, k, 768:1280], k == 0, k == 7,
               [("hT", bI, tl), ("WA", k)], [PK[pb]])
        for k in range(8):
            mm(psf[pc][:, 0:24], hT[bI][:, k, tl * 128:(tl + 1) * 128], WA[:, k, 1280:1304], k == 0, k == 7,
               [("hT", bI, tl), ("WA", k)], [PK[pc]])
        if LVL < 4:
            return
        cosq = CS[:, tt, 0:8].unsqueeze(1).unsqueeze(1).to_broadcast([128, 2, 4, 8])
        sinq = CS[:, tt, 8:16].unsqueeze(1).unsqueeze(1).to_broadcast([128, 2, 4, 8])
        qv = psf[pa][:, 0:512].rearrange("p (g i d) -> p g i d", g=2, i=4)
        qo = qr[b][:, 0:512].rearrange("p (i g d) -> p g i d", i=4, g=2)
        kv4 = psf[pb][:, 0:512].rearrange("p (a r d) -> p a r d", a=2, r=4)
        kvw = kv4[:, :, 0:2, :]
        vvw = kv4[:, :, 2:4, :]
        ko = qr[b][:, 512:768].rearrange("p (a g d) -> p a g d", a=2, g=2)
        cosk = CS[:, tt, 0:8].unsqueeze(1).unsqueeze(1).to_broadcast([128, 2, 2, 8])
        sink = CS[:, tt, 8:16].unsqueeze(1).unsqueeze(1).to_broadcast([128, 2, 2, 8])

        def rope(src, dst, cos, sin, n, rk, wk):
            shp = dict(g=2, i=n)
            t1 = rt[:, 0, 0:16 * n].rearrange("p (g i d) -> p g i d", **shp)
            t2 = rt[:, 1, 0:16 * n].rearrange("p (g i d) -> p g i d", **shp)
            t3 = rt[:, 2, 0:16 * n].rearrange("p (g i d) -> p g i d", **shp)
            t4 = rt[:, 3, 0:16 * n].rearrange("p (g i d) -> p g i d", **shp)
            x1, x2 = src[:, :, :, 0:8], src[:, :, :, 8:16]
            add(DVE, lambda e: e.tensor_tensor(out=t1, in0=x1, in1=cos, op=ALU.mult), rk + ["cs"], ["rt1"])
            add(DVE, lambda e: e.tensor_tensor(out=t2, in0=x2, in1=sin, op=ALU.mult), rk + ["cs"], ["rt2"])
            add(DVE, lambda e: e.tensor_tensor(out=t3, in0=x2, in1=cos, op=ALU.mult), rk + ["cs"], ["rt3"])
            add(DVE, lambda e: e.tensor_tensor(out=t4, in0=x1, in1=sin, op=ALU.mult), rk + ["cs"], ["rt4"])
            add(DVE, lambda e: e.tensor_tensor(out=dst[:, :, :, 0:8], in0=t1, in1=t2, op=ALU.subtract), ["rt1", "rt2"], wk)
            add(DVE, lambda e: e.tensor_tensor(out=dst[:, :, :, 8:16], in0=t3, in1=t4, op=ALU.add), ["rt3", "rt4"], wk)
            add(ACT, lambda e: e.copy(out=dst[:, :, :, 16:64], in_=src[:, :, :, 16:64]), rk, wk)

        rope(qv, qo, cosq, sinq, 4, [PK[pa]], [("qr", b)])
        rope(kvw, ko, cosk, sink, 2, [PK[pb]], [("qr", b)])
        add(ACT, lambda e: e.copy(out=V2[:, tt, :, :, 0:64], in_=vvw), [PK[pb]], [("V2", tt)])
        add(ACT, lambda e: e.copy(out=G[:, tt, :], in_=psf[pc][:, 0:24]), [PK[pc]], [("G", tt)])

    def stage_qT(tt):
        b = tt % 2
        tb = nb()
        for m in range(6):
            add(PE, lambda e, m=m: e.transpose(out=psb[tb][:, m * 128:(m + 1) * 128], in_=qr[b][:, m * 128:(m + 1) * 128],
                                               identity=identb[:]), [("qr", b), "identb"], [PK[tb]])
        cs = slice(tt * 128, (tt + 1) * 128)
        add(ACT, lambda e: e.copy(out=QT[:, :, cs], in_=psb[tb][:, 0:512].rearrange("p (m t) -> p m t", m=4)),
            [PK[tb]], [("QT", tt)])
        if LVL < 7:
            return
        add(ACT, lambda e: e.copy(out=KSA[0][0:64, cs], in_=psb[tb][0:64, 512:640]), [PK[tb]], [("KSA", 0, tt)])
        add(ACT, lambda e: e.copy(out=KSA[1][64:128, cs], in_=psb[tb][64:128, 512:640]), [PK[tb]], [("KSA", 1, tt)])
        if LVL < 8:
            return
        add(ACT, lambda e: e.copy(out=KWT[:, cs], in_=psb[tb][:, 640:768]), [PK[tb]], [("KWT", tt)])

    def stage_fm(I):
        bI = I % 2
        for m, dst, key in ((0, KCT, "KCT"), (1, VCT, "VCT")):
            p = nb()
            for k in range(8):
                mm(psf[p][:, 0:512], WA[:, k, 512 + 128 * m:640 + 128 * m], hT[bI][:, k, :], k == 0, k == 7,
                   [("hT", bI, 0), ("hT", bI, 1), ("hT", bI, 2), ("hT", bI, 3), ("WA", k)], [PK[p]])
            if m == 0:
                add(DVE, lambda e, p=p, dst=dst: e.tensor_copy(out=dst[:, I * 512:(I + 1) * 512], in_=psf[p][:, 0:512]),
                    [PK[p]], [(key, I)])
            else:
                add(ACT, lambda e, p=p, dst=dst: e.copy(out=dst[:, I * 512:(I + 1) * 512], in_=psf[p][:, 0:512]),
                    [PK[p]], [(key, I)])

    LVL = int(os.environ.get("MK_LVL", "9"))
    NTR = int(os.environ.get("MK_NT", str(NT)))
    for it in range(NTR + 2):
        if it < NTR and LVL >= 2:
            stage_norm(it)
        if 1 <= it <= NTR and LVL >= 3:
            stage_proj(it - 1)
            if (it - 1) % 4 == 3 and LVL >= 5:
                stage_fm((it - 1) // 4)
        if it >= 2 and LVL >= 6:
            stage_qT(it - 2)

    if stop_after < 2:
        return finish()
    gkeys_all = [("G", t_) for t_ in range(NT)]
    add(ACT, lambda e: e.activation(out=G[:, :, :].rearrange("p t c -> p (t c)"), in_=G[:, :, :].rearrange("p t c -> p (t c)"),
                                    func=AF.Sigmoid), gkeys_all, gkeys_all)
    S.barrier()
    A.release(m_p1a)
    W1 = [A([128, 32, 256], BF16), A([128, 32, 256], BF16)]
    pass
    W2 = A([128, 2, 2, 64], BF16)
    w2st = A([128, 2, 2, 64], F32)
    pest = A([32, 2, 64], F32)
    pebf = A([32, 2, 64], BF16)
    peT = A([64, 2, 32], BF16)
    hbias = A([128, 4], F32)
    zt = A([128, 256], F32)
    z2t = A([128, 256], F32)
    sgt = A([128, 256], F32)
    gel = [[A([128, 2, 256], BF16) for g in range(2)] for xi in range(2)]
    kcr = A([128, 128], BF16)
    rt = A([128, 4, 64], F32)
    for xi_ in range(2):
        for g_ in range(2):
            add(POOL, lambda e: e.memset(gel[xi_][g_][:], 0.0), [], [("gel", xi_, g_)])

    for xi, w1d in enumerate((w1k_d, w1v_d)):
        w1v_ = w1d.rearrange("(p d) n -> d p n", d=64)
        for j in range(4):
            for dup in range(2):
                add(PQ, lambda e: e.dma_start(out=W1[xi][dup * 64:(dup + 1) * 64, 8 * j:8 * j + 8, :],
                                              in_=w1v_[:, 8 * j:8 * j + 8, :]), [], [("W1", xi)])
    for xi, (w2d, ped) in enumerate(((w2k_d, pe_k_d), (w2v_d, pe_v_d))):
        add(SP, lambda e, xi=xi, w2d=w2d: e.dma_start(out=w2st[:, xi, :, :], in_=w2d.rearrange("(c p) n -> p c n", p=128)),
            [], [("w2st", xi)])
        add(SP, lambda e, xi=xi, ped=ped: e.dma_start(out=pest[:, xi, :], in_=ped[:, :]), [], [("pest", xi)])
    add(DVE, lambda e: e.tensor_copy(out=W2[:], in_=w2st[:]), [("w2st", 0), ("w2st", 1)], ["W2"])
    add(DVE, lambda e: e.tensor_copy(out=pebf[:], in_=pest[:]), [("pest", 0), ("pest", 1)], ["pebf"])
    L2 = int(os.environ.get("MK_L2", "9"))
    if L2 < 2:
        return finish()
    tb = nb()
    for xi in range(2):
        add(PE, lambda e, xi=xi: e.transpose(out=psb[tb][0:64, xi * 32:(xi + 1) * 32], in_=pebf[0:32, xi, :],
                                             identity=identb[0:32, 0:32]), ["pebf", "identb"], [PK[tb]])
    add(ACT, lambda e: e.copy(out=peT[:], in_=psb[tb][0:64, 0:64].rearrange("p (x t) -> p x t", x=2)), [PK[tb]], ["peT"])
    pbias = nb()
    for xi in range(2):
        for c in range(2):
            col = 2 * xi + c
            for p in range(32):
                mm(psf[pbias][:, col:col + 1], W1[xi][0:64, p, c * 128:(c + 1) * 128], peT[0:64, xi, p:p + 1],
                   p == 0, p == 31, [("W1", xi), "peT"], [PK[pbias]])
    add(DVE, lambda e: e.tensor_copy(out=hbias[:], in_=psf[pbias][:, 0:4]), [PK[pbias]], ["hbias"])
    if L2 < 3:
        return finish()
    KCK = [("KCT", I) for I in range(NB)]
    VCK = [("VCT", I) for I in range(NB)]
    for xi, (XT, xkeys) in enumerate(((KCT, KCK), (VCT, VCK))):
        for g in range(2):
            for c in range(2):
                p = nb()
                for pp in range(32):
                    mm(psf[p][:, 0:255], W1[xi][64 * g:64 * g + 64, pp, c * 128:(c + 1) * 128],
                       XT[64 * g:64 * g + 64, pp:pp + 4065:16], pp == 0, pp == 31, [("W1", xi)] + xkeys, [PK[p]])
                col = 2 * xi + c
                zv, z2v, sv = zt[:, 0:255], z2t[:, 0:255], sgt[:, 0:255]
                add(ACT, lambda e, p=p, col=col: e.activation(out=zv, in_=psf[p][:, 0:255], func=AF.Identity,
                                                              bias=hbias[:, col:col + 1], scale=1.0), [PK[p], "hbias"], ["zt"])
                add(POOL, lambda e: e.tensor_tensor(out=z2v, in0=zv, in1=zv, op=ALU.mult), ["zt"], ["z2t"])
                add(POOL, lambda e: e.tensor_scalar(out=z2v, in0=z2v, scalar1=0.044715, scalar2=1.0, op0=ALU.mult,
                                                    op1=ALU.add), ["z2t"], ["z2t"])
                add(POOL, lambda e: e.tensor_tensor(out=z2v, in0=z2v, in1=zv, op=ALU.mult), ["z2t", "zt"], ["z2t"])
                add(ACT, lambda e: e.activation(out=sv, in_=z2v, func=AF.Sigmoid, scale=1.5957691216057308), ["z2t"], ["sgt"])
                add(DVE, lambda e, xi=xi, g=g, c=c: e.tensor_tensor(out=gel[xi][g][:, c, 0:255], in0=sv, in1=zv, op=ALU.mult),
                    ["sgt", "zt"], [("gel", xi, g)])
    if L2 < 4:
        return finish()
    for ct in range(2):
        M = 128
        p = nb()
        for xi in range(2):
            for g in range(2):
                off = xi * 128 + g * 64
                for c in range(2):
                    mm(psf[p][0:M, off:off + 64], gel[xi][g][:, c, ct * 128:ct * 128 + M], W2[:, xi, c, :], c == 0, c == 1,
                       [("gel", xi, g), "W2"], [PK[p]])
        if L2 < 5:
            continue
        src = psf[p][:, 0:128].rearrange("p (g d) -> p g d", g=2)
        dst = kcr[:, :].rearrange("p (g d) -> p g d", g=2)
        cosc = CSC[:, ct, 0:8].unsqueeze(1).to_broadcast([128, 2, 8])
        sinc = CSC[:, ct, 8:16].unsqueeze(1).to_broadcast([128, 2, 8])
        t = [rt[:, q, 0:16].rearrange("p (g d) -> p g d", g=2) for q in range(4)]
        x1, x2 = src[:, :, 0:8], src[:, :, 8:16]
        add(DVE, lambda e: e.tensor_tensor(out=t[0], in0=x1, in1=cosc, op=ALU.mult), [PK[p], "cs"], ["rt1"])
        add(DVE, lambda e: e.tensor_tensor(out=t[1], in0=x2, in1=sinc, op=ALU.mult), [PK[p], "cs"], ["rt2"])
        add(DVE, lambda e: e.tensor_tensor(out=t[2], in0=x2, in1=cosc, op=ALU.mult), [PK[p], "cs"], ["rt3"])
        add(DVE, lambda e: e.tensor_tensor(out=t[3], in0=x1, in1=sinc, op=ALU.mult), [PK[p], "cs"], ["rt4"])
        add(DVE, lambda e: e.tensor_tensor(out=dst[:, :, 0:8], in0=t[0], in1=t[1], op=ALU.subtract), ["rt1", "rt2"], ["kcr"])
        add(DVE, lambda e: e.tensor_tensor(out=dst[:, :, 8:16], in0=t[2], in1=t[3], op=ALU.add), ["rt3", "rt4"], ["kcr"])
        if L2 < 6:
            continue
        add(ACT, lambda e: e.copy(out=dst[:, :, 16:64], in_=src[:, :, 16:64]), [PK[p]], ["kcr"])
        add(ACT, lambda e: e.copy(out=VCA[:, ct, :, 0:64], in_=psf[p][:, 128:256].rearrange("p (g d) -> p g d", g=2)),
            [PK[p]], ["VCA"])
        if L2 < 7:
            continue
        tb = nb()
        add(PE, lambda e: e.transpose(out=psb[tb][:, 0:128], in_=kcr[:, :], identity=identb[:]), ["kcr", "identb"], [PK[tb]])
        add(ACT, lambda e: e.copy(out=KCMPT[:, ct * 128:(ct + 1) * 128], in_=psb[tb][:, 0:128]), [PK[tb]], ["KCMPT"])

    if stop_after < 3:
        return finish()
    S.barrier()
    A.release(m_p1)
    AUG = [A([128, 8, 512], BF16), A([128, 8, 512], BF16)]
    Praw = [A([128, 512], BF16) for _ in range(3)]
    Pm = [A([128, 512], BF16) for _ in range(3)]
    oacc = A([128, 4, 8, 64], F32)
    impn = A([128, 4, 8, 64], F32)
    obf = A([128, 4, 512], BF16)
    dn = A([128, 4], F32)
    rd = A([128, 4], F32)
    cf = A([128, 4], F32)
    tmpo = A([128, 4, 64], F32)
    IG = A([128, 64], F32)
    IG2 = A([128, 64], F32)
    top16 = A([128, 16], F32)
    Mb = A([128, 4, 2, 64], BF16)
    add(DVE, lambda e: e.memset(Mb[:], 0.0), [], ["Mb"])

    SBK = [0, 1, 2]
    OBK = [3, 4, 5, 6]
    TBK = 7
    orr = [0]

    def qt_keys(I):
        return [("QT", 4 * I + q) for q in range(4)]

    items = []
    real_cnt = [0]

    def add_real(Afn, Bfn):
        items.append(('R', Afn, Bfn, real_cnt[0] % 3))
        real_cnt[0] += 1

    def do_block(I):
        a = I % 2
        gk = [("G", 4 * I + q) for q in range(4)]
        jobs_evac = {}

        def finish_job(br, h, banks):
            Gv = G[:, 4 * I:4 * I + 4, br * 8 + h]
            if br == 0:
                for bi, bk in enumerate(banks):
                    v = psf[bk][:, 0:258].rearrange("p (q c) -> p q c", q=2)
                    add(DVE, lambda e, v=v, bi=bi: e.tensor_scalar(out=dn[:, 2 * bi:2 * bi + 2], in0=v[:, :, 64], scalar1=1e-30,
                                                                   scalar2=None, op0=ALU.max), [PK[bk]], ["dn"])
            else:
                v = psf[banks[0]][:, 0:260].rearrange("p (q c) -> p q c", q=4)
                add(DVE, lambda e, v=v: e.tensor_scalar(out=dn[:], in0=v[:, :, 64], scalar1=1e-30, scalar2=None, op0=ALU.max),
                    [PK[banks[0]]], ["dn"])
            add(DVE, lambda e: e.reciprocal(out=rd[:], in_=dn[:]), ["dn"], ["rd"])
            add(DVE, lambda e: e.tensor_tensor(out=cf[:], in0=rd[:], in1=Gv, op=ALU.mult), ["rd"] + gk, ["cf"])
            if br == 0:
                for bi, bk in enumerate(banks):
                    v = psf[bk][:, 0:258].rearrange("p (q c) -> p q c", q=2)
                    add(DVE, lambda e, v=v, bi=bi: e.tensor_tensor(
                        out=oacc[:, 2 * bi:2 * bi + 2, h, :], in0=v[:, :, 0:64],
                        in1=cf[:, 2 * bi:2 * bi + 2].unsqueeze(2).to_broadcast([128, 2, 64]), op=ALU.mult),
                        [PK[bk], "cf"], [("oacc", h)])
                    if I >= 2:
                        add(DVE, lambda e, v=v, bi=bi: e.tensor_tensor(
                            out=impn[:, 2 * bi:2 * bi + 2, h, :], in0=v[:, :, 65:129],
                            in1=rd[:, 2 * bi:2 * bi + 2].unsqueeze(2).to_broadcast([128, 2, 64]), op=ALU.mult),
                            [PK[bk], "rd"], [("impn", h)])
            else:
                v = psf[banks[0]][:, 0:260].rearrange("p (q c) -> p q c", q=4)
                add(DVE, lambda e, v=v: e.tensor_tensor(out=tmpo[:], in0=v[:, :, 0:64],
                                                        in1=cf[:].unsqueeze(2).to_broadcast([128, 4, 64]), op=ALU.mult),
                    [PK[banks[0]], "cf"], ["tmpo"])
                add(DVE, lambda e: e.tensor_tensor(out=oacc[:, :, h, :], in0=oacc[:, :, h, :], in1=tmpo[:], op=ALU.add),
                    ["tmpo", ("oacc", h)], [("oacc", h)])

        def cmp_items(h):
            g = h // 4
            hp = slice(64 * g, 64 * g + 64)
            nct = 1 if I <= 3 else 2
            b0 = OBK[orr[0] % 4]
            b1 = OBK[(orr[0] + 1) % 4]
            orr[0] += 2
            banks = [b0, b1]
            first = [True, True]
            for ct in range(nct):
                M = 128

                def Afn(slot, ct=ct, M=M):
                    sb_ = SBK[slot]
                    mm(psf[sb_][0:M, 0:512], KCMPT[hp, ct * 128:ct * 128 + M], QT[hp, h % 4, I * 512:(I + 1) * 512],
                       True, True, ["KCMPT"] + qt_keys(I), [PK[sb_]])
                    add(ACT, lambda e: e.activation(out=Praw[slot][0:M, :], in_=psf[sb_][0:M, 0:512], func=AF.Exp, scale=0.125),
                        [PK[sb_]], [("Praw", slot)])
                    add(POOL, lambda e: e.affine_select(out=Pm[slot][0:M, :], in_=Praw[slot][0:M, :], pattern=[[1, 512]],
                                                        compare_op=ALU.is_ge, fill=0.0, base=512 * I - 2048 * ct - 31,
                                                        channel_multiplier=-16), [("Praw", slot)], [("Pm", slot)])

                def Bfn(slot, ct=ct, M=M):
                    for ql in range(4):
                        bk = banks[ql // 2]
                        st_ = first[ql // 2]
                        first[ql // 2] = False
                        mm(psf[bk][:, (ql % 2) * 129:(ql % 2) * 129 + 129], Pm[slot][0:M, ql * 128:(ql + 1) * 128],
                           VCA[0:M, ct, g, 0:129], st_, ct == nct - 1, [("Pm", slot), "VCA"], [PK[bk]])
                    if ct == nct - 1:
                        finish_job(0, h, banks)

                add_real(Afn, Bfn)

        def sel_items(h):
            g = h // 4
            bk = OBK[orr[0] % 4]
            orr[0] += 1
            first = [True]
            nkt = 4 * I + 4
            for kt in range(nkt):
                m = kt - 4 * I
                c0 = max(0, m) * 128

                def Afn(slot, kt=kt, m=m, c0=c0):
                    sb_ = SBK[slot]
                    mm(psf[sb_][:, c0:512], KSA[g][:, kt * 128:(kt + 1) * 128], AUG[a][:, h, c0:512], True, m < 0,
                       [("KSA", g, kt), ("KSAE", g), ("AUG", a, h)], [PK[sb_]])
                    if m >= 0:
                        mm(psf[sb_][:, c0:c0 + 128], identb[:], CBt[:], False, True, ["identb", "CB"], [PK[sb_]])
                    add(ACT, lambda e: e.activation(out=Praw[slot][:, c0:512], in_=psf[sb_][:, c0:512], func=AF.Exp, scale=0.125),
                        [PK[sb_]], [("Praw", slot)])

                def Bfn(slot, kt=kt, m=m):
                    for ql in range(max(0, m), 4):
                        st_ = first[0]
                        first[0] = False
                        mm(psf[bk][:, ql * 65:ql * 65 + 65], Praw[slot][:, ql * 128:(ql + 1) * 128], V2[:, kt, 0, g, 0:65],
                           st_, kt == 4 * I + ql, [("Praw", slot), ("V2", kt)], [PK[bk]])
                    if kt == nkt - 1:
                        finish_job(1, h, [bk])

                add_real(Afn, Bfn)

        def win_items(h):
            g = h // 4
            hp = slice(64 * g, 64 * g + 64)
            bk = OBK[orr[0] % 4]
            orr[0] += 1
            first = [True]
            kts = list(range(max(0, 4 * I - 4), 4 * I + 4))
            for kt in kts:
                q0 = max(0, kt - 4 * I)
                q1 = min(3, kt + 4 - 4 * I)
                c0, c1 = q0 * 128, (q1 + 1) * 128

                def Afn(slot, kt=kt, c0=c0, c1=c1):
                    sb_ = SBK[slot]
                    diag = kt >= 4 * I
                    old = (kt + 4 <= 4 * I + 3)
                    mm(psf[sb_][:, c0:c1], KWT[hp, kt * 128:(kt + 1) * 128], QT[hp, h % 4, I * 512 + c0:I * 512 + c1],
                       True, not (diag or old), [("KWT", kt)] + qt_keys(I), [PK[sb_]])
                    if diag:
                        qd = kt - 4 * I
                        mm(psf[sb_][:, qd * 128:(qd + 1) * 128], identb[:], CBt[:], False, not old, ["identb", "CB"], [PK[sb_]])
                    if old:
                        qo_ = kt + 4 - 4 * I
                        mm(psf[sb_][:, qo_ * 128:(qo_ + 1) * 128], identb[:], WBt[:], False, True, ["identb", "WB"], [PK[sb_]])
                    add(ACT, lambda e: e.activation(out=Praw[slot][:, c0:c1], in_=psf[sb_][:, c0:c1], func=AF.Exp, scale=0.125),
                        [PK[sb_]], [("Praw", slot)])

                def Bfn(slot, kt=kt, q0=q0, q1=q1):
                    for ql in range(q0, q1 + 1):
                        st_ = first[0]
                        first[0] = False
                        last_kt = 4 * I + ql
                        mm(psf[bk][:, ql * 65:ql * 65 + 65], Praw[slot][:, ql * 128:(ql + 1) * 128], V2[:, kt, 1, g, 0:65],
                           st_, kt == last_kt, [("Praw", slot), ("V2", kt)], [PK[bk]])
                    if kt == kts[-1]:
                        finish_job(2, h, [bk])

                add_real(Afn, Bfn)

        def aug_setup():
            add(POOL, lambda e: e.tensor_copy(out=AUG[a][0:64, 0:4, :], in_=QT[0:64, :, I * 512:(I + 1) * 512]),
                qt_keys(I), [("AUG", a, h) for h in range(4)])
            add(POOL, lambda e: e.tensor_copy(out=AUG[a][64:128, 4:8, :], in_=QT[64:128, :, I * 512:(I + 1) * 512]),
                qt_keys(I), [("AUG", a, h) for h in range(4, 8)])
            if I < 2:
                add(POOL, lambda e: e.memset(AUG[a][64:128, 0:4, :], 0.0), [], [("AUG", a, h) for h in range(4)])
                add(POOL, lambda e: e.memset(AUG[a][0:64, 4:8, :], 0.0), [], [("AUG", a, h) for h in range(4, 8)])

        def topk_dve():
            for ql in range(4):
                i = 4 * I + ql
                hi = 2 * i
                for g in range(2):
                    add(DVE, lambda e: e.tensor_tensor(out=IG[:], in0=impn[:, ql, 4 * g, :], in1=impn[:, ql, 4 * g + 1, :],
                                                       op=ALU.add), [("impn", 4 * g), ("impn", 4 * g + 1)], ["IG"])
                    add(DVE, lambda e: e.tensor_tensor(out=IG[:], in0=IG[:], in1=impn[:, ql, 4 * g + 2, :], op=ALU.add),
                        ["IG", ("impn", 4 * g + 2)], ["IG"])
                    add(DVE, lambda e: e.tensor_tensor(out=IG[:], in0=IG[:], in1=impn[:, ql, 4 * g + 3, :], op=ALU.add),
                        ["IG", ("impn", 4 * g + 3)], ["IG"])
                    add(DVE, lambda e: e.tensor_scalar(out=IG[:, hi - 1:hi], in0=IG[:, hi - 1:hi], scalar1=notHalf[:, 0:1],
                                                       scalar2=negHalf[:, 0:1], op0=ALU.mult, op1=ALU.add), ["IG", "half"], ["IG"])
                    add(DVE, lambda e: e.max(out=top16[:, 0:8], in_=IG[:, 1:hi]), ["IG"], ["top16"])
                    add(DVE, lambda e: e.match_replace(out=IG2[:, 1:hi], in_to_replace=top16[:, 0:8], in_values=IG[:, 1:hi],
                                                       imm_value=-1e30), ["IG", "top16"], ["IG2"])
                    add(DVE, lambda e: e.max(out=top16[:, 8:16], in_=IG2[:, 1:hi]), ["IG2"], ["top16"])
                    add(DVE, lambda e: e.tensor_scalar(out=Mb[:, ql, g, 1:hi], in0=IG[:, 1:hi], scalar1=top16[:, 12:13],
                                                       scalar2=NEGB, op0=ALU.is_lt, op1=ALU.mult), ["IG", "top16"], ["Mb"])
                    add(DVE, lambda e: e.tensor_scalar(out=Mb[:, ql, g, hi - 1:hi], in0=Mb[:, ql, g, hi - 1:hi],
                                                       scalar1=notHalf[:, 0:1], scalar2=None, op0=ALU.mult),
                        ["Mb", "half"], ["Mb"])

        def mb_transposes():
            for ql in range(4):
                for g in range(2):
                    prt = slice(64, 128) if g == 0 else slice(0, 64)
                    add(PE, lambda e: e.transpose(out=psb[TBK][prt, ql * 128:(ql + 1) * 128], in_=Mb[:, ql, g, :],
                                                  identity=identb[:]), ["Mb", "identb"], ["T7"])
            add(ACT, lambda e: e.copy(out=AUG[a][64:128, 0:4, :],
                                      in_=psb[TBK][64:128, 0:512].unsqueeze(1).to_broadcast([64, 4, 512])),
                ["T7"], [("AUG", a, h) for h in range(4)])
            add(ACT, lambda e: e.copy(out=AUG[a][0:64, 4:8, :],
                                      in_=psb[TBK][0:64, 0:512].unsqueeze(1).to_broadcast([64, 4, 512])),
                ["T7"], [("AUG", a, h) for h in range(4, 8)])

        def obf_out():
            add(POOL, lambda e: e.tensor_copy(out=obf[:], in_=oacc[:].rearrange("p q h d -> p q (h d)")),
                [("oacc", h) for h in range(8)], ["obf"])
            for ql in range(4):
                half = ql % 2
                for kc in range(4):
                    add(PE, lambda e: e.transpose(
                        out=psb[TBK][:, half * 512 + kc * 128:half * 512 + (kc + 1) * 128], in_=obf[:, ql, kc * 128:(kc + 1) * 128],
                        identity=identb[:]), ["obf", "identb"], ["T7"])
                add(DVE, lambda e: e.tensor_copy(
                    out=OT[:, :, I * 512 + ql * 128:I * 512 + (ql + 1) * 128],
                    in_=psb[TBK][:, half * 512:(half + 1) * 512].rearrange("p (k t) -> p k t", k=4)), ["T7"], [("OT", I)])

        items.append(('A', aug_setup))
        for h in range(8):
            cmp_items(h)
        if I >= 2:
            items.append(('B', topk_dve))
        for h in range(8):
            win_items(h)
        if I >= 2:
            items.append(('A', mb_transposes))
        for h in range(8):
            sel_items(h)
        items.append(('B', obf_out))

    for I_ in range(NB):
        do_block(I_)
    n_items = len(items)
    for i in range(n_items + 2):
        if i < n_items:
            it = items[i]
            if it[0] == 'R':
                it[1](it[3])
            elif it[0] == 'A':
                it[1]()
        if i >= 2:
            it = items[i - 2]
            if it[0] == 'R':
                it[2](it[3])
            elif it[0] == 'B':
                it[1]()

    if stop_after < 4:
        return finish()
    S.barrier()
    A.release(m_l2)
    W3 = A([128, 8, 2560], BF16)
    WBP = A([128, 4, D], BF16)
    WBA = A([128, 4, D], BF16)
    WOUT = A([128, 8, D], BF16)
    PW = A([128, 4, 128], BF16)
    pscl = A([128, 4], F32)
    XB = A([128, 4, D], F32)
    hb3 = A([128, D], BF16)
    hT3 = A([128, 8, 512], BF16)
    UT = A([128, 4, 528], F32)
    pt = [A([128, 528], F32), A([128, 528], F32)]
    pooled = A([128, 4, 512], BF16)
    mixed = A([128, 4, 512], BF16)
    sga = [A([128, 512], F32), A([128, 512], F32)]
    sgb = [A([128, 512], F32), A([128, 512], F32)]
    merged = A([128, 8, 512], BF16)
    st3 = A([128, 8], F32)
    XBK = [("XB", t) for t in range(4)]
    add(POOL, lambda e: e.memset(UT[:], 0.0), [], [("UT", gi) for gi in range(4)])

    wbp_v = wbp_d.rearrange("(k p) n -> p k n", p=128)
    wba_v = wba_d.rearrange("(k p) n -> p k n", p=128)
    wout_v = wout_d.rearrange("(k p) n -> p k n", p=128)
    add(PQ, lambda e: e.dma_start(out=W3[:, :, 0:512], in_=w_in_v[:, :, 1304:1816]), [], [("W3c", "u")])
    add(PQ, lambda e: e.dma_start(out=PW[:, :, :], in_=poolw_d.rearrange("(g p) n -> p g n", p=128)), [], ["PW"])

    def p3_rest_of_weights():
        for dc2 in range(4):
            add(PQ, lambda e: e.dma_start(out=W3[:, :, 512 + dc2 * 256:512 + (dc2 + 1) * 256],
                                          in_=w_in_v[:, :, 1816 + dc2 * 256:1816 + (dc2 + 1) * 256]), [], [("W3c", "a", dc2)])
            add(PQ, lambda e: e.dma_start(out=W3[:, :, 1536 + dc2 * 256:1536 + (dc2 + 1) * 256],
                                          in_=w_in_v[:, :, 2840 + dc2 * 256:2840 + (dc2 + 1) * 256]), [], [("W3c", "b", dc2)])
            if dc2 == 0:
                for k in range(4):
                    add(PQ, lambda e: e.dma_start(out=WBA[:, k, :], in_=wba_v[:, k, :]), [], ["WBA"])
                    add(PQ, lambda e: e.dma_start(out=WBP[:, k, :], in_=wbp_v[:, k, :]), [], ["WBP"])
        for k in range(8):
            add(PQ, lambda e: e.dma_start(out=WOUT[:, k, :], in_=wout_v[:, k, :]), [], ["WOUT"])

    for g_ in range(4):
        add(SP, lambda e: e.dma_start(out=pscl[:, g_:g_ + 1], in_=pscale_d[g_ * 128:(g_ + 1) * 128, 0:1]), [], ["pscl"])
    W3K = [("W3", k) for k in range(8)]

    XB2 = [XB, A([128, 4, D], F32)]
    hk3 = [("hT3", t) for t in range(4)]
    mk3 = [("mixed", gi) for gi in range(4)]

    def p3_load(I):
        xb = I % 2
        for tl in range(4):
            tt = 4 * I + tl
            add(SP, lambda e: e.dma_start(out=XB2[xb][:, tl, :], in_=x_d[tt * 128:(tt + 1) * 128, :]), [], [("XB", xb, tl)])

    def p3_norm_tile(I, tl):
        xb = I % 2
        rms_stats(XB2[xb][:, tl, :], ("XB", xb, tl), st3[:, 2 * tl:2 * tl + 1], st3[:, 2 * tl + 1:2 * tl + 2], ("st3", tl))
        add(DVE, lambda e: e.scalar_tensor_tensor(out=hb3[:], in0=XB2[xb][:, tl, :], scalar=st3[:, 2 * tl + 1:2 * tl + 2],
                                                  in1=gmix[:], op0=ALU.mult, op1=ALU.mult),
            [("XB", xb, tl), ("st3", tl), "gmix"], ["hb3"])
        tb = nb()
        for k in range(8):
            add(PE, lambda e: e.transpose(out=psb[tb][:, k * 128:(k + 1) * 128], in_=hb3[:, k * 128:(k + 1) * 128],
                                          identity=identb[:]), ["hb3", "identb"], [PK[tb]])
        add(ACT, lambda e: e.copy(out=hT3[:, :, tl * 128:(tl + 1) * 128],
                                  in_=psb[tb][:, :].rearrange("p (k t) -> p k t", k=8)), [PK[tb]], [("hT3", tl)])

    def p3_pool(I):
        for gi in range(4):
            w = 2 ** (gi + 1)
            add(POOL, lambda e: e.tensor_copy(out=UT[:, gi, 0:16], in_=UT[:, gi, 512:528]), [("UT", gi)], [("UT", gi)])
            p = nb()
            for k in range(8):
                mm(psf[p][:, 0:512], W3[:, k, gi * 128:(gi + 1) * 128], hT3[:, k, :], k == 0, k == 7, hk3 + [("W3c", "u")], [PK[p]])
            add(ACT, lambda e: e.copy(out=UT[:, gi, 16:528], in_=psf[p][:, 0:512]), [PK[p]], [("UT", gi)])
            src = UT[:, gi, :]
            srck = ("UT", gi)
            lo = 0
            step = 1
            j = 0
            while step < w:
                lo += step
                dstt = pt[j % 2]
                add(POOL, lambda e: e.tensor_tensor(out=dstt[:, lo:528], in0=src[:, lo:528], in1=src[:, lo - step:528 - step],
                                                    op=ALU.add), [srck], [("pt", j % 2)])
                src = dstt[:, :]
                srck = ("pt", j % 2)
                step *= 2
                j += 1
            add(DVE, lambda e: e.scalar_tensor_tensor(out=pooled[:, gi, :], in0=src[:, 16:528], scalar=1.0 / w,
                                                      in1=UT[:, gi, 16:528], op0=ALU.mult, op1=ALU.subtract),
                [srck, ("UT", gi)], [("pooled", gi)])
            if I == 0:
                for t in range(w - 1):
                    add(DVE, lambda e: e.scalar_tensor_tensor(
                        out=pooled[:, gi, t:t + 1], in0=src[:, 16 + t:17 + t], scalar=1.0 / (t + 1), in1=UT[:, gi, 16 + t:17 + t],
                        op0=ALU.mult, op1=ALU.subtract), [srck, ("UT", gi)], [("pooled", gi)])
        for gi in range(4):
            p2 = nb()
            mm(psf[p2][:, 0:512], PW[:, gi, :], pooled[:, gi, :], True, True, ["PW", ("pooled", gi)], [PK[p2]])
            add(DVE, lambda e: e.tensor_scalar(out=mixed[:, gi, :], in0=psf[p2][:, 0:512], scalar1=pscl[:, gi:gi + 1],
                                               scalar2=None, op0=ALU.mult), [PK[p2], "pscl"], [("mixed", gi)])

    def p3_merge(I):
        for dc in range(8):
            j = dc % 2
            pa, pb_, pga, pgb = nb(), nb(), nb(), nb()
            dsl = slice(dc * 128, (dc + 1) * 128)
            for k in range(8):
                mm(psf[pga][:, 0:512], W3[:, k, 512 + dc * 128:512 + (dc + 1) * 128], hT3[:, k, :], k == 0, k == 7,
                   hk3 + [("W3c", "a", dc // 2)], [PK[pga]])
            for k in range(8):
                mm(psf[pgb][:, 0:512], W3[:, k, 1536 + dc * 128:1536 + (dc + 1) * 128], hT3[:, k, :], k == 0, k == 7,
                   hk3 + [("W3c", "b", dc // 2)], [PK[pgb]])
            for k in range(4):
                mm(psf[pa][:, 0:512], WBA[:, k, dsl], OT[:, k, I * 512:(I + 1) * 512], k == 0, k == 3, ["WBA", ("OT", I)], [PK[pa]])
            for k in range(4):
                mm(psf[pb_][:, 0:512], WBP[:, k, dsl], mixed[:, k, :], k == 0, k == 3, ["WBP"] + mk3, [PK[pb_]])
            add(ACT, lambda e: e.activation(out=sga[j][:], in_=psf[pga][:, 0:512], func=AF.Sigmoid), [PK[pga]], [("sga", j)])
            add(ACT, lambda e: e.activation(out=sgb[j][:], in_=psf[pgb][:, 0:512], func=AF.Sigmoid), [PK[pgb]], [("sgb", j)])
            add(DVE, lambda e: e.tensor_tensor(out=sga[j][:], in0=sga[j][:], in1=psf[pa][:, 0:512], op=ALU.mult),
                [("sga", j), PK[pa]], [("sga", j)])
            add(DVE, lambda e: e.tensor_tensor(out=sgb[j][:], in0=sgb[j][:], in1=psf[pb_][:, 0:512], op=ALU.mult),
                [("sgb", j), PK[pb_]], [("sgb", j)])
            add(POOL, lambda e: e.tensor_tensor(out=merged[:, dc, :], in0=sga[j][:], in1=sgb[j][:], op=ALU.add),
                [("sga", j), ("sgb", j)], [("merged", dc)])

    def p3_out_tile(I, tl):
        xb = I % 2
        tt = 4 * I + tl
        mgk = [("merged", dc) for dc in range(8)]
        for half in range(2):
            p = nb()
            for k in range(8):
                mm(psf[p][:, 0:512], merged[:, k, tl * 128:(tl + 1) * 128], WOUT[:, k, half * 512:(half + 1) * 512],
                   k == 0, k == 7, mgk + ["WOUT"], [PK[p]])
            add(DVE, lambda e: e.tensor_tensor(
                out=XB2[xb][:, tl, half * 512:(half + 1) * 512], in0=XB2[xb][:, tl, half * 512:(half + 1) * 512],
                in1=psf[p][:, 0:512], op=ALU.add), [("XB", xb, tl), PK[p]], [("XB", xb, tl)])
        add(SP, lambda e: e.dma_start(out=x1_d[tt * 128:(tt + 1) * 128, :], in_=XB2[xb][:, tl, :]),
            [("XB", xb, tl)], [("x1d", tt)])

    p3_load(0)
    for tl in range(4):
        p3_norm_tile(0, tl)
    p3_pool(0)
    p3_rest_of_weights()
    for I in range(NB):
        p3_merge(I)
        if I + 1 < NB:
            p3_load(I + 1)
        for tl in range(4):
            if I + 1 < NB:
                p3_norm_tile(I + 1, tl)
            p3_out_tile(I, tl)
        if I + 1 < NB:
            p3_pool(I + 1)

    if stop_after < 5:
        return finish()
    S.barrier()
    A.release(m_l1)
    WF1 = A([128, 8, DFF], BF16)
    WF2 = A([128, 32, D], BF16)
    X1 = [A([128, 2, D], F32), A([128, 2, D], F32)]
    hb4 = A([128, D], BF16)
    h2T = A([128, 8, 256], BF16)
    fT = A([128, 32, 256], BF16)
    st4 = A([128, 8], F32)
    rl = [A([128, 256], F32), A([128, 256], F32)]
    wff1_v = wff1_d.rearrange("(k p) n -> p k n", p=128)
    wff2_v = wff2_d.rearrange("(k p) n -> p k n", p=128)
    for fg in range(8):
        add(PQ, lambda e: e.dma_start(out=WF1[:, :, fg * 512:(fg + 1) * 512], in_=wff1_v[:, :, fg * 512:(fg + 1) * 512]),
            [], [("WF1c", fg)])
    for k2 in range(16):
        add(PQ, lambda e: e.dma_start(out=WF2[:, 2 * k2:2 * k2 + 2, :], in_=wff2_v[:, 2 * k2:2 * k2 + 2, :]), [], [("WF2", k2)])
    WF1K = [("WF1", k) for k in range(8)]

    NB4 = S_LEN // 256
    h2Tb = [h2T, A([128, 8, 256], BF16)]
    st4b = [st4, A([128, 8], F32)]

    hb4d = [hb4, A([128, D], BF16)]

    def p4_stageA_norm(J):
        b = J % 2
        for tl in range(2):
            tt = 2 * J + tl
            add(SP, lambda e: e.dma_start(out=X1[b][:, tl, :], in_=x1_d[tt * 128:(tt + 1) * 128, :]),
                [("x1d", tt)], [("X1", b, tl)])
        for tl in range(2):
            rms_stats(X1[b][:, tl, :], ("X1", b, tl), st4b[b][:, 2 * tl:2 * tl + 1], st4b[b][:, 2 * tl + 1:2 * tl + 2], ("st4", b, tl))
            add(DVE, lambda e: e.scalar_tensor_tensor(out=hb4d[tl][:], in0=X1[b][:, tl, :], scalar=st4b[b][:, 2 * tl + 1:2 * tl + 2],
                                                      in1=gmlp[:], op0=ALU.mult, op1=ALU.mult),
                [("X1", b, tl), ("st4", b, tl), "gmlp"], [("hb4", tl)])

    def p4_stageA_tr(J):
        b = J % 2
        for tl in range(2):
            tb = nb()
            for k in range(8):
                add(PE, lambda e: e.transpose(out=psb[tb][:, k * 128:(k + 1) * 128], in_=hb4d[tl][:, k * 128:(k + 1) * 128],
                                              identity=identb[:]), [("hb4", tl), "identb"], [PK[tb]])
            add(ACT, lambda e: e.copy(out=h2Tb[b][:, :, tl * 128:(tl + 1) * 128],
                                      in_=psb[tb][:, :].rearrange("p (k t) -> p k t", k=8)), [PK[tb]], [("h2T", b, tl)])

    def p4_stageB(J):
        b = J % 2
        hk = [("h2T", b, 0), ("h2T", b, 1)]
        for fc in range(32):
            p = nb()
            for k in range(8):
                mm(psf[p][:, 0:256], WF1[:, k, fc * 128:(fc + 1) * 128], h2Tb[b][:, k, :], k == 0, k == 7, hk + [("WF1c", fc // 4)], [PK[p]])
            rj = fc % 2
            add(ACT, lambda e: e.activation(out=rl[rj][:], in_=psf[p][:, 0:256], func=AF.Relu), [PK[p]], [("rl", rj)])
            add(DVE if rj == 0 else POOL, lambda e: e.tensor_tensor(out=fT[:, fc, :], in0=rl[rj][:], in1=rl[rj][:], op=ALU.mult),
                [("rl", rj)], [("fT", fc)])
            if fc == 10 and J + 1 < NB4:
                p4_stageA_norm(J + 1)

    def p4_stageC(J):
        b = J % 2
        ftk = [("fT", fc) for fc in range(32)]
        for tl in range(2):
            tt = 2 * J + tl
            for half in range(2):
                p = nb()
                for fc in range(32):
                    mm(psf[p][:, 0:512], fT[:, fc, tl * 128:(tl + 1) * 128], WF2[:, fc, half * 512:(half + 1) * 512],
                       fc == 0, fc == 31, ftk + [("WF2", fc // 2)], [PK[p]])
                add(DVE, lambda e: e.tensor_tensor(
                    out=X1[b][:, tl, half * 512:(half + 1) * 512], in0=X1[b][:, tl, half * 512:(half + 1) * 512],
                    in1=psf[p][:, 0:512], op=ALU.add), [("X1", b, tl), PK[p]], [("X1", b, tl)])
            rms_stats(X1[b][:, tl, :], ("X1", b, tl), st4b[b][:, 4 + 2 * tl:5 + 2 * tl], st4b[b][:, 5 + 2 * tl:6 + 2 * tl], ("st4f", b, tl))
            add(DVE, lambda e: e.scalar_tensor_tensor(out=X1[b][:, tl, :], in0=X1[b][:, tl, :],
                                                      scalar=st4b[b][:, 5 + 2 * tl:6 + 2 * tl], in1=gfin[:], op0=ALU.mult,
                                                      op1=ALU.mult), [("X1", b, tl), ("st4f", b, tl), "gfin"], [("X1", b, tl)])
            add(SP, lambda e: e.dma_start(out=out_d[tt * 128:(tt + 1) * 128, :], in_=X1[b][:, tl, :]),
                [("X1", b, tl)], [("out", tt)])

    p4_stageA_norm(0)
    p4_stageA_tr(0)
    for J in range(NB4):
        p4_stageB(J)
        if J + 1 < NB4:
            p4_stageA_tr(J + 1)
        p4_stageC(J)

    return finish()


_INPUT_ORDER = ["x", "norm_mix", "w_in", "cmp_pe_k", "cmp_pe_v", "cmp_k_w1", "cmp_k_w2", "cmp_v_w1", "cmp_v_w2",
                "w_branch_attn", "pool_w", "pool_scale", "w_branch_pool", "w_out", "norm_mlp", "w_ff1", "w_ff2", "norm_final"]


def kernel(**inputs):
    f = lambda a: np.ascontiguousarray(np.asarray(a, dtype=np.float32))
    x = f(inputs["x"])
    shared = {
        "norm_mix": f(inputs["norm_mix"]).reshape(1, D),
        "w_in": f(inputs["w_in"]).reshape(D, 3864),
        "cmp_pe_k": f(inputs["cmp_pe_k"]).reshape(32, 64),
        "cmp_pe_v": f(inputs["cmp_pe_v"]).reshape(32, 64),
        "cmp_k_w1": f(inputs["cmp_k_w1"]).reshape(2048, 256),
        "cmp_k_w2": f(inputs["cmp_k_w2"]).reshape(256, 64),
        "cmp_v_w1": f(inputs["cmp_v_w1"]).reshape(2048, 256),
        "cmp_v_w2": f(inputs["cmp_v_w2"]).reshape(256, 64),
        "w_branch_attn": f(inputs["w_branch_attn"]).reshape(512, D),
        "pool_w": f(inputs["pool_w"]).reshape(512, 128),
        "pool_scale": f(inputs["pool_scale"]).reshape(512, 1),
        "w_branch_pool": f(inputs["w_branch_pool"]).reshape(512, D),
        "w_out": f(inputs["w_out"]).reshape(D, D),
        "norm_mlp": f(inputs["norm_mlp"]).reshape(1, D),
        "w_ff1": f(inputs["w_ff1"]).reshape(D, DFF),
        "w_ff2": f(inputs["w_ff2"]).reshape(DFF, D),
        "norm_final": f(inputs["norm_final"]).reshape(1, D),
    }
    nc = build_nc()
    in_maps = []
    for c in range(8):
        m = dict(shared)
        m["x"] = np.ascontiguousarray(x[c])
        in_maps.append(m)
    res = run_bass_kernel_spmd(nc, in_maps, core_ids=list(range(8)))
    return np.stack([np.asarray(r["out"], dtype=np.float32).reshape(S_LEN, D) for r in res.results], axis=0)
```

```python
import os
import numpy as np
import concourse.bass as bass
import concourse.mybir as mybir
from concourse.bass_utils import run_bass_kernel_spmd

F32 = mybir.dt.float32
BF16 = mybir.dt.bfloat16
I32 = mybir.dt.int32
ALU = mybir.AluOpType
AF = mybir.ActivationFunctionType

PE, ACT, DVE, POOL, SP, PQ = "pe", "act", "dve", "pool", "sp", "pq"
COMPUTE = (PE, ACT, DVE, POOL)
N_DMA_SEMS = 16
N_PQ_SEMS = 8
N_ALL_DSEMS = N_DMA_SEMS + N_PQ_SEMS

S_LEN = 4096
D = 1024
NT = 32
NB = 8
DFF = 4096
NEGB = -3000.0
INV_FREQ = [float(np.float32(500000.0) ** (-np.float32(2 * i) / np.float32(16))) for i in range(8)]
TWO_PI = float(2 * np.pi)


class Op:
    __slots__ = ("eng", "emit", "deps", "idx", "dsem", "dval", "dprev")

    def __init__(self, eng, emit):
        self.eng = eng
        self.emit = emit
        self.deps = set()
        self.idx = 0
        self.dsem = None
        self.dval = 0
        self.dprev = 0


class Sched:
    def __init__(self):
        self.streams = {e: [] for e in COMPUTE + (SP,)}
        self.nreal = {e: 0 for e in COMPUTE + (SP,)}
        self.lastreal = {e: None for e in COMPUTE}
        self.last_w = {}
        self.readers = {}
        self.dma_rr = 0
        self.pq_rr = 0
        self.dma_cnt = [0] * N_ALL_DSEMS
        self.dma_last = [None] * N_ALL_DSEMS

    def add(self, eng, emit, reads=(), writes=()):
        op = Op(eng, emit)
        for k in reads:
            w = self.last_w.get(k)
            if w is not None:
                op.deps.add(w)
        for k in writes:
            w = self.last_w.get(k)
            if w is not None:
                op.deps.add(w)
            for r in self.readers.get(k, ()):
                op.deps.add(r)
        for k in writes:
            self.last_w[k] = op
            self.readers[k] = []
        ws = set(writes)
        for k in reads:
            if k not in ws:
                self.readers.setdefault(k, []).append(op)
        op.deps.discard(op)
        self.streams[POOL if eng == PQ else eng].append(op)
        if eng == SP or eng == PQ:
            if eng == SP:
                s = self.dma_rr
                self.dma_rr = (self.dma_rr + 1) % N_DMA_SEMS
            else:
                s = N_DMA_SEMS + self.pq_rr
                self.pq_rr = (self.pq_rr + 1) % N_PQ_SEMS
            op.dsem = s
            op.dprev = self.dma_cnt[s] * 16
            self.dma_cnt[s] += 1
            op.dval = self.dma_cnt[s] * 16
            self.dma_last[s] = op
        else:
            self.nreal[eng] += 1
            op.idx = self.nreal[eng]
            self.lastreal[eng] = op
        return op

    def barrier(self):
        deps = set()
        for e in COMPUTE:
            if self.lastreal[e] is not None:
                deps.add(self.lastreal[e])
        for s in range(N_ALL_DSEMS):
            if self.dma_last[s] is not None:
                deps.add(self.dma_last[s])
        for e in COMPUTE + (SP,):
            op = Op(e, None)
            op.deps = set(deps)
            self.streams[e].append(op)
        self.last_w = {}
        self.readers = {}

    def emit_all(self, block, sems, dsems):
        engmap = {PE: "tensor", ACT: "scalar", DVE: "vector", POOL: "gpsimd", SP: "sync"}
        for ename in (SP, PE, ACT, DVE, POOL):
            ops = self.streams[ename]

            def body(eng, ops=ops, ename=ename):
                waited = {}
                for op in ops:
                    need = {}
                    for d in op.deps:
                        if d.eng == SP or d.eng == PQ:
                            key = ("d", d.dsem)
                            val = d.dval
                        else:
                            if d.eng == ename and ename == PE:
                                continue
                            key = ("c", d.eng)
                            val = d.idx
                        if need.get(key, 0) < val:
                            need[key] = val
                    if (op.eng == SP or op.eng == PQ) and op.emit is not None and op.dprev > 0:
                        key = ("d", op.dsem)
                        if need.get(key, 0) < op.dprev:
                            need[key] = op.dprev
                    for key, val in need.items():
                        if waited.get(key, 0) >= val:
                            continue
                        waited[key] = val
                        sem = dsems[key[1]] if key[0] == "d" else sems[key[1]]
                        eng.wait_ge(sem, val)
                    if op.emit is None:
                        continue
                    ins = op.emit(eng)
                    if op.eng == SP or op.eng == PQ:
                        ins.then_inc(dsems[op.dsem], 16)
                    else:
                        ins.then_inc(sems[op.eng], 1)
                if ename == SP:
                    for s in range(N_DMA_SEMS):
                        if self.dma_cnt[s] > 0 and waited.get(("d", s), 0) < self.dma_cnt[s] * 16:
                            eng.wait_ge(dsems[s], self.dma_cnt[s] * 16)

            getattr(block, engmap[ename])(body)


class Alloc:
    def __init__(self, nc, base=16512, top=229344):
        self.nc = nc
        self.cur = base
        self.top = top
        self.n = 0

    def mark(self):
        return self.cur

    def release(self, m):
        self.cur = m

    def __call__(self, shape, dt):
        nbytes = int(np.prod(shape[1:])) * (2 if dt == BF16 else 4)
        nbytes = (nbytes + 31) // 32 * 32
        off = self.cur
        assert off + nbytes <= self.top, ("SBUF overflow", off, nbytes, self.top)
        self.cur += nbytes
        self.n += 1
        return self.nc.alloc_sbuf_tensor_at("t%d" % self.n, list(shape), dt, offset=off)


def build_nc(stop_after=99):
    nc = bass.Bass("TRN2", target_bir_lowering=False)

    def din(name, shape):
        return nc.dram_tensor(name, list(shape), F32, kind="ExternalInput").ap()

    x_d = din("x", [S_LEN, D])
    norm_mix_d = din("norm_mix", [1, D])
    w_in_d = din("w_in", [D, 3864])
    pe_k_d = din("cmp_pe_k", [32, 64])
    pe_v_d = din("cmp_pe_v", [32, 64])
    w1k_d = din("cmp_k_w1", [2048, 256])
    w2k_d = din("cmp_k_w2", [256, 64])
    w1v_d = din("cmp_v_w1", [2048, 256])
    w2v_d = din("cmp_v_w2", [256, 64])
    wba_d = din("w_branch_attn", [512, D])
    poolw_d = din("pool_w", [512, 128])
    pscale_d = din("pool_scale", [512, 1])
    wbp_d = din("w_branch_pool", [512, D])
    wout_d = din("w_out", [D, D])
    norm_mlp_d = din("norm_mlp", [1, D])
    wff1_d = din("w_ff1", [D, DFF])
    wff2_d = din("w_ff2", [DFF, D])
    norm_fin_d = din("norm_final", [1, D])
    out_d = nc.dram_tensor("out", [S_LEN, D], F32, kind="ExternalOutput").ap()
    x1_d = nc.dram_tensor("x1_scratch", [S_LEN, D], F32).ap()

    S = Sched()
    A = Alloc(nc)

    class _Rec:
        def __getattr__(self, name):
            def f(*a, **kw):
                return (name, a, kw)
            return f

    _REC = _Rec()

    def add(eng, fn, reads=(), writes=()):
        call = fn(_REC)
        op = S.add(eng, lambda e: getattr(e, call[0])(*call[1], **call[2]), reads, writes)
        if os.environ.get("MK_DUMP"):
            S.dbg = getattr(S, "dbg", [])
            S.dbg.append((eng, op.idx, call[0], [(d.eng, d.idx, d.dval) for d in op.deps], list(reads), list(writes)))
        return op


    def finish():
        from contextlib import ExitStack
        with ExitStack() as es:
            sems = {e: es.enter_context(nc.semaphore("sem_" + e)) for e in COMPUTE}
            dsems = [es.enter_context(nc.semaphore("dsem%d" % i)) for i in range(N_ALL_DSEMS)]
            block = es.enter_context(nc.Block())
            S.emit_all(block, sems, dsems)
        return nc

    psf = [nc.alloc_psum_tensor("psb%d" % i, [128, 512], F32) for i in range(8)]
    psb = [p[:, :].bitcast(BF16) for p in psf]
    PK = [("ps", i) for i in range(8)]
    rr = [0]

    def nb():
        i = rr[0]
        rr[0] = (i + 1) % 8
        return i

    def mm(out, lhsT, rhs, start, stop, r, w):
        add(PE, lambda e: e.matmul(out, lhsT=lhsT, rhs=rhs, start=start, stop=stop, skip_group_check=True), r, w)

    identb = A([128, 128], BF16)
    onesf = A([128, 128], F32)
    zerosb = A([128, 128], BF16)
    CBt = A([128, 128], BF16)
    WBt = A([128, 128], BF16)
    gmix = A([128, D], F32)
    gmlp = A([128, D], F32)
    gfin = A([128, D], F32)
    epst = A([128, 1], F32)
    notHalf = A([128, 1], F32)
    negHalf = A([128, 1], F32)
    junk = A([128, D], BF16)

    add(POOL, lambda e: e.memset(onesf[:], 1.0), [], ["onesf"])
    add(POOL, lambda e: e.affine_select(out=notHalf[:], in_=onesf[:, 0:1], pattern=[[0, 1]], compare_op=ALU.is_ge,
                                        fill=0.0, base=-64, channel_multiplier=1), ["onesf"], ["half"])
    add(POOL, lambda e: e.tensor_scalar(out=negHalf[:], in0=notHalf[:], scalar1=-1.0, scalar2=None, op0=ALU.add), ["half"], ["half"])
    add(POOL, lambda e: e.memset(zerosb[:], 0.0), [], ["zerosb"])
    add(POOL, lambda e: e.memset(epst[:], 1e-6), [], ["eps"])
    add(POOL, lambda e: e.affine_select(out=identb[:], in_=onesf[:], pattern=[[-1, 128]], compare_op=ALU.is_equal,
                                        fill=0.0, base=0, channel_multiplier=1), ["onesf"], ["identb"])
    add(POOL, lambda e: e.affine_select(out=CBt[:], in_=zerosb[:], pattern=[[1, 128]], compare_op=ALU.is_ge,
                                        fill=NEGB, base=0, channel_multiplier=-1), ["zerosb"], ["CB"])
    add(POOL, lambda e: e.affine_select(out=WBt[:], in_=zerosb[:], pattern=[[-1, 128]], compare_op=ALU.is_ge,
                                        fill=NEGB, base=-1, channel_multiplier=1), ["zerosb"], ["WB"])
    add(SP, lambda e: e.dma_start(out=gmix[:], in_=norm_mix_d.partition_broadcast(128)), [], ["gmix"])
    add(SP, lambda e: e.dma_start(out=gmlp[:], in_=norm_mlp_d.partition_broadcast(128)), [], ["gmlp"])
    add(SP, lambda e: e.dma_start(out=gfin[:], in_=norm_fin_d.partition_broadcast(128)), [], ["gfin"])

    def rms_stats(xt_ap, kx, ss, rstd, kst):
        add(ACT, lambda e: e.activation(out=junk[:], in_=xt_ap, func=AF.Square, accum_out=ss), [kx], ["junk", kst])
        add(ACT, lambda e: e.activation(out=rstd, in_=ss, func=AF.Sqrt, bias=epst[:, 0:1], scale=1.0 / D),
            [kst, "eps"], [kst])
        add(DVE, lambda e: e.reciprocal(out=rstd, in_=rstd), [kst], [kst])

    m_l1 = A.mark()
    OT = A([128, 4, S_LEN], BF16)
    m_l2 = A.mark()
    QT = A([128, 4, S_LEN], BF16)
    KSA = [A([128, S_LEN], BF16), A([128, S_LEN], BF16)]
    KWT = A([128, S_LEN], BF16)
    V2 = A([128, NT, 2, 2, 66], BF16)
    G = A([128, NT, 24], F32)
    KCMPT = A([128, 256], BF16)
    VCA = A([128, 2, 2, 130], BF16)
    CS = A([128, NT, 16], F32)
    CSC = A([128, 2, 16], F32)
    m_p1 = A.mark()
    KCT = A([128, S_LEN], BF16)
    VCT = A([128, S_LEN], BF16)
    m_p1a = A.mark()

    posi = A([128, NT], I32)
    posf = A([128, NT], F32)
    ang = A([128, NT * 16], F32)
    angn = A([128, NT * 16], F32)
    angi = A([128, NT * 16], I32)

    def build_cs(dst, ntile, base, cm, step):
        n = ntile * 16
        add(POOL, lambda e: e.iota(posi[:, 0:ntile], pattern=[[step, ntile]], base=base, channel_multiplier=cm), [], ["posi"])
        add(POOL, lambda e: e.tensor_copy(out=posf[:, 0:ntile], in_=posi[:, 0:ntile]), ["posi"], ["posf"])
        a3 = ang[:, 0:n].rearrange("p (t f) -> p t f", f=16)
        for f in range(8):
            add(DVE, lambda e, f=f: e.tensor_scalar(out=a3[:, :, 8 + f], in0=posf[:, 0:ntile], scalar1=INV_FREQ[f],
                                                    scalar2=None, op0=ALU.mult), ["posf"], ["ang"])
        add(DVE, lambda e: e.tensor_scalar(out=a3[:, :, 0:8], in0=a3[:, :, 8:16], scalar1=float(np.pi / 2), scalar2=None,
                                           op0=ALU.add), ["ang"], ["ang"])
        av, nv, iv = ang[:, 0:n], angn[:, 0:n], angi[:, 0:n]
        add(DVE, lambda e: e.tensor_scalar(out=nv, in0=av, scalar1=1.0 / TWO_PI, scalar2=None, op0=ALU.mult), ["ang"], ["angn"])
        add(DVE, lambda e: e.tensor_copy(out=iv, in_=nv), ["angn"], ["angi"])
        add(DVE, lambda e: e.tensor_copy(out=nv, in_=iv), ["angi"], ["angn"])
        add(DVE, lambda e: e.scalar_tensor_tensor(out=av, in0=nv, scalar=-6.28125, in1=av, op0=ALU.mult, op1=ALU.add),
            ["ang", "angn"], ["ang"])
        add(DVE, lambda e: e.scalar_tensor_tensor(out=av, in0=nv, scalar=-(TWO_PI - 6.28125), in1=av, op0=ALU.mult,
                                                  op1=ALU.add), ["ang", "angn"], ["ang"])
        add(DVE, lambda e: e.tensor_scalar(out=nv, in0=av, scalar1=float(np.pi), scalar2=-TWO_PI, op0=ALU.is_gt,
                                           op1=ALU.mult), ["ang"], ["angn"])
        add(DVE, lambda e: e.tensor_tensor(out=av, in0=av, in1=nv, op=ALU.add), ["ang", "angn"], ["ang"])
        add(DVE, lambda e: e.tensor_scalar(out=nv, in0=av, scalar1=-float(np.pi), scalar2=TWO_PI, op0=ALU.is_lt,
                                           op1=ALU.mult), ["ang"], ["angn"])
        add(DVE, lambda e: e.tensor_tensor(out=av, in0=av, in1=nv, op=ALU.add), ["ang", "angn"], ["ang"])
        add(DVE, lambda e: e.tensor_scalar(out=av, in0=av, scalar1=3.141592, scalar2=-3.141592, op0=ALU.min,
                                           op1=ALU.max), ["ang"], ["ang"])
        add(ACT, lambda e: e.activation(out=dst, in_=av, func=AF.Sin), ["ang"], ["cs"])

    build_cs(CS[:, :, :].rearrange("p t f -> p (t f)"), NT, 0, 1, 128)
    build_cs(CSC[:, :, :].rearrange("p t f -> p (t f)"), 2, 31, 16, 2048)

    for g_, prt_ in ((0, slice(64, 128)), (1, slice(0, 64))):
        ev = KSA[g_][prt_, :].rearrange("p (j c) -> p j c", c=64)
        add(POOL, lambda e: e.memset(KSA[g_][prt_, :], 1.0), [], [("KSAE", g_)])
        add(POOL, lambda e: e.affine_select(out=ev, in_=ev, pattern=[[-1, 64], [0, 64]], compare_op=ALU.is_equal,
                                            fill=0.0, base=0, channel_multiplier=1), [("KSAE", g_)], [("KSAE", g_)])
    v2keys = [("V2", t) for t in range(NT)]
    add(POOL, lambda e: e.memset(V2[:], 1.0), [], v2keys)
    add(POOL, lambda e: e.memset(VCA[:], 1.0), [], ["VCA"])
    mtmp = A([128, 2, 64], F32)
    mtmp2 = A([128, 2, 64], F32)
    for ct in range(2):
        add(POOL, lambda e, ct=ct: e.affine_select(out=mtmp[:, ct, :], in_=onesf[:, 0:64], pattern=[[-4, 64]],
                                                   compare_op=ALU.is_ge, fill=0.0, base=128 * ct + 1, channel_multiplier=1),
            ["onesf"], ["mtmp"])
        add(POOL, lambda e, ct=ct: e.affine_select(out=mtmp[:, ct, :], in_=mtmp[:, ct, :], pattern=[[4, 64]],
                                                   compare_op=ALU.is_ge, fill=0.0, base=3 - 128 * ct, channel_multiplier=-1),
            ["mtmp"], ["mtmp"])
        add(POOL, lambda e, ct=ct: e.affine_select(out=mtmp2[:, ct, :], in_=onesf[:, 0:64], pattern=[[-4, 64]],
                                                   compare_op=ALU.is_ge, fill=0.0, base=128 * ct, channel_multiplier=1),
            ["onesf"], ["mtmp2"])
        add(POOL, lambda e, ct=ct: e.affine_select(out=mtmp2[:, ct, :], in_=mtmp2[:, ct, :], pattern=[[4, 64]],
                                                   compare_op=ALU.is_ge, fill=0.0, base=2 - 128 * ct, channel_multiplier=-1),
            ["mtmp2"], ["mtmp2"])
    add(POOL, lambda e: e.tensor_tensor(out=mtmp[:], in0=mtmp[:], in1=mtmp2[:], op=ALU.add), ["mtmp", "mtmp2"], ["mtmp"])
    for g in range(2):
        add(POOL, lambda e, g=g: e.tensor_copy(out=VCA[:, :, g, 65:129], in_=mtmp[:]), ["mtmp"], ["VCA"])

    if stop_after < 1:
        return finish()
    WA = A([128, 8, 1304], BF16)
    xt = [A([128, D], F32), A([128, D], F32)]
    hbf = [A([128, D], BF16), A([128, D], BF16)]
    hT = [A([128, 8, 512], BF16), A([128, 8, 512], BF16)]
    qr = [A([128, 768], BF16), A([128, 768], BF16)]
    rt = A([128, 4, 64], F32)
    st = [A([128, 2], F32), A([128, 2], F32)]

    w_in_v = w_in_d.rearrange("(k p) n -> p k n", p=128)
    cast_engs = [POOL, ACT, DVE]
    for k in range(8):
        add(PQ, lambda e: e.dma_start(out=WA[:, k, :], in_=w_in_v[:, k, 0:1304]), [], [("WA", k)])
    WAK = [("WA", k) for k in range(8)]

    def stage_norm(tt):
        b = tt % 2
        add(SP, lambda e: e.dma_start(out=xt[b][:], in_=x_d[tt * 128:(tt + 1) * 128, :]), [], [("xt", b)])
        rms_stats(xt[b][:], ("xt", b), st[b][:, 0:1], st[b][:, 1:2], ("st", b))
        add(DVE, lambda e: e.scalar_tensor_tensor(out=hbf[b][:], in0=xt[b][:], scalar=st[b][:, 1:2], in1=gmix[:],
                                                  op0=ALU.mult, op1=ALU.mult), [("xt", b), ("st", b), "gmix"], [("hbf", b)])

    def stage_proj(tt):
        b = tt % 2
        I, tl = tt // 4, tt % 4
        bI = I % 2
        tb = nb()
        for k in range(8):
            add(PE, lambda e, k=k: e.transpose(out=psb[tb][:, k * 128:(k + 1) * 128], in_=hbf[b][:, k * 128:(k + 1) * 128],
                                               identity=identb[:]), [("hbf", b), "identb"], [PK[tb]])
        add(ACT, lambda e: e.copy(out=hT[bI][:, :, tl * 128:(tl + 1) * 128],
                                  in_=psb[tb][:, :].rearrange("p (k t) -> p k t", k=8)), [PK[tb]], [("hT", bI, tl)])
        pa, pb, pc = nb(), nb(), nb()
        for k in range(8):
            mm(psf[pa][:, 0:512], hT[bI][:, k, tl * 128:(tl + 1) * 128], WA[:, k, 0:512], k == 0, k == 7,
               [("hT", bI, tl), ("WA", k)], [PK[pa]])
        for k in range(8):
            mm(psf[pb][:, 0:512], hT[bI][:, k, tl * 128:(tl + 1) * 128], WA[:, k, 768:1280], k == 0, k == 7,
               [("hT", bI, tl), ("WA", k)], [PK[pb]])
        for k in range(8):
            mm(psf[pc][:, 0:24], hT[bI][:, k, tl * 128:(tl + 1) * 128], WA[:, k, 1280:1304], k == 0, k == 7,
               [("hT", bI, tl), ("WA", k)], [PK[pc]])
        if LVL < 4:
            return
        cosq = CS[:, tt, 0:8].unsqueeze(1).unsqueeze(1).to_broadcast([128, 2, 4, 8])
        sinq = CS[:, tt, 8:16].unsqueeze(1).unsqueeze(1).to_broadcast([128, 2, 4, 8])
        qv = psf[pa][:, 0:512].rearrange("p (g i d) -> p g i d", g=2, i=4)
        qo = qr[b][:, 0:512].rearrange("p (i g d) -> p g i d", i=4, g=2)
        kv4 = psf[pb][:, 0:512].rearrange("p (a r d) -> p a r d", a=2, r=4)
        kvw = kv4[:, :, 0:2, :]
        vvw = kv4[:, :, 2:4, :]
        ko = qr[b][:, 512:768].rearrange("p (a g d) -> p a g d", a=2, g=2)
        cosk = CS[:, tt, 0:8].unsqueeze(1).unsqueeze(1).to_broadcast([128, 2, 2, 8])
        sink = CS[:, tt, 8:16].unsqueeze(1).unsqueeze(1).to_broadcast([128, 2, 2, 8])

        def rope(src, dst, cos, sin, n, rk, wk):
            shp = dict(g=2, i=n)
            t1 = rt[:, 0, 0:16 * n].rearrange("p (g i d) -> p g i d", **shp)
            t2 = rt[:, 1, 0:16 * n].rearrange("p (g i d) -> p g i d", **shp)
            t3 = rt[:, 2, 0:16 * n].rearrange("p (g i d) -> p g i d", **shp)
            t4 = rt[:, 3, 0:16 * n].rearrange("p (g i d) -> p g i d", **shp)
            x1, x2 = src[:, :, :, 0:8], src[:, :, :, 8:16]
            add(DVE, lambda e: e.tensor_tensor(out=t1, in0=x1, in1=cos, op=ALU.mult), rk + ["cs"], ["rt1"])
            add(DVE, lambda e: e.tensor_tensor(out=t2, in0=x2, in1=sin, op=ALU.mult), rk + ["cs"], ["rt2"])
            add(DVE, lambda e: e.tensor_tensor(out=t3, in0=x2, in1=cos, op=ALU.mult), rk + ["cs"], ["rt3"])
            add(DVE, lambda e: e.tensor_tensor(out=t4, in0=x1, in1=sin, op=ALU.mult), rk + ["cs"], ["rt4"])
            add(DVE, lambda e: e.tensor_tensor(out=dst[:, :, :, 0:8], in0=t1, in1=t2, op=ALU.subtract), ["rt1", "rt2"], wk)
            add(DVE, lambda e: e.tensor_tensor(out=dst[:, :, :, 8:16], in0=t3, in1=t4, op=ALU.add), ["rt3", "rt4"], wk)
            add(ACT, lambda e: e.copy(out=dst[:, :, :, 16:64], in_=src[:, :, :, 16:64]), rk, wk)

        rope(qv, qo, cosq, sinq, 4, [PK[pa]], [("qr", b)])
        rope(kvw, ko, cosk, sink, 2, [PK[pb]], [("qr", b)])
        add(ACT, lambda e: e.copy(out=V2[:, tt, :, :, 0:64], in_=vvw), [PK[pb]], [("V2", tt)])
        add(ACT, lambda e: e.copy(out=G[:, tt, :], in_=psf[pc][:, 0:24]), [PK[pc]], [("G", tt)])

    def stage_qT(tt):
        b = tt % 2
        tb = nb()
        for m in range(6):
            add(PE, lambda e, m=m: e.transpose(out=psb[tb][:, m * 128:(m + 1) * 128], in_=qr[b][:, m * 128:(m + 1) * 128],
                                               identity=identb[:]), [("qr", b), "identb"], [PK[tb]])
        cs = slice(tt * 128, (tt + 1) * 128)
        add(ACT, lambda e: e.copy(out=QT[:, :, cs], in_=psb[tb][:, 0:512].rearrange("p (m t) -> p m t", m=4)),
            [PK[tb]], [("QT", tt)])
        if LVL < 7:
            return
        add(ACT, lambda e: e.copy(out=KSA[0][0:64, cs], in_=psb[tb][0:64, 512:640]), [PK[tb]], [("KSA", 0, tt)])
        add(ACT, lambda e: e.copy(out=KSA[1][64:128, cs], in_=psb[tb][64:128, 512:640]), [PK[tb]], [("KSA", 1, tt)])
        if LVL < 8:
            return
        add(ACT, lambda e: e.copy(out=KWT[:, cs], in_=psb[tb][:, 640:768]), [PK[tb]], [("KWT", tt)])

    def stage_fm(I):
        bI = I % 2
        for m, dst, key in ((0, KCT, "KCT"), (1, VCT, "VCT")):
            p = nb()
            for k in range(8):
                mm(psf[p][:, 0:512], WA[:, k, 512 + 128 * m:640 + 128 * m], hT[bI][:, k, :], k == 0, k == 7,
                   [("hT", bI, 0), ("hT", bI, 1), ("hT", bI, 2), ("hT", bI, 3), ("WA", k)], [PK[p]])
            if m == 0:
                add(DVE, lambda e, p=p, dst=dst: e.tensor_copy(out=dst[:, I * 512:(I + 1) * 512], in_=psf[p][:, 0:512]),
                    [PK[p]], [(key, I)])
            else:
                add(ACT, lambda e, p=p, dst=dst: e.copy(out=dst[:, I * 512:(I + 1) * 512], in_=psf[p][:, 0:512]),
                    [PK[p]], [(key, I)])

    LVL = int(os.environ.get("MK_LVL", "9"))
    NTR = int(os.environ.get("MK_NT", str(NT)))
    for it in range(NTR + 2):
        if it < NTR and LVL >= 2:
            stage_norm(it)
        if 1 <= it <= NTR and LVL >= 3:
            stage_proj(it - 1)
            if (it - 1) % 4 == 3 and LVL >= 5:
                stage_fm((it - 1) // 4)
        if it >= 2 and LVL >= 6:
            stage_qT(it - 2)

    if stop_after < 2:
        return finish()
    gkeys_all = [("G", t_) for t_ in range(NT)]
    add(ACT, lambda e: e.activation(out=G[:, :, :].rearrange("p t c -> p (t c)"), in_=G[:, :, :].rearrange("p t c -> p (t c)"),
                                    func=AF.Sigmoid), gkeys_all, gkeys_all)
    S.barrier()
    A.release(m_p1a)
    W1 = [A([128, 32, 256], BF16), A([128, 32, 256], BF16)]
    pass
    W2 = A([128, 2, 2, 64], BF16)
    w2st = A([128, 2, 2, 64], F32)
    pest = A([32, 2, 64], F32)
    pebf = A([32, 2, 64], BF16)
    peT = A([64, 2, 32], BF16)
    hbias = A([128, 4], F32)
    zt = A([128, 256], F32)
    z2t = A([128, 256], F32)
    sgt = A([128, 256], F32)
    gel = [[A([128, 2, 256], BF16) for g in range(2)] for xi in range(2)]
    kcr = A([128, 128], BF16)
    rt = A([128, 4, 64], F32)
    for xi_ in range(2):
        for g_ in range(2):
            add(POOL, lambda e: e.memset(gel[xi_][g_][:], 0.0), [], [("gel", xi_, g_)])

    for xi, w1d in enumerate((w1k_d, w1v_d)):
        w1v_ = w1d.rearrange("(p d) n -> d p n", d=64)
        for j in range(4):
            for dup in range(2):
                add(PQ, lambda e: e.dma_start(out=W1[xi][dup * 64:(dup + 1) * 64, 8 * j:8 * j + 8, :],
                                              in_=w1v_[:, 8 * j:8 * j + 8, :]), [], [("W1", xi)])
    for xi, (w2d, ped) in enumerate(((w2k_d, pe_k_d), (w2v_d, pe_v_d))):
        add(SP, lambda e, xi=xi, w2d=w2d: e.dma_start(out=w2st[:, xi, :, :], in_=w2d.rearrange("(c p) n -> p c n", p=128)),
            [], [("w2st", xi)])
        add(SP, lambda e, xi=xi, ped=ped: e.dma_start(out=pest[:, xi, :], in_=ped[:, :]), [], [("pest", xi)])
    add(DVE, lambda e: e.tensor_copy(out=W2[:], in_=w2st[:]), [("w2st", 0), ("w2st", 1)], ["W2"])
    add(DVE, lambda e: e.tensor_copy(out=pebf[:], in_=pest[:]), [("pest", 0), ("pest", 1)], ["pebf"])
    L2 = int(os.environ.get("MK_L2", "9"))
    if L2 < 2:
        return finish()
    tb = nb()
    for xi in range(2):
        add(PE, lambda e, xi=xi: e.transpose(out=psb[tb][0:64, xi * 32:(xi + 1) * 32], in_=pebf[0:32, xi, :],
                                             identity=identb[0:32, 0:32]), ["pebf", "identb"], [PK[tb]])
    add(ACT, lambda e: e.copy(out=peT[:], in_=psb[tb][0:64, 0:64].rearrange("p (x t) -> p x t", x=2)), [PK[tb]], ["peT"])
    pbias = nb()
    for xi in range(2):
        for c in range(2):
            col = 2 * xi + c
            for p in range(32):
                mm(psf[pbias][:, col:col + 1], W1[xi][0:64, p, c * 128:(c + 1) * 128], peT[0:64, xi, p:p + 1],
                   p == 0, p == 31, [("W1", xi), "peT"], [PK[pbias]])
    add(DVE, lambda e: e.tensor_copy(out=hbias[:], in_=psf[pbias][:, 0:4]), [PK[pbias]], ["hbias"])
    if L2 < 3:
        return finish()
    KCK = [("KCT", I) for I in range(NB)]
    VCK = [("VCT", I) for I in range(NB)]
    for xi, (XT, xkeys) in enumerate(((KCT, KCK), (VCT, VCK))):
        for g in range(2):
            for c in range(2):
                p = nb()
                for pp in range(32):
                    mm(psf[p][:, 0:255], W1[xi][64 * g:64 * g + 64, pp, c * 128:(c + 1) * 128],
                       XT[64 * g:64 * g + 64, pp:pp + 4065:16], pp == 0, pp == 31, [("W1", xi)] + xkeys, [PK[p]])
                col = 2 * xi + c
                zv, z2v, sv = zt[:, 0:255], z2t[:, 0:255], sgt[:, 0:255]
                add(ACT, lambda e, p=p, col=col: e.activation(out=zv, in_=psf[p][:, 0:255], func=AF.Identity,
                                                              bias=hbias[:, col:col + 1], scale=1.0), [PK[p], "hbias"], ["zt"])
                add(POOL, lambda e: e.tensor_tensor(out=z2v, in0=zv, in1=zv, op=ALU.mult), ["zt"], ["z2t"])
                add(POOL, lambda e: e.tensor_scalar(out=z2v, in0=z2v, scalar1=0.044715, scalar2=1.0, op0=ALU.mult,
                                                    op1=ALU.add), ["z2t"], ["z2t"])
                add(POOL, lambda e: e.tensor_tensor(out=z2v, in0=z2v, in1=zv, op=ALU.mult), ["z2t", "zt"], ["z2t"])
                add(ACT, lambda e: e.activation(out=sv, in_=z2v, func=AF.Sigmoid, scale=1.5957691216057308), ["z2t"], ["sgt"])
                add(DVE, lambda e, xi=xi, g=g, c=c: e.tensor_tensor(out=gel[xi][g][:, c, 0:255], in0=sv, in1=zv, op=ALU.mult),
                    ["sgt", "zt"], [("gel", xi, g)])
    if L2 < 4:
        return finish()
    for ct in range(2):
        M = 128
        p = nb()
        for xi in range(2):
            for g in range(2):
                off = xi * 128 + g * 64
                for c in range(2):
                    mm(psf[p][0:M, off:off + 64], gel[xi][g][:, c, ct * 128:ct * 128 + M], W2[:, xi, c, :], c == 0, c == 1,
                       [("gel", xi, g), "W2"], [PK[p]])
        if L2 < 5:
            continue
        src = psf[p][:, 0:128].rearrange("p (g d) -> p g d", g=2)
        dst = kcr[:, :].rearrange("p (g d) -> p g d", g=2)
        cosc = CSC[:, ct, 0:8].unsqueeze(1).to_broadcast([128, 2, 8])
        sinc = CSC[:, ct, 8:16].unsqueeze(1).to_broadcast([128, 2, 8])
        t = [rt[:, q, 0:16].rearrange("p (g d) -> p g d", g=2) for q in range(4)]
        x1, x2 = src[:, :, 0:8], src[:, :, 8:16]
        add(DVE, lambda e: e.tensor_tensor(out=t[0], in0=x1, in1=cosc, op=ALU.mult), [PK[p], "cs"], ["rt1"])
        add(DVE, lambda e: e.tensor_tensor(out=t[1], in0=x2, in1=sinc, op=ALU.mult), [PK[p], "cs"], ["rt2"])
        add(DVE, lambda e: e.tensor_tensor(out=t[2], in0=x2, in1=cosc, op=ALU.mult), [PK[p], "cs"], ["rt3"])
        add(DVE, lambda e: e.tensor_tensor(out=t[3], in0=x1, in1=sinc, op=ALU.mult), [PK[p], "cs"], ["rt4"])
        add(DVE, lambda e: e.tensor_tensor(out=dst[:, :, 0:8], in0=t[0], in1=t[1], op=ALU.subtract), ["rt1", "rt2"], ["kcr"])
        add(DVE, lambda e: e.tensor_tensor(out=dst[:, :, 8:16], in0=t[2], in1=t[3], op=ALU.add), ["rt3", "rt4"], ["kcr"])
        if L2 < 6:
            continue
        add(ACT, lambda e: e.copy(out=dst[:, :, 16:64], in_=src[:, :, 16:64]), [PK[p]], ["kcr"])
        add(ACT, lambda e: e.copy(out=VCA[:, ct, :, 0:64], in_=psf[p][:, 128:256].rearrange("p (g d) -> p g d", g=2)),
            [PK[p]], ["VCA"])
        if L2 < 7:
            continue
        tb = nb()
        add(PE, lambda e: e.transpose(out=psb[tb][:, 0:128], in_=kcr[:, :], identity=identb[:]), ["kcr", "identb"], [PK[tb]])
        add(ACT, lambda e: e.copy(out=KCMPT[:, ct * 128:(ct + 1) * 128], in_=psb[tb][:, 0:128]), [PK[tb]], ["KCMPT"])

    if stop_after < 3:
        return finish()
    S.barrier()
    A.release(m_p1)
    AUG = [A([128, 8, 512], BF16), A([128, 8, 512], BF16)]
    Praw = [A([128, 512], BF16) for _ in range(3)]
    Pm = [A([128, 512], BF16) for _ in range(3)]
    oacc = A([128, 4, 8, 64], F32)
    impn = A([128, 4, 8, 64], F32)
    obf = A([128, 4, 512], BF16)
    dn = A([128, 4], F32)
    rd = A([128, 4], F32)
    cf = A([128, 4], F32)
    tmpo = A([128, 4, 64], F32)
    IG = A([128, 64], F32)
    IG2 = A([128, 64], F32)
    top16 = A([128, 16], F32)
    Mb = A([128, 4, 2, 64], BF16)
    add(DVE, lambda e: e.memset(Mb[:], 0.0), [], ["Mb"])

    SBK = [0, 1, 2]
    OBK = [3, 4, 5, 6]
    TBK = 7
    orr = [0]

    def qt_keys(I):
        return [("QT", 4 * I + q) for q in range(4)]

    L3 = int(os.environ.get("MK_L3", "9"))
    NB2 = int(os.environ.get("MK_NB2", str(NB)))
    for I in range(NB2):
        a = I % 2
        gk = [("G", 4 * I + q) for q in range(4)]
        items = []
        jobs_evac = {}

        def finish_job(br, h, banks):
            if os.environ.get("MK_NOFIN"):
                return
            Gv = G[:, 4 * I:4 * I + 4, br * 8 + h]
            if br == 0:
                for bi, bk in enumerate(banks):
                    v = psf[bk][:, 0:258].rearrange("p (q c) -> p q c", q=2)
                    add(DVE, lambda e, v=v, bi=bi: e.tensor_scalar(out=dn[:, 2 * bi:2 * bi + 2], in0=v[:, :, 64], scalar1=1e-30,
                                                                   scalar2=None, op0=ALU.max), [PK[bk]], ["dn"])
            else:
                v = psf[banks[0]][:, 0:260].rearrange("p (q c) -> p q c", q=4)
                add(DVE, lambda e, v=v: e.tensor_scalar(out=dn[:], in0=v[:, :, 64], scalar1=1e-30, scalar2=None, op0=ALU.max),
                    [PK[banks[0]]], ["dn"])
            add(DVE, lambda e: e.reciprocal(out=rd[:], in_=dn[:]), ["dn"], ["rd"])
            add(DVE, lambda e: e.tensor_tensor(out=cf[:], in0=rd[:], in1=Gv, op=ALU.mult), ["rd"] + gk, ["cf"])
            if br == 0:
                for bi, bk in enumerate(banks):
                    v = psf[bk][:, 0:258].rearrange("p (q c) -> p q c", q=2)
                    add(DVE, lambda e, v=v, bi=bi: e.tensor_tensor(
                        out=oacc[:, 2 * bi:2 * bi + 2, h, :], in0=v[:, :, 0:64],
                        in1=cf[:, 2 * bi:2 * bi + 2].unsqueeze(2).to_broadcast([128, 2, 64]), op=ALU.mult),
                        [PK[bk], "cf"], [("oacc", h)])
                    if I >= 2:
                        add(DVE, lambda e, v=v, bi=bi: e.tensor_tensor(
                            out=impn[:, 2 * bi:2 * bi + 2, h, :], in0=v[:, :, 65:129],
                            in1=rd[:, 2 * bi:2 * bi + 2].unsqueeze(2).to_broadcast([128, 2, 64]), op=ALU.mult),
                            [PK[bk], "rd"], [("impn", h)])
            else:
                v = psf[banks[0]][:, 0:260].rearrange("p (q c) -> p q c", q=4)
                add(DVE, lambda e, v=v: e.tensor_tensor(out=tmpo[:], in0=v[:, :, 0:64],
                                                        in1=cf[:].unsqueeze(2).to_broadcast([128, 4, 64]), op=ALU.mult),
                    [PK[banks[0]], "cf"], ["tmpo"])
                add(DVE, lambda e: e.tensor_tensor(out=oacc[:, :, h, :], in0=oacc[:, :, h, :], in1=tmpo[:], op=ALU.add),
                    ["tmpo", ("oacc", h)], [("oacc", h)])

        def cmp_items(h):
            g = h // 4
            hp = slice(64 * g, 64 * g + 64)
            nct = 1 if I <= 3 else 2
            b0 = OBK[orr[0] % 4]
            b1 = OBK[(orr[0] + 1) % 4]
            orr[0] += 2
            banks = [b0, b1]
            first = [True, True]
            for ct in range(nct):
                M = 128

                def Afn(slot, ct=ct, M=M):
                    sb_ = SBK[slot]
                    mm(psf[sb_][0:M, 0:512], KCMPT[hp, ct * 128:ct * 128 + M], QT[hp, h % 4, I * 512:(I + 1) * 512],
                       True, True, ["KCMPT"] + qt_keys(I), [PK[sb_]])
                    add(ACT, lambda e: e.activation(out=Praw[slot][0:M, :], in_=psf[sb_][0:M, 0:512], func=AF.Exp, scale=0.125),
                        [PK[sb_]], [("Praw", slot)])
                    add(POOL, lambda e: e.affine_select(out=Pm[slot][0:M, :], in_=Praw[slot][0:M, :], pattern=[[1, 512]],
                                                        compare_op=ALU.is_ge, fill=0.0, base=512 * I - 2048 * ct - 31,
                                                        channel_multiplier=-16), [("Praw", slot)], [("Pm", slot)])

                def Bfn(slot, ct=ct, M=M):
                    for ql in range(4):
                        bk = banks[ql // 2]
                        st_ = first[ql // 2]
                        first[ql // 2] = False
                        mm(psf[bk][:, (ql % 2) * 129:(ql % 2) * 129 + 129], Pm[slot][0:M, ql * 128:(ql + 1) * 128],
                           VCA[0:M, ct, g, 0:129], st_, ct == nct - 1, [("Pm", slot), "VCA"], [PK[bk]])
                    if ct == nct - 1:
                        finish_job(0, h, banks)

                items.append((Afn, Bfn))

        def sel_items(h):
            g = h // 4
            bk = OBK[orr[0] % 4]
            orr[0] += 1
            first = [True]
            nkt = 4 * I + 4
            for kt in range(nkt):
                m = kt - 4 * I
                c0 = max(0, m) * 128

                def Afn(slot, kt=kt, m=m, c0=c0):
                    sb_ = SBK[slot]
                    mm(psf[sb_][:, c0:512], KSA[g][:, kt * 128:(kt + 1) * 128], AUG[a][:, h, c0:512], True, m < 0,
                       [("KSA", g, kt), ("KSAE", g), ("AUG", a, h)], [PK[sb_]])
                    if m >= 0:
                        mm(psf[sb_][:, c0:c0 + 128], identb[:], CBt[:], False, True, ["identb", "CB"], [PK[sb_]])
                    add(ACT, lambda e: e.activation(out=Praw[slot][:, c0:512], in_=psf[sb_][:, c0:512], func=AF.Exp, scale=0.125),
                        [PK[sb_]], [("Praw", slot)])

                def Bfn(slot, kt=kt, m=m):
                    for ql in range(max(0, m), 4):
                        st_ = first[0]
                        first[0] = False
                        mm(psf[bk][:, ql * 65:ql * 65 + 65], Praw[slot][:, ql * 128:(ql + 1) * 128], V2[:, kt, 0, g, 0:65],
                           st_, kt == 4 * I + ql, [("Praw", slot), ("V2", kt)], [PK[bk]])
                    if kt == nkt - 1:
                        finish_job(1, h, [bk])

                items.append((Afn, Bfn))

        def win_items(h):
            g = h // 4
            hp = slice(64 * g, 64 * g + 64)
            bk = OBK[orr[0] % 4]
            orr[0] += 1
            first = [True]
            kts = list(range(max(0, 4 * I - 4), 4 * I + 4))
            for kt in kts:
                q0 = max(0, kt - 4 * I)
                q1 = min(3, kt + 4 - 4 * I)
                c0, c1 = q0 * 128, (q1 + 1) * 128

                def Afn(slot, kt=kt, c0=c0, c1=c1):
                    sb_ = SBK[slot]
                    diag = kt >= 4 * I
                    old = (kt + 4 <= 4 * I + 3)
                    mm(psf[sb_][:, c0:c1], KWT[hp, kt * 128:(kt + 1) * 128], QT[hp, h % 4, I * 512 + c0:I * 512 + c1],
                       True, not (diag or old), [("KWT", kt)] + qt_keys(I), [PK[sb_]])
                    if diag:
                        qd = kt - 4 * I
                        mm(psf[sb_][:, qd * 128:(qd + 1) * 128], identb[:], CBt[:], False, not old, ["identb", "CB"], [PK[sb_]])
                    if old:
                        qo_ = kt + 4 - 4 * I
                        mm(psf[sb_][:, qo_ * 128:(qo_ + 1) * 128], identb[:], WBt[:], False, True, ["identb", "WB"], [PK[sb_]])
                    add(ACT, lambda e: e.activation(out=Praw[slot][:, c0:c1], in_=psf[sb_][:, c0:c1], func=AF.Exp, scale=0.125),
                        [PK[sb_]], [("Praw", slot)])

                def Bfn(slot, kt=kt, q0=q0, q1=q1):
                    for ql in range(q0, q1 + 1):
                        st_ = first[0]
                        first[0] = False
                        last_kt = 4 * I + ql
                        mm(psf[bk][:, ql * 65:ql * 65 + 65], Praw[slot][:, ql * 128:(ql + 1) * 128], V2[:, kt, 1, g, 0:65],
                           st_, kt == last_kt, [("Praw", slot), ("V2", kt)], [PK[bk]])
                    if kt == kts[-1]:
                        finish_job(2, h, [bk])

                items.append((Afn, Bfn))

        def run_items():
            n = len(items)
            for i in range(n + 2):
                if i < n:
                    items[i][0](i % 3)
                if i >= 2:
                    items[i - 2][1]((i - 2) % 3)
            del items[:]

        add(POOL, lambda e: e.tensor_copy(out=AUG[a][0:64, 0:4, :], in_=QT[0:64, :, I * 512:(I + 1) * 512]),
            qt_keys(I), [("AUG", a, h) for h in range(4)])
        add(POOL, lambda e: e.tensor_copy(out=AUG[a][64:128, 4:8, :], in_=QT[64:128, :, I * 512:(I + 1) * 512]),
            qt_keys(I), [("AUG", a, h) for h in range(4, 8)])
        if I < 2:
            add(POOL, lambda e: e.memset(AUG[a][64:128, 0:4, :], 0.0), [], [("AUG", a, h) for h in range(4)])
            add(POOL, lambda e: e.memset(AUG[a][0:64, 4:8, :], 0.0), [], [("AUG", a, h) for h in range(4, 8)])

        if L3 < 1:
            continue
        for h in range(int(os.environ.get("MK_NH", "8"))):
            cmp_items(h)
        run_items()
        if L3 < 2:
            continue
        if I >= 2:
            for ql in range(4):
                i = 4 * I + ql
                hi = 2 * i
                for g in range(2):
                    add(DVE, lambda e, ql=ql, g=g: e.tensor_tensor(out=IG[:], in0=impn[:, ql, 4 * g, :], in1=impn[:, ql, 4 * g + 1, :],
                                                                   op=ALU.add), [("impn", 4 * g), ("impn", 4 * g + 1)], ["IG"])
                    add(DVE, lambda e, ql=ql, g=g: e.tensor_tensor(out=IG[:], in0=IG[:], in1=impn[:, ql, 4 * g + 2, :], op=ALU.add),
                        ["IG", ("impn", 4 * g + 2)], ["IG"])
                    add(DVE, lambda e, ql=ql, g=g: e.tensor_tensor(out=IG[:], in0=IG[:], in1=impn[:, ql, 4 * g + 3, :], op=ALU.add),
                        ["IG", ("impn", 4 * g + 3)], ["IG"])
                    add(DVE, lambda e, hi=hi: e.tensor_scalar(out=IG[:, hi - 1:hi], in0=IG[:, hi - 1:hi], scalar1=notHalf[:, 0:1],
                                                              scalar2=negHalf[:, 0:1], op0=ALU.mult, op1=ALU.add), ["IG", "half"], ["IG"])
                    add(DVE, lambda e, hi=hi: e.max(out=top16[:, 0:8], in_=IG[:, 1:hi]), ["IG"], ["top16"])
                    add(DVE, lambda e, hi=hi: e.match_replace(out=IG2[:, 1:hi], in_to_replace=top16[:, 0:8], in_values=IG[:, 1:hi],
                                                              imm_value=-1e30), ["IG", "top16"], ["IG2"])
                    add(DVE, lambda e, hi=hi: e.max(out=top16[:, 8:16], in_=IG2[:, 1:hi]), ["IG2"], ["top16"])
                    add(DVE, lambda e, hi=hi, ql=ql, g=g: e.tensor_scalar(out=Mb[:, ql, g, 1:hi], in0=IG[:, 1:hi],
                                                                          scalar1=top16[:, 12:13], scalar2=NEGB, op0=ALU.is_lt,
                                                                          op1=ALU.mult), ["IG", "top16"], ["Mb"])
                    add(DVE, lambda e, hi=hi, ql=ql, g=g: e.tensor_scalar(out=Mb[:, ql, g, hi - 1:hi], in0=Mb[:, ql, g, hi - 1:hi],
                                                                          scalar1=notHalf[:, 0:1], scalar2=None, op0=ALU.mult),
                        ["Mb", "half"], ["Mb"])
        for h in range(8):
            win_items(h)
        run_items()
        if L3 < 3:
            continue
        if I >= 2:
            for ql in range(4):
                for g in range(2):
                    prt = slice(64, 128) if g == 0 else slice(0, 64)
                    add(PE, lambda e: e.transpose(out=psb[TBK][prt, ql * 128:(ql + 1) * 128], in_=Mb[:, ql, g, :],
                                                  identity=identb[:]), ["Mb", "identb"], ["T7"])
            add(ACT, lambda e: e.copy(out=AUG[a][64:128, 0:4, :],
                                      in_=psb[TBK][64:128, 0:512].unsqueeze(1).to_broadcast([64, 4, 512])),
                ["T7"], [("AUG", a, h) for h in range(4)])
            add(ACT, lambda e: e.copy(out=AUG[a][0:64, 4:8, :],
                                      in_=psb[TBK][0:64, 0:512].unsqueeze(1).to_broadcast([64, 4, 512])),
                ["T7"], [("AUG", a, h) for h in range(4, 8)])
        for h in range(8):
            sel_items(h)
        run_items()
        if L3 < 4:
            continue
        add(POOL, lambda e: e.tensor_copy(out=obf[:], in_=oacc[:].rearrange("p q h d -> p q (h d)")),
            [("oacc", h) for h in range(8)], ["obf"])
        L4 = int(os.environ.get("MK_L4", "9"))
        for ql in range(4):
            if L4 < 2:
                break
            half = ql % 2
            for kc in range(4):
                add(PE, lambda e, ql=ql, kc=kc, half=half: e.transpose(
                    out=psb[TBK][:, half * 512 + kc * 128:half * 512 + (kc + 1) * 128], in_=obf[:, ql, kc * 128:(kc + 1) * 128],
                    identity=identb[:]), ["obf", "identb"], ["T7"])
            if L4 < 3:
                continue
            add(DVE, lambda e, ql=ql, half=half: e.tensor_copy(
                out=OT[:, :, I * 512 + ql * 128:I * 512 + (ql + 1) * 128],
                in_=psb[TBK][:, half * 512:(half + 1) * 512].rearrange("p (k t) -> p k t", k=4)), ["T7"], [("OT", I)])

    if stop_after < 4:
        return finish()
    S.barrier()
    A.release(m_l2)
    W3 = A([128, 8, 2560], BF16)
    WBP = A([128, 4, D], BF16)
    WBA = A([128, 4, D], BF16)
    WOUT = A([128, 8, D], BF16)
    PW = A([128, 4, 128], BF16)
    pscl = A([128, 4], F32)
    XB = A([128, 4, D], F32)
    hb3 = A([128, D], BF16)
    hT3 = A([128, 8, 512], BF16)
    UT = A([128, 4, 528], F32)
    pt = [A([128, 528], F32), A([128, 528], F32)]
    pooled = A([128, 4, 512], BF16)
    mixed = A([128, 4, 512], BF16)
    sga = [A([128, 512], F32), A([128, 512], F32)]
    sgb = [A([128, 512], F32), A([128, 512], F32)]
    merged = A([128, 8, 512], BF16)
    st3 = A([128, 8], F32)
    XBK = [("XB", t) for t in range(4)]
    add(POOL, lambda e: e.memset(UT[:], 0.0), [], [("UT", gi) for gi in range(4)])

    wbp_v = wbp_d.rearrange("(k p) n -> p k n", p=128)
    wba_v = wba_d.rearrange("(k p) n -> p k n", p=128)
    wout_v = wout_d.rearrange("(k p) n -> p k n", p=128)
    add(PQ, lambda e: e.dma_start(out=W3[:, :, 0:512], in_=w_in_v[:, :, 1304:1816]), [], [("W3c", "u")])
    add(PQ, lambda e: e.dma_start(out=PW[:, :, :], in_=poolw_d.rearrange("(g p) n -> p g n", p=128)), [], ["PW"])

    def p3_rest_of_weights():
        for dc2 in range(4):
            add(PQ, lambda e: e.dma_start(out=W3[:, :, 512 + dc2 * 256:512 + (dc2 + 1) * 256],
                                          in_=w_in_v[:, :, 1816 + dc2 * 256:1816 + (dc2 + 1) * 256]), [], [("W3c", "a", dc2)])
            add(PQ, lambda e: e.dma_start(out=W3[:, :, 1536 + dc2 * 256:1536 + (dc2 + 1) * 256],
                                          in_=w_in_v[:, :, 2840 + dc2 * 256:2840 + (dc2 + 1) * 256]), [], [("W3c", "b", dc2)])
            if dc2 == 0:
                for k in range(4):
                    add(PQ, lambda e: e.dma_start(out=WBA[:, k, :], in_=wba_v[:, k, :]), [], ["WBA"])
                    add(PQ, lambda e: e.dma_start(out=WBP[:, k, :], in_=wbp_v[:, k, :]), [], ["WBP"])
        for k in range(8):
            add(PQ, lambda e: e.dma_start(out=WOUT[:, k, :], in_=wout_v[:, k, :]), [], ["WOUT"])

    for g_ in range(4):
        add(SP, lambda e: e.dma_start(out=pscl[:, g_:g_ + 1], in_=pscale_d[g_ * 128:(g_ + 1) * 128, 0:1]), [], ["pscl"])
    W3K = [("W3", k) for k in range(8)]

    XB2 = [XB, A([128, 4, D], F32)]
    pt2 = [A([128, 528], F32), A([128, 528], F32)]
    ptg = [pt, pt, pt2, pt2]
    hk3 = [("hT3", t) for t in range(4)]
    mk3 = [("mixed", gi) for gi in range(4)]

    def p3_load(I):
        xb = I % 2
        for tl in range(4):
            tt = 4 * I + tl
            add(SP, lambda e: e.dma_start(out=XB2[xb][:, tl, :], in_=x_d[tt * 128:(tt + 1) * 128, :]), [], [("XB", xb, tl)])

    def p3_norm_tile(I, tl):
        xb = I % 2
        rms_stats(XB2[xb][:, tl, :], ("XB", xb, tl), st3[:, 2 * tl:2 * tl + 1], st3[:, 2 * tl + 1:2 * tl + 2], ("st3", tl))
        add(DVE, lambda e: e.scalar_tensor_tensor(out=hb3[:], in0=XB2[xb][:, tl, :], scalar=st3[:, 2 * tl + 1:2 * tl + 2],
                                                  in1=gmix[:], op0=ALU.mult, op1=ALU.mult),
            [("XB", xb, tl), ("st3", tl), "gmix"], ["hb3"])
        tb = nb()
        for k in range(8):
            add(PE, lambda e: e.transpose(out=psb[tb][:, k * 128:(k + 1) * 128], in_=hb3[:, k * 128:(k + 1) * 128],
                                          identity=identb[:]), ["hb3", "identb"], [PK[tb]])
        add(ACT, lambda e: e.copy(out=hT3[:, :, tl * 128:(tl + 1) * 128],
                                  in_=psb[tb][:, :].rearrange("p (k t) -> p k t", k=8)), [PK[tb]], [("hT3", tl)])

    def p3_pool_u(I):
        for gi in range(4):
            w = 2 ** (gi + 1)
            add(POOL, lambda e: e.tensor_copy(out=UT[:, gi, 0:16], in_=UT[:, gi, 512:528]), [("UT", gi)], [("UT", gi)])
            p = nb()
            for k in range(8):
                mm(psf[p][:, 0:512], W3[:, k, gi * 128:(gi + 1) * 128], hT3[:, k, :], k == 0, k == 7, hk3 + [("W3c", "u")], [PK[p]])
            add(ACT, lambda e: e.copy(out=UT[:, gi, 16:528], in_=psf[p][:, 0:512]), [PK[p]], [("UT", gi)])
            src = UT[:, gi, :]
            srck = ("UT", gi)
            lo = 0
            step = 1
            j = 0
            while step < w:
                lo += step
                dstt = ptg[gi][j % 2]
                add(POOL if gi < 2 else DVE, lambda e: e.tensor_tensor(out=dstt[:, lo:528], in0=src[:, lo:528],
                                                                      in1=src[:, lo - step:528 - step], op=ALU.add),
                    [srck], [("pt", gi // 2, j % 2)])
                src = dstt[:, :]
                srck = ("pt", gi // 2, j % 2)
                step *= 2
                j += 1
            add(DVE, lambda e: e.scalar_tensor_tensor(out=pooled[:, gi, :], in0=src[:, 16:528], scalar=1.0 / w,
                                                      in1=UT[:, gi, 16:528], op0=ALU.mult, op1=ALU.subtract),
                [srck, ("UT", gi)], [("pooled", gi)])
            if I == 0:
                for t in range(w - 1):
                    add(DVE, lambda e: e.scalar_tensor_tensor(
                        out=pooled[:, gi, t:t + 1], in0=src[:, 16 + t:17 + t], scalar=1.0 / (t + 1), in1=UT[:, gi, 16 + t:17 + t],
                        op0=ALU.mult, op1=ALU.subtract), [srck, ("UT", gi)], [("pooled", gi)])

    def p3_pool_mix(I):
        for gi in range(4):
            p2 = nb()
            mm(psf[p2][:, 0:512], PW[:, gi, :], pooled[:, gi, :], True, True, ["PW", ("pooled", gi)], [PK[p2]])
            add(DVE, lambda e: e.tensor_scalar(out=mixed[:, gi, :], in0=psf[p2][:, 0:512], scalar1=pscl[:, gi:gi + 1],
                                               scalar2=None, op0=ALU.mult), [PK[p2], "pscl"], [("mixed", gi)])

    def p3_merge(I):
        for dc in range(8):
            j = dc % 2
            pa, pb_, pga, pgb = nb(), nb(), nb(), nb()
            dsl = slice(dc * 128, (dc + 1) * 128)
            for k in range(8):
                mm(psf[pga][:, 0:512], W3[:, k, 512 + dc * 128:512 + (dc + 1) * 128], hT3[:, k, :], k == 0, k == 7,
                   hk3 + [("W3c", "a", dc // 2)], [PK[pga]])
            for k in range(8):
                mm(psf[pgb][:, 0:512], W3[:, k, 1536 + dc * 128:1536 + (dc + 1) * 128], hT3[:, k, :], k == 0, k == 7,
                   hk3 + [("W3c", "b", dc // 2)], [PK[pgb]])
            for k in range(4):
                mm(psf[pa][:, 0:512], WBA[:, k, dsl], OT[:, k, I * 512:(I + 1) * 512], k == 0, k == 3, ["WBA", ("OT", I)], [PK[pa]])
            for k in range(4):
                mm(psf[pb_][:, 0:512], WBP[:, k, dsl], mixed[:, k, :], k == 0, k == 3, ["WBP"] + mk3, [PK[pb_]])
            add(ACT, lambda e: e.activation(out=sga[j][:], in_=psf[pga][:, 0:512], func=AF.Sigmoid), [PK[pga]], [("sga", j)])
            add(ACT, lambda e: e.activation(out=sgb[j][:], in_=psf[pgb][:, 0:512], func=AF.Sigmoid), [PK[pgb]], [("sgb", j)])
            add(DVE, lambda e: e.tensor_tensor(out=sga[j][:], in0=sga[j][:], in1=psf[pa][:, 0:512], op=ALU.mult),
                [("sga", j), PK[pa]], [("sga", j)])
            add(DVE, lambda e: e.tensor_tensor(out=sgb[j][:], in0=sgb[j][:], in1=psf[pb_][:, 0:512], op=ALU.mult),
                [("sgb", j), PK[pb_]], [("sgb", j)])
            add(POOL, lambda e: e.tensor_tensor(out=merged[:, dc, :], in0=sga[j][:], in1=sgb[j][:], op=ALU.add),
                [("sga", j), ("sgb", j)], [("merged", dc)])

    def p3_out_tile(I, tl):
        xb = I % 2
        tt = 4 * I + tl
        mgk = [("merged", dc) for dc in range(8)]
        for half in range(2):
            p = nb()
            for k in range(8):
                mm(psf[p][:, 0:512], merged[:, k, tl * 128:(tl + 1) * 128], WOUT[:, k, half * 512:(half + 1) * 512],
                   k == 0, k == 7, mgk + ["WOUT"], [PK[p]])
            add(DVE, lambda e: e.tensor_tensor(
                out=XB2[xb][:, tl, half * 512:(half + 1) * 512], in0=XB2[xb][:, tl, half * 512:(half + 1) * 512],
                in1=psf[p][:, 0:512], op=ALU.add), [("XB", xb, tl), PK[p]], [("XB", xb, tl)])
        add(SP, lambda e: e.dma_start(out=x1_d[tt * 128:(tt + 1) * 128, :], in_=XB2[xb][:, tl, :]),
            [("XB", xb, tl)], [("x1d", tt)])

    p3_load(0)
    for tl in range(4):
        p3_norm_tile(0, tl)
    p3_pool_u(0)
    p3_pool_mix(0)
    p3_rest_of_weights()
    for I in range(NB):
        p3_merge(I)
        if I + 1 < NB:
            p3_load(I + 1)
        for tl in range(4):
            if I + 1 < NB:
                p3_norm_tile(I + 1, tl)
                if tl == 3:
                    p3_pool_u(I + 1)
            p3_out_tile(I, tl)
        if I + 1 < NB:
            p3_pool_mix(I + 1)

    if stop_after < 5:
        return finish()
    S.barrier()
    A.release(m_l1)
    WF1 = A([128, 8, DFF], BF16)
    WF2 = A([128, 32, D], BF16)
    X1 = [A([128, 2, D], F32), A([128, 2, D], F32)]
    hb4 = A([128, D], BF16)
    h2T = A([128, 8, 256], BF16)
    fT = A([128, 32, 256], BF16)
    st4 = A([128, 8], F32)
    rl = [A([128, 256], F32), A([128, 256], F32)]
    wff1_v = wff1_d.rearrange("(k p) n -> p k n", p=128)
    wff2_v = wff2_d.rearrange("(k p) n -> p k n", p=128)
    for fg in range(8):
        add(PQ, lambda e: e.dma_start(out=WF1[:, :, fg * 512:(fg + 1) * 512], in_=wff1_v[:, :, fg * 512:(fg + 1) * 512]),
            [], [("WF1c", fg)])
    for k2 in range(16):
        add(PQ, lambda e: e.dma_start(out=WF2[:, 2 * k2:2 * k2 + 2, :], in_=wff2_v[:, 2 * k2:2 * k2 + 2, :]), [], [("WF2", k2)])
    WF1K = [("WF1", k) for k in range(8)]

    NB4 = S_LEN // 256
    h2Tb = [h2T, A([128, 8, 256], BF16)]
    st4b = [st4, A([128, 8], F32)]

    hb4d = [hb4, A([128, D], BF16)]

    def p4_stageA_norm(J):
        b = J % 2
        for tl in range(2):
            tt = 2 * J + tl
            add(SP, lambda e: e.dma_start(out=X1[b][:, tl, :], in_=x1_d[tt * 128:(tt + 1) * 128, :]),
                [("x1d", tt)], [("X1", b, tl)])
        for tl in range(2):
            rms_stats(X1[b][:, tl, :], ("X1", b, tl), st4b[b][:, 2 * tl:2 * tl + 1], st4b[b][:, 2 * tl + 1:2 * tl + 2], ("st4", b, tl))
            add(DVE, lambda e: e.scalar_tensor_tensor(out=hb4d[tl][:], in0=X1[b][:, tl, :], scalar=st4b[b][:, 2 * tl + 1:2 * tl + 2],
                                                      in1=gmlp[:], op0=ALU.mult, op1=ALU.mult),
                [("X1", b, tl), ("st4", b, tl), "gmlp"], [("hb4", tl)])

    def p4_stageA_tr(J):
        b = J % 2
        for tl in range(2):
            tb = nb()
            for k in range(8):
                add(PE, lambda e: e.transpose(out=psb[tb][:, k * 128:(k + 1) * 128], in_=hb4d[tl][:, k * 128:(k + 1) * 128],
                                              identity=identb[:]), [("hb4", tl), "identb"], [PK[tb]])
            add(ACT, lambda e: e.copy(out=h2Tb[b][:, :, tl * 128:(tl + 1) * 128],
                                      in_=psb[tb][:, :].rearrange("p (k t) -> p k t", k=8)), [PK[tb]], [("h2T", b, tl)])

    def p4_stageB(J):
        b = J % 2
        hk = [("h2T", b, 0), ("h2T", b, 1)]
        for fc in range(32):
            p = nb()
            for k in range(8):
                mm(psf[p][:, 0:256], WF1[:, k, fc * 128:(fc + 1) * 128], h2Tb[b][:, k, :], k == 0, k == 7, hk + [("WF1c", fc // 4)], [PK[p]])
            rj = fc % 2
            add(ACT, lambda e: e.activation(out=rl[rj][:], in_=psf[p][:, 0:256], func=AF.Relu), [PK[p]], [("rl", rj)])
            add(DVE if rj == 0 else POOL, lambda e: e.tensor_tensor(out=fT[:, fc, :], in0=rl[rj][:], in1=rl[rj][:], op=ALU.mult),
                [("rl", rj)], [("fT", fc)])
            if fc == 10 and J + 1 < NB4:
                p4_stageA_norm(J + 1)

    def p4_stageC(J):
        b = J % 2
        ftk = [("fT", fc) for fc in range(32)]
        for tl in range(2):
            tt = 2 * J + tl
            for half in range(2):
                p = nb()
                for fc in range(32):
                    mm(psf[p][:, 0:512], fT[:, fc, tl * 128:(tl + 1) * 128], WF2[:, fc, half * 512:(half + 1) * 512],
                       fc == 0, fc == 31, ftk + [("WF2", fc // 2)], [PK[p]])
                add(DVE, lambda e: e.tensor_tensor(
                    out=X1[b][:, tl, half * 512:(half + 1) * 512], in0=X1[b][:, tl, half * 512:(half + 1) * 512],
                    in1=psf[p][:, 0:512], op=ALU.add), [("X1", b, tl), PK[p]], [("X1", b, tl)])
            rms_stats(X1[b][:, tl, :], ("X1", b, tl), st4b[b][:, 4 + 2 * tl:5 + 2 * tl], st4b[b][:, 5 + 2 * tl:6 + 2 * tl], ("st4f", b, tl))
            add(DVE, lambda e: e.scalar_tensor_tensor(out=X1[b][:, tl, :], in0=X1[b][:, tl, :],
                                                      scalar=st4b[b][:, 5 + 2 * tl:6 + 2 * tl], in1=gfin[:], op0=ALU.mult,
                                                      op1=ALU.mult), [("X1", b, tl), ("st4f", b, tl), "gfin"], [("X1", b, tl)])
            add(SP, lambda e: e.dma_start(out=out_d[tt * 128:(tt + 1) * 128, :], in_=X1[b][:, tl, :]),
                [("X1", b, tl)], [("out", tt)])

    p4_stageA_norm(0)
    p4_stageA_tr(0)
    for J in range(NB4):
        p4_stageB(J)
        if J + 1 < NB4:
            p4_stageA_tr(J + 1)
        p4_stageC(J)

    return finish()


_INPUT_ORDER = ["x", "norm_mix", "w_in", "cmp_pe_k", "cmp_pe_v", "cmp_k_w1", "cmp_k_w2", "cmp_v_w1", "cmp_v_w2",
                "w_branch_attn", "pool_w", "pool_scale", "w_branch_pool", "w_out", "norm_mlp", "w_ff1", "w_ff2", "norm_final"]


def kernel(**inputs):
    f = lambda a: np.ascontiguousarray(np.asarray(a, dtype=np.float32))
    x = f(inputs["x"])
    shared = {
        "norm_mix": f(inputs["norm_mix"]).reshape(1, D),
        "w_in": f(inputs["w_in"]).reshape(D, 3864),
        "cmp_pe_k": f(inputs["cmp_pe_k"]).reshape(32, 64),
        "cmp_pe_v": f(inputs["cmp_pe_v"]).reshape(32, 64),
        "cmp_k_w1": f(inputs["cmp_k_w1"]).reshape(2048, 256),
        "cmp_k_w2": f(inputs["cmp_k_w2"]).reshape(256, 64),
        "cmp_v_w1": f(inputs["cmp_v_w1"]).reshape(2048, 256),
        "cmp_v_w2": f(inputs["cmp_v_w2"]).reshape(256, 64),
        "w_branch_attn": f(inputs["w_branch_attn"]).reshape(512, D),
        "pool_w": f(inputs["pool_w"]).reshape(512, 128),
        "pool_scale": f(inputs["pool_scale"]).reshape(512, 1),
        "w_branch_pool": f(inputs["w_branch_pool"]).reshape(512, D),
        "w_out": f(inputs["w_out"]).reshape(D, D),
        "norm_mlp": f(inputs["norm_mlp"]).reshape(1, D),
        "w_ff1": f(inputs["w_ff1"]).reshape(D, DFF),
        "w_ff2": f(inputs["w_ff2"]).reshape(DFF, D),
        "norm_final": f(inputs["norm_final"]).reshape(1, D),
    }
    nc = build_nc()
    in_maps = []
    for c in range(8):
        m = dict(shared)
        m["x"] = np.ascontiguousarray(x[c])
        in_maps.append(m)
    res = run_bass_kernel_spmd(nc, in_maps, core_ids=list(range(8)))
    return np.stack([np.asarray(r["out"], dtype=np.float32).reshape(S_LEN, D) for r in res.results], axis=0)
```

```python
import os
import numpy as np
import concourse.bass as bass
import concourse.mybir as mybir
from concourse.bass_utils import run_bass_kernel_spmd

F32 = mybir.dt.float32
BF16 = mybir.dt.bfloat16
I32 = mybir.dt.int32
ALU = mybir.AluOpType
AF = mybir.ActivationFunctionType

PE, ACT, DVE, POOL, SP, PQ = "pe", "act", "dve", "pool", "sp", "pq"
COMPUTE = (PE, ACT, DVE, POOL)
N_DMA_SEMS = 16
N_PQ_SEMS = 8
N_ALL_DSEMS = N_DMA_SEMS + N_PQ_SEMS

S_LEN = 4096
D = 1024
NT = 32
NB = 8
DFF = 4096
NEGB = -3000.0
INV_FREQ = [float(np.float32(500000.0) ** (-np.float32(2 * i) / np.float32(16))) for i in range(8)]
TWO_PI = float(2 * np.pi)


class Op:
    __slots__ = ("eng", "emit", "deps", "idx", "dsem", "dval", "dprev")

    def __init__(self, eng, emit):
        self.eng = eng
        self.emit = emit
        self.deps = set()
        self.idx = 0
        self.dsem = None
        self.dval = 0
        self.dprev = 0


class Sched:
    def __init__(self):
        self.streams = {e: [] for e in COMPUTE + (SP,)}
        self.nreal = {e: 0 for e in COMPUTE + (SP,)}
        self.lastreal = {e: None for e in COMPUTE}
        self.last_w = {}
        self.readers = {}
        self.dma_rr = 0
        self.pq_rr = 0
        self.dma_cnt = [0] * N_ALL_DSEMS
        self.dma_last = [None] * N_ALL_DSEMS

    def add(self, eng, emit, reads=(), writes=()):
        op = Op(eng, emit)
        for k in reads:
            w = self.last_w.get(k)
            if w is not None:
                op.deps.add(w)
        for k in writes:
            w = self.last_w.get(k)
            if w is not None:
                op.deps.add(w)
            for r in self.readers.get(k, ()):
                op.deps.add(r)
        for k in writes:
            self.last_w[k] = op
            self.readers[k] = []
        ws = set(writes)
        for k in reads:
            if k not in ws:
                self.readers.setdefault(k, []).append(op)
        op.deps.discard(op)
        self.streams[POOL if eng == PQ else eng].append(op)
        if eng == SP or eng == PQ:
            if eng == SP:
                s = self.dma_rr
                self.dma_rr = (self.dma_rr + 1) % N_DMA_SEMS
            else:
                s = N_DMA_SEMS + self.pq_rr
                self.pq_rr = (self.pq_rr + 1) % N_PQ_SEMS
            op.dsem = s
            op.dprev = self.dma_cnt[s] * 16
            self.dma_cnt[s] += 1
            op.dval = self.dma_cnt[s] * 16
            self.dma_last[s] = op
        else:
            self.nreal[eng] += 1
            op.idx = self.nreal[eng]
            self.lastreal[eng] = op
        return op

    def barrier(self):
        deps = set()
        for e in COMPUTE:
            if self.lastreal[e] is not None:
                deps.add(self.lastreal[e])
        for s in range(N_ALL_DSEMS):
            if self.dma_last[s] is not None:
                deps.add(self.dma_last[s])
        for e in COMPUTE + (SP,):
            op = Op(e, None)
            op.deps = set(deps)
            self.streams[e].append(op)
        self.last_w = {}
        self.readers = {}

    def emit_all(self, block, sems, dsems):
        engmap = {PE: "tensor", ACT: "scalar", DVE: "vector", POOL: "gpsimd", SP: "sync"}
        for ename in (SP, PE, ACT, DVE, POOL):
            ops = self.streams[ename]

            def body(eng, ops=ops, ename=ename):
                waited = {}
                for op in ops:
                    need = {}
                    for d in op.deps:
                        if d.eng == SP or d.eng == PQ:
                            key = ("d", d.dsem)
                            val = d.dval
                        else:
                            if d.eng == ename and ename == PE:
                                continue
                            key = ("c", d.eng)
                            val = d.idx
                        if need.get(key, 0) < val:
                            need[key] = val
                    if (op.eng == SP or op.eng == PQ) and op.emit is not None and op.dprev > 0:
                        key = ("d", op.dsem)
                        if need.get(key, 0) < op.dprev:
                            need[key] = op.dprev
                    for key, val in need.items():
                        if waited.get(key, 0) >= val:
                            continue
                        waited[key] = val
                        sem = dsems[key[1]] if key[0] == "d" else sems[key[1]]
                        eng.wait_ge(sem, val)
                    if op.emit is None:
                        continue
                    ins = op.emit(eng)
                    if op.eng == SP or op.eng == PQ:
                        ins.then_inc(dsems[op.dsem], 16)
                    else:
                        ins.then_inc(sems[op.eng], 1)
                if ename == SP:
                    for s in range(N_DMA_SEMS):
                        if self.dma_cnt[s] > 0 and waited.get(("d", s), 0) < self.dma_cnt[s] * 16:
                            eng.wait_ge(dsems[s], self.dma_cnt[s] * 16)

            getattr(block, engmap[ename])(body)


class Alloc:
    def __init__(self, nc, base=16512, top=229344):
        self.nc = nc
        self.cur = base
        self.top = top
        self.n = 0

    def mark(self):
        return self.cur

    def release(self, m):
        self.cur = m

    def __call__(self, shape, dt):
        nbytes = int(np.prod(shape[1:])) * (2 if dt == BF16 else 4)
        nbytes = (nbytes + 31) // 32 * 32
        off = self.cur
        assert off + nbytes <= self.top, ("SBUF overflow", off, nbytes, self.top)
        self.cur += nbytes
        self.n += 1
        return self.nc.alloc_sbuf_tensor_at("t%d" % self.n, list(shape), dt, offset=off)


def build_nc(stop_after=99):
    nc = bass.Bass("TRN2", target_bir_lowering=False)

    def din(name, shape):
        return nc.dram_tensor(name, list(shape), F32, kind="ExternalInput").ap()

    x_d = din("x", [S_LEN, D])
    norm_mix_d = din("norm_mix", [1, D])
    w_in_d = din("w_in", [D, 3864])
    pe_k_d = din("cmp_pe_k", [32, 64])
    pe_v_d = din("cmp_pe_v", [32, 64])
    w1k_d = din("cmp_k_w1", [2048, 256])
    w2k_d = din("cmp_k_w2", [256, 64])
    w1v_d = din("cmp_v_w1", [2048, 256])
    w2v_d = din("cmp_v_w2", [256, 64])
    wba_d = din("w_branch_attn", [512, D])
    poolw_d = din("pool_w", [512, 128])
    pscale_d = din("pool_scale", [512, 1])
    wbp_d = din("w_branch_pool", [512, D])
    wout_d = din("w_out", [D, D])
    norm_mlp_d = din("norm_mlp", [1, D])
    wff1_d = din("w_ff1", [D, DFF])
    wff2_d = din("w_ff2", [DFF, D])
    norm_fin_d = din("norm_final", [1, D])
    out_d = nc.dram_tensor("out", [S_LEN, D], F32, kind="ExternalOutput").ap()
    x1_d = nc.dram_tensor("x1_scratch", [S_LEN, D], F32).ap()

    S = Sched()
    A = Alloc(nc)

    class _Rec:
        def __getattr__(self, name):
            def f(*a, **kw):
                return (name, a, kw)
            return f

    _REC = _Rec()

    def add(eng, fn, reads=(), writes=()):
        call = fn(_REC)
        op = S.add(eng, lambda e: getattr(e, call[0])(*call[1], **call[2]), reads, writes)
        if os.environ.get("MK_DUMP"):
            S.dbg = getattr(S, "dbg", [])
            S.dbg.append((eng, op.idx, call[0], [(d.eng, d.idx, d.dval) for d in op.deps], list(reads), list(writes)))
        return op


    def finish():
        from contextlib import ExitStack
        with ExitStack() as es:
            sems = {e: es.enter_context(nc.semaphore("sem_" + e)) for e in COMPUTE}
            dsems = [es.enter_context(nc.semaphore("dsem%d" % i)) for i in range(N_ALL_DSEMS)]
            block = es.enter_context(nc.Block())
            S.emit_all(block, sems, dsems)
        return nc

    psf = [nc.alloc_psum_tensor("psb%d" % i, [128, 512], F32) for i in range(8)]
    psb = [p[:, :].bitcast(BF16) for p in psf]
    PK = [("ps", i) for i in range(8)]
    rr = [0]

    def nb():
        i = rr[0]
        rr[0] = (i + 1) % 8
        return i

    def mm(out, lhsT, rhs, start, stop, r, w):
        add(PE, lambda e: e.matmul(out, lhsT=lhsT, rhs=rhs, start=start, stop=stop, skip_group_check=True), r, w)

    identb = A([128, 128], BF16)
    onesf = A([128, 128], F32)
    zerosb = A([128, 128], BF16)
    CBt = A([128, 128], BF16)
    WBt = A([128, 128], BF16)
    gmix = A([128, D], F32)
    gmlp = A([128, D], F32)
    gfin = A([128, D], F32)
    epst = A([128, 1], F32)
    notHalf = A([128, 1], F32)
    negHalf = A([128, 1], F32)
    junk = A([128, D], BF16)

    add(POOL, lambda e: e.memset(onesf[:], 1.0), [], ["onesf"])
    add(POOL, lambda e: e.affine_select(out=notHalf[:], in_=onesf[:, 0:1], pattern=[[0, 1]], compare_op=ALU.is_ge,
                                        fill=0.0, base=-64, channel_multiplier=1), ["onesf"], ["half"])
    add(POOL, lambda e: e.tensor_scalar(out=negHalf[:], in0=notHalf[:], scalar1=-1.0, scalar2=None, op0=ALU.add), ["half"], ["half"])
    add(POOL, lambda e: e.memset(zerosb[:], 0.0), [], ["zerosb"])
    add(POOL, lambda e: e.memset(epst[:], 1e-6), [], ["eps"])
    add(POOL, lambda e: e.affine_select(out=identb[:], in_=onesf[:], pattern=[[-1, 128]], compare_op=ALU.is_equal,
                                        fill=0.0, base=0, channel_multiplier=1), ["onesf"], ["identb"])
    add(POOL, lambda e: e.affine_select(out=CBt[:], in_=zerosb[:], pattern=[[1, 128]], compare_op=ALU.is_ge,
                                        fill=NEGB, base=0, channel_multiplier=-1), ["zerosb"], ["CB"])
    add(POOL, lambda e: e.affine_select(out=WBt[:], in_=zerosb[:], pattern=[[-1, 128]], compare_op=ALU.is_ge,
                                        fill=NEGB, base=-1, channel_multiplier=1), ["zerosb"], ["WB"])
    add(SP, lambda e: e.dma_start(out=gmix[:], in_=norm_mix_d.partition_broadcast(128)), [], ["gmix"])
    add(SP, lambda e: e.dma_start(out=gmlp[:], in_=norm_mlp_d.partition_broadcast(128)), [], ["gmlp"])
    add(SP, lambda e: e.dma_start(out=gfin[:], in_=norm_fin_d.partition_broadcast(128)), [], ["gfin"])

    def rms_stats(xt_ap, kx, ss, rstd, kst):
        add(ACT, lambda e: e.activation(out=junk[:], in_=xt_ap, func=AF.Square, accum_out=ss), [kx], ["junk", kst])
        add(ACT, lambda e: e.activation(out=rstd, in_=ss, func=AF.Sqrt, bias=epst[:, 0:1], scale=1.0 / D),
            [kst, "eps"], [kst])
        add(DVE, lambda e: e.reciprocal(out=rstd, in_=rstd), [kst], [kst])

    m_l1 = A.mark()
    OT = A([128, 4, S_LEN], BF16)
    m_l2 = A.mark()
    QT = A([128, 4, S_LEN], BF16)
    KSA = [A([128, S_LEN], BF16), A([128, S_LEN], BF16)]
    KWT = A([128, S_LEN], BF16)
    V2 = A([128, NT, 2, 2, 66], BF16)
    G = A([128, NT, 24], F32)
    KCMPT = A([128, 256], BF16)
    VCA = A([128, 2, 2, 130], BF16)
    CS = A([128, NT, 16], F32)
    CSC = A([128, 2, 16], F32)
    m_p1 = A.mark()
    KCT = A([128, S_LEN], BF16)
    VCT = A([128, S_LEN], BF16)
    m_p1a = A.mark()

    posi = A([128, NT], I32)
    posf = A([128, NT], F32)
    ang = A([128, NT * 16], F32)
    angn = A([128, NT * 16], F32)
    angi = A([128, NT * 16], I32)

    def build_cs(dst, ntile, base, cm, step):
        n = ntile * 16
        add(POOL, lambda e: e.iota(posi[:, 0:ntile], pattern=[[step, ntile]], base=base, channel_multiplier=cm), [], ["posi"])
        add(POOL, lambda e: e.tensor_copy(out=posf[:, 0:ntile], in_=posi[:, 0:ntile]), ["posi"], ["posf"])
        a3 = ang[:, 0:n].rearrange("p (t f) -> p t f", f=16)
        for f in range(8):
            add(DVE, lambda e, f=f: e.tensor_scalar(out=a3[:, :, 8 + f], in0=posf[:, 0:ntile], scalar1=INV_FREQ[f],
                                                    scalar2=None, op0=ALU.mult), ["posf"], ["ang"])
        add(DVE, lambda e: e.tensor_scalar(out=a3[:, :, 0:8], in0=a3[:, :, 8:16], scalar1=float(np.pi / 2), scalar2=None,
                                           op0=ALU.add), ["ang"], ["ang"])
        av, nv, iv = ang[:, 0:n], angn[:, 0:n], angi[:, 0:n]
        add(DVE, lambda e: e.tensor_scalar(out=nv, in0=av, scalar1=1.0 / TWO_PI, scalar2=None, op0=ALU.mult), ["ang"], ["angn"])
        add(DVE, lambda e: e.tensor_copy(out=iv, in_=nv), ["angn"], ["angi"])
        add(DVE, lambda e: e.tensor_copy(out=nv, in_=iv), ["angi"], ["angn"])
        add(DVE, lambda e: e.scalar_tensor_tensor(out=av, in0=nv, scalar=-6.28125, in1=av, op0=ALU.mult, op1=ALU.add),
            ["ang", "angn"], ["ang"])
        add(DVE, lambda e: e.scalar_tensor_tensor(out=av, in0=nv, scalar=-(TWO_PI - 6.28125), in1=av, op0=ALU.mult,
                                                  op1=ALU.add), ["ang", "angn"], ["ang"])
        add(DVE, lambda e: e.tensor_scalar(out=nv, in0=av, scalar1=float(np.pi), scalar2=-TWO_PI, op0=ALU.is_gt,
                                           op1=ALU.mult), ["ang"], ["angn"])
        add(DVE, lambda e: e.tensor_tensor(out=av, in0=av, in1=nv, op=ALU.add), ["ang", "angn"], ["ang"])
        add(DVE, lambda e: e.tensor_scalar(out=nv, in0=av, scalar1=-float(np.pi), scalar2=TWO_PI, op0=ALU.is_lt,
                                           op1=ALU.mult), ["ang"], ["angn"])
        add(DVE, lambda e: e.tensor_tensor(out=av, in0=av, in1=nv, op=ALU.add), ["ang", "angn"], ["ang"])
        add(DVE, lambda e: e.tensor_scalar(out=av, in0=av, scalar1=3.141592, scalar2=-3.141592, op0=ALU.min,
                                           op1=ALU.max), ["ang"], ["ang"])
        add(ACT, lambda e: e.activation(out=dst, in_=av, func=AF.Sin), ["ang"], ["cs"])

    build_cs(CS[:, :, :].rearrange("p t f -> p (t f)"), NT, 0, 1, 128)
    build_cs(CSC[:, :, :].rearrange("p t f -> p (t f)"), 2, 31, 16, 2048)

    for g_, prt_ in ((0, slice(64, 128)), (1, slice(0, 64))):
        ev = KSA[g_][prt_, :].rearrange("p (j c) -> p j c", c=64)
        add(POOL, lambda e: e.memset(KSA[g_][prt_, :], 1.0), [], [("KSAE", g_)])
        add(POOL, lambda e: e.affine_select(out=ev, in_=ev, pattern=[[-1, 64], [0, 64]], compare_op=ALU.is_equal,
                                            fill=0.0, base=0, channel_multiplier=1), [("KSAE", g_)], [("KSAE", g_)])
    v2keys = [("V2", t) for t in range(NT)]
    add(POOL, lambda e: e.memset(V2[:], 1.0), [], v2keys)
    add(POOL, lambda e: e.memset(VCA[:], 1.0), [], ["VCA"])
    mtmp = A([128, 2, 64], F32)
    mtmp2 = A([128, 2, 64], F32)
    for ct in range(2):
        add(POOL, lambda e, ct=ct: e.affine_select(out=mtmp[:, ct, :], in_=onesf[:, 0:64], pattern=[[-4, 64]],
                                                   compare_op=ALU.is_ge, fill=0.0, base=128 * ct + 1, channel_multiplier=1),
            ["onesf"], ["mtmp"])
        add(POOL, lambda e, ct=ct: e.affine_select(out=mtmp[:, ct, :], in_=mtmp[:, ct, :], pattern=[[4, 64]],
                                                   compare_op=ALU.is_ge, fill=0.0, base=3 - 128 * ct, channel_multiplier=-1),
            ["mtmp"], ["mtmp"])
        add(POOL, lambda e, ct=ct: e.affine_select(out=mtmp2[:, ct, :], in_=onesf[:, 0:64], pattern=[[-4, 64]],
                                                   compare_op=ALU.is_ge, fill=0.0, base=128 * ct, channel_multiplier=1),
            ["onesf"], ["mtmp2"])
        add(POOL, lambda e, ct=ct: e.affine_select(out=mtmp2[:, ct, :], in_=mtmp2[:, ct, :], pattern=[[4, 64]],
                                                   compare_op=ALU.is_ge, fill=0.0, base=2 - 128 * ct, channel_multiplier=-1),
            ["mtmp2"], ["mtmp2"])
    add(POOL, lambda e: e.tensor_tensor(out=mtmp[:], in0=mtmp[:], in1=mtmp2[:], op=ALU.add), ["mtmp", "mtmp2"], ["mtmp"])
    for g in range(2):
        add(POOL, lambda e, g=g: e.tensor_copy(out=VCA[:, :, g, 65:129], in_=mtmp[:]), ["mtmp"], ["VCA"])

    if stop_after < 1:
        return finish()
    WA = A([128, 8, 1304], BF16)
    xt = [A([128, D], F32), A([128, D], F32)]
    hbf = [A([128, D], BF16), A([128, D], BF16)]
    hT = [A([128, 8, 512], BF16), A([128, 8, 512], BF16)]
    qr = [A([128, 768], BF16), A([128, 768], BF16)]
    rt = A([128, 4, 64], F32)
    st = [A([128, 2], F32), A([128, 2], F32)]

    w_in_v = w_in_d.rearrange("(k p) n -> p k n", p=128)
    cast_engs = [POOL, ACT, DVE]
    for k in range(8):
        add(PQ, lambda e: e.dma_start(out=WA[:, k, :], in_=w_in_v[:, k, 0:1304]), [], [("WA", k)])
    WAK = [("WA", k) for k in range(8)]

    def stage_norm(tt):
        b = tt % 2
        add(SP, lambda e: e.dma_start(out=xt[b][:], in_=x_d[tt * 128:(tt + 1) * 128, :]), [], [("xt", b)])
        rms_stats(xt[b][:], ("xt", b), st[b][:, 0:1], st[b][:, 1:2], ("st", b))
        add(DVE, lambda e: e.scalar_tensor_tensor(out=hbf[b][:], in0=xt[b][:], scalar=st[b][:, 1:2], in1=gmix[:],
                                                  op0=ALU.mult, op1=ALU.mult), [("xt", b), ("st", b), "gmix"], [("hbf", b)])

    def stage_proj(tt):
        b = tt % 2
        I, tl = tt // 4, tt % 4
        bI = I % 2
        tb = nb()
        for k in range(8):
            add(PE, lambda e, k=k: e.transpose(out=psb[tb][:, k * 128:(k + 1) * 128], in_=hbf[b][:, k * 128:(k + 1) * 128],
                                               identity=identb[:]), [("hbf", b), "identb"], [PK[tb]])
        add(ACT, lambda e: e.copy(out=hT[bI][:, :, tl * 128:(tl + 1) * 128],
                                  in_=psb[tb][:, :].rearrange("p (k t) -> p k t", k=8)), [PK[tb]], [("hT", bI, tl)])
        pa, pb, pc = nb(), nb(), nb()
        for k in range(8):
            mm(psf[pa][:, 0:512], hT[bI][:, k, tl * 128:(tl + 1) * 128], WA[:, k, 0:512], k == 0, k == 7,
               [("hT", bI, tl), ("WA", k)], [PK[pa]])
        for k in range(8):
            mm(psf[pb][:, 0:512], hT[bI][:, k, tl * 128:(tl + 1) * 128], WA[:, k, 768:1280], k == 0, k == 7,
               [("hT", bI, tl), ("WA", k)], [PK[pb]])
        for k in range(8):
            mm(psf[pc][:, 0:24], hT[bI][:, k, tl * 128:(tl + 1) * 128], WA[:, k, 1280:1304], k == 0, k == 7,
               [("hT", bI, tl), ("WA", k)], [PK[pc]])
        if LVL < 4:
            return
        cosq = CS[:, tt, 0:8].unsqueeze(1).unsqueeze(1).to_broadcast([128, 2, 4, 8])
        sinq = CS[:, tt, 8:16].unsqueeze(1).unsqueeze(1).to_broadcast([128, 2, 4, 8])
        qv = psf[pa][:, 0:512].rearrange("p (g i d) -> p g i d", g=2, i=4)
        qo = qr[b][:, 0:512].rearrange("p (i g d) -> p g i d", i=4, g=2)
        kv4 = psf[pb][:, 0:512].rearrange("p (a r d) -> p a r d", a=2, r=4)
        kvw = kv4[:, :, 0:2, :]
        vvw = kv4[:, :, 2:4, :]
        ko = qr[b][:, 512:768].rearrange("p (a g d) -> p a g d", a=2, g=2)
        cosk = CS[:, tt, 0:8].unsqueeze(1).unsqueeze(1).to_broadcast([128, 2, 2, 8])
        sink = CS[:, tt, 8:16].unsqueeze(1).unsqueeze(1).to_broadcast([128, 2, 2, 8])

        def rope(src, dst, cos, sin, n, rk, wk):
            shp = dict(g=2, i=n)
            t1 = rt[:, 0, 0:16 * n].rearrange("p (g i d) -> p g i d", **shp)
            t2 = rt[:, 1, 0:16 * n].rearrange("p (g i d) -> p g i d", **shp)
            t3 = rt[:, 2, 0:16 * n].rearrange("p (g i d) -> p g i d", **shp)
            t4 = rt[:, 3, 0:16 * n].rearrange("p (g i d) -> p g i d", **shp)
            x1, x2 = src[:, :, :, 0:8], src[:, :, :, 8:16]
            add(DVE, lambda e: e.tensor_tensor(out=t1, in0=x1, in1=cos, op=ALU.mult), rk + ["cs"], ["rt1"])
            add(DVE, lambda e: e.tensor_tensor(out=t2, in0=x2, in1=sin, op=ALU.mult), rk + ["cs"], ["rt2"])
            add(DVE, lambda e: e.tensor_tensor(out=t3, in0=x2, in1=cos, op=ALU.mult), rk + ["cs"], ["rt3"])
            add(DVE, lambda e: e.tensor_tensor(out=t4, in0=x1, in1=sin, op=ALU.mult), rk + ["cs"], ["rt4"])
            add(DVE, lambda e: e.tensor_tensor(out=dst[:, :, :, 0:8], in0=t1, in1=t2, op=ALU.subtract), ["rt1", "rt2"], wk)
            add(DVE, lambda e: e.tensor_tensor(out=dst[:, :, :, 8:16], in0=t3, in1=t4, op=ALU.add), ["rt3", "rt4"], wk)
            add(ACT, lambda e: e.copy(out=dst[:, :, :, 16:64], in_=src[:, :, :, 16:64]), rk, wk)

        rope(qv, qo, cosq, sinq, 4, [PK[pa]], [("qr", b)])
        rope(kvw, ko, cosk, sink, 2, [PK[pb]], [("qr", b)])
        add(ACT, lambda e: e.copy(out=V2[:, tt, :, :, 0:64], in_=vvw), [PK[pb]], [("V2", tt)])
        add(ACT, lambda e: e.copy(out=G[:, tt, :], in_=psf[pc][:, 0:24]), [PK[pc]], [("G", tt)])

    def stage_qT(tt):
        b = tt % 2
        tb = nb()
        for m in range(6):
            add(PE, lambda e, m=m: e.transpose(out=psb[tb][:, m * 128:(m + 1) * 128], in_=qr[b][:, m * 128:(m + 1) * 128],
                                               identity=identb[:]), [("qr", b), "identb"], [PK[tb]])
        cs = slice(tt * 128, (tt + 1) * 128)
        add(ACT, lambda e: e.copy(out=QT[:, :, cs], in_=psb[tb][:, 0:512].rearrange("p (m t) -> p m t", m=4)),
            [PK[tb]], [("QT", tt)])
        if LVL < 7:
            return
        add(ACT, lambda e: e.copy(out=KSA[0][0:64, cs], in_=psb[tb][0:64, 512:640]), [PK[tb]], [("KSA", 0, tt)])
        add(ACT, lambda e: e.copy(out=KSA[1][64:128, cs], in_=psb[tb][64:128, 512:640]), [PK[tb]], [("KSA", 1, tt)])
        if LVL < 8:
            return
        add(ACT, lambda e: e.copy(out=KWT[:, cs], in_=psb[tb][:, 640:768]), [PK[tb]], [("KWT", tt)])

    def stage_fm(I):
        bI = I % 2
        for m, dst, key in ((0, KCT, "KCT"), (1, VCT, "VCT")):
            p = nb()
            for k in range(8):
                mm(psf[p][:, 0:512], WA[:, k, 512 + 128 * m:640 + 128 * m], hT[bI][:, k, :], k == 0, k == 7,
                   [("hT", bI, 0), ("hT", bI, 1), ("hT", bI, 2), ("hT", bI, 3), ("WA", k)], [PK[p]])
            if m == 0:
                add(DVE, lambda e, p=p, dst=dst: e.tensor_copy(out=dst[:, I * 512:(I + 1) * 512], in_=psf[p][:, 0:512]),
                    [PK[p]], [(key, I)])
            else:
                add(ACT, lambda e, p=p, dst=dst: e.copy(out=dst[:, I * 512:(I + 1) * 512], in_=psf[p][:, 0:512]),
                    [PK[p]], [(key, I)])

    LVL = int(os.environ.get("MK_LVL", "9"))
    NTR = int(os.environ.get("MK_NT", str(NT)))
    for it in range(NTR + 2):
        if it < NTR and LVL >= 2:
            stage_norm(it)
        if 1 <= it <= NTR and LVL >= 3:
            stage_proj(it - 1)
            if (it - 1) % 4 == 3 and LVL >= 5:
                stage_fm((it - 1) // 4)
        if it >= 2 and LVL >= 6:
            stage_qT(it - 2)

    if stop_after < 2:
        return finish()
    gkeys_all = [("G", t_) for t_ in range(NT)]
    add(ACT, lambda e: e.activation(out=G[:, :, :].rearrange("p t c -> p (t c)"), in_=G[:, :, :].rearrange("p t c -> p (t c)"),
                                    func=AF.Sigmoid), gkeys_all, gkeys_all)
    S.barrier()
    A.release(m_p1a)
    W1 = [A([128, 32, 256], BF16), A([128, 32, 256], BF16)]
    pass
    W2 = A([128, 2, 2, 64], BF16)
    w2st = A([128, 2, 2, 64], F32)
    pest = A([32, 2, 64], F32)
    pebf = A([32, 2, 64], BF16)
    peT = A([64, 2, 32], BF16)
    hbias = A([128, 4], F32)
    zt = A([128, 256], F32)
    z2t = A([128, 256], F32)
    sgt = A([128, 256], F32)
    gel = [[A([128, 2, 256], BF16) for g in range(2)] for xi in range(2)]
    kcr = A([128, 128], BF16)
    rt = A([128, 4, 64], F32)
    for xi_ in range(2):
        for g_ in range(2):
            add(POOL, lambda e: e.memset(gel[xi_][g_][:], 0.0), [], [("gel", xi_, g_)])

    for xi, w1d in enumerate((w1k_d, w1v_d)):
        w1v_ = w1d.rearrange("(p d) n -> d p n", d=64)
        for j in range(4):
            for dup in range(2):
                add(PQ, lambda e: e.dma_start(out=W1[xi][dup * 64:(dup + 1) * 64, 8 * j:8 * j + 8, :],
                                              in_=w1v_[:, 8 * j:8 * j + 8, :]), [], [("W1", xi)])
    for xi, (w2d, ped) in enumerate(((w2k_d, pe_k_d), (w2v_d, pe_v_d))):
        add(SP, lambda e, xi=xi, w2d=w2d: e.dma_start(out=w2st[:, xi, :, :], in_=w2d.rearrange("(c p) n -> p c n", p=128)),
            [], [("w2st", xi)])
        add(SP, lambda e, xi=xi, ped=ped: e.dma_start(out=pest[:, xi, :], in_=ped[:, :]), [], [("pest", xi)])
    add(DVE, lambda e: e.tensor_copy(out=W2[:], in_=w2st[:]), [("w2st", 0), ("w2st", 1)], ["W2"])
    add(DVE, lambda e: e.tensor_copy(out=pebf[:], in_=pest[:]), [("pest", 0), ("pest", 1)], ["pebf"])
    L2 = int(os.environ.get("MK_L2", "9"))
    if L2 < 2:
        return finish()
    tb = nb()
    for xi in range(2):
        add(PE, lambda e, xi=xi: e.transpose(out=psb[tb][0:64, xi * 32:(xi + 1) * 32], in_=pebf[0:32, xi, :],
                                             identity=identb[0:32, 0:32]), ["pebf", "identb"], [PK[tb]])
    add(ACT, lambda e: e.copy(out=peT[:], in_=psb[tb][0:64, 0:64].rearrange("p (x t) -> p x t", x=2)), [PK[tb]], ["peT"])
    pbias = nb()
    for xi in range(2):
        for c in range(2):
            col = 2 * xi + c
            for p in range(32):
                mm(psf[pbias][:, col:col + 1], W1[xi][0:64, p, c * 128:(c + 1) * 128], peT[0:64, xi, p:p + 1],
                   p == 0, p == 31, [("W1", xi), "peT"], [PK[pbias]])
    add(DVE, lambda e: e.tensor_copy(out=hbias[:], in_=psf[pbias][:, 0:4]), [PK[pbias]], ["hbias"])
    if L2 < 3:
        return finish()
    KCK = [("KCT", I) for I in range(NB)]
    VCK = [("VCT", I) for I in range(NB)]
    for xi, (XT, xkeys) in enumerate(((KCT, KCK), (VCT, VCK))):
        for g in range(2):
            for c in range(2):
                p = nb()
                for pp in range(32):
                    mm(psf[p][:, 0:255], W1[xi][64 * g:64 * g + 64, pp, c * 128:(c + 1) * 128],
                       XT[64 * g:64 * g + 64, pp:pp + 4065:16], pp == 0, pp == 31, [("W1", xi)] + xkeys, [PK[p]])
                col = 2 * xi + c
                zv, z2v, sv = zt[:, 0:255], z2t[:, 0:255], sgt[:, 0:255]
                add(ACT, lambda e, p=p, col=col: e.activation(out=zv, in_=psf[p][:, 0:255], func=AF.Identity,
                                                              bias=hbias[:, col:col + 1], scale=1.0), [PK[p], "hbias"], ["zt"])
                add(POOL, lambda e: e.tensor_tensor(out=z2v, in0=zv, in1=zv, op=ALU.mult), ["zt"], ["z2t"])
                add(POOL, lambda e: e.tensor_scalar(out=z2v, in0=z2v, scalar1=0.044715, scalar2=1.0, op0=ALU.mult,
                                                    op1=ALU.add), ["z2t"], ["z2t"])
                add(POOL, lambda e: e.tensor_tensor(out=z2v, in0=z2v, in1=zv, op=ALU.mult), ["z2t", "zt"], ["z2t"])
                add(ACT, lambda e: e.activation(out=sv, in_=z2v, func=AF.Sigmoid, scale=1.5957691216057308), ["z2t"], ["sgt"])
                add(DVE, lambda e, xi=xi, g=g, c=c: e.tensor_tensor(out=gel[xi][g][:, c, 0:255], in0=sv, in1=zv, op=ALU.mult),
                    ["sgt", "zt"], [("gel", xi, g)])
    if L2 < 4:
        return finish()
    for ct in range(2):
        M = 128
        p = nb()
        for xi in range(2):
            for g in range(2):
                off = xi * 128 + g * 64
                for c in range(2):
                    mm(psf[p][0:M, off:off + 64], gel[xi][g][:, c, ct * 128:ct * 128 + M], W2[:, xi, c, :], c == 0, c == 1,
                       [("gel", xi, g), "W2"], [PK[p]])
        if L2 < 5:
            continue
        src = psf[p][:, 0:128].rearrange("p (g d) -> p g d", g=2)
        dst = kcr[:, :].rearrange("p (g d) -> p g d", g=2)
        cosc = CSC[:, ct, 0:8].unsqueeze(1).to_broadcast([128, 2, 8])
        sinc = CSC[:, ct, 8:16].unsqueeze(1).to_broadcast([128, 2, 8])
        t = [rt[:, q, 0:16].rearrange("p (g d) -> p g d", g=2) for q in range(4)]
        x1, x2 = src[:, :, 0:8], src[:, :, 8:16]
        add(DVE, lambda e: e.tensor_tensor(out=t[0], in0=x1, in1=cosc, op=ALU.mult), [PK[p], "cs"], ["rt1"])
        add(DVE, lambda e: e.tensor_tensor(out=t[1], in0=x2, in1=sinc, op=ALU.mult), [PK[p], "cs"], ["rt2"])
        add(DVE, lambda e: e.tensor_tensor(out=t[2], in0=x2, in1=cosc, op=ALU.mult), [PK[p], "cs"], ["rt3"])
        add(DVE, lambda e: e.tensor_tensor(out=t[3], in0=x1, in1=sinc, op=ALU.mult), [PK[p], "cs"], ["rt4"])
        add(DVE, lambda e: e.tensor_tensor(out=dst[:, :, 0:8], in0=t[0], in1=t[1], op=ALU.subtract), ["rt1", "rt2"], ["kcr"])
        add(DVE, lambda e: e.tensor_tensor(out=dst[:, :, 8:16], in0=t[2], in1=t[3], op=ALU.add), ["rt3", "rt4"], ["kcr"])
        if L2 < 6:
            continue
        add(ACT, lambda e: e.copy(out=dst[:, :, 16:64], in_=src[:, :, 16:64]), [PK[p]], ["kcr"])
        add(ACT, lambda e: e.copy(out=VCA[:, ct, :, 0:64], in_=psf[p][:, 128:256].rearrange("p (g d) -> p g d", g=2)),
            [PK[p]], ["VCA"])
        if L2 < 7:
            continue
        tb = nb()
        add(PE, lambda e: e.transpose(out=psb[tb][:, 0:128], in_=kcr[:, :], identity=identb[:]), ["kcr", "identb"], [PK[tb]])
        add(ACT, lambda e: e.copy(out=KCMPT[:, ct * 128:(ct + 1) * 128], in_=psb[tb][:, 0:128]), [PK[tb]], ["KCMPT"])

    if stop_after < 3:
        return finish()
    S.barrier()
    A.release(m_p1)
    AUG = [A([128, 8, 512], BF16), A([128, 8, 512], BF16)]
    Praw = [A([128, 512], BF16) for _ in range(3)]
    Pm = [A([128, 512], BF16) for _ in range(3)]
    oacc = A([128, 4, 8, 64], F32)
    impn = A([128, 4, 8, 64], F32)
    obf = A([128, 4, 512], BF16)
    dn = A([128, 4], F32)
    rd = A([128, 4], F32)
    cf = A([128, 4], F32)
    tmpo = A([128, 4, 64], F32)
    IG = A([128, 64], F32)
    IG2 = A([128, 64], F32)
    top16 = A([128, 16], F32)
    Mb = A([128, 4, 2, 64], BF16)
    add(DVE, lambda e: e.memset(Mb[:], 0.0), [], ["Mb"])

    SBK = [0, 1, 2]
    OBK = [3, 4, 5, 6]
    TBK = 7
    orr = [0]

    def qt_keys(I):
        return [("QT", 4 * I + q) for q in range(4)]

    L3 = int(os.environ.get("MK_L3", "9"))
    NB2 = int(os.environ.get("MK_NB2", str(NB)))
    for I in range(NB2):
        a = I % 2
        gk = [("G", 4 * I + q) for q in range(4)]
        items = []
        jobs_evac = {}

        def finish_job(br, h, banks):
            if os.environ.get("MK_NOFIN"):
                return
            Gv = G[:, 4 * I:4 * I + 4, br * 8 + h]
            if br == 0:
                for bi, bk in enumerate(banks):
                    v = psf[bk][:, 0:258].rearrange("p (q c) -> p q c", q=2)
                    add(DVE, lambda e, v=v, bi=bi: e.tensor_scalar(out=dn[:, 2 * bi:2 * bi + 2], in0=v[:, :, 64], scalar1=1e-30,
                                                                   scalar2=None, op0=ALU.max), [PK[bk]], ["dn"])
            else:
                v = psf[banks[0]][:, 0:260].rearrange("p (q c) -> p q c", q=4)
                add(DVE, lambda e, v=v: e.tensor_scalar(out=dn[:], in0=v[:, :, 64], scalar1=1e-30, scalar2=None, op0=ALU.max),
                    [PK[banks[0]]], ["dn"])
            add(DVE, lambda e: e.reciprocal(out=rd[:], in_=dn[:]), ["dn"], ["rd"])
            add(DVE, lambda e: e.tensor_tensor(out=cf[:], in0=rd[:], in1=Gv, op=ALU.mult), ["rd"] + gk, ["cf"])
            if br == 0:
                for bi, bk in enumerate(banks):
                    v = psf[bk][:, 0:258].rearrange("p (q c) -> p q c", q=2)
                    add(DVE, lambda e, v=v, bi=bi: e.tensor_tensor(
                        out=oacc[:, 2 * bi:2 * bi + 2, h, :], in0=v[:, :, 0:64],
                        in1=cf[:, 2 * bi:2 * bi + 2].unsqueeze(2).to_broadcast([128, 2, 64]), op=ALU.mult),
                        [PK[bk], "cf"], [("oacc", h)])
                    if I >= 2:
                        add(DVE, lambda e, v=v, bi=bi: e.tensor_tensor(
                            out=impn[:, 2 * bi:2 * bi + 2, h, :], in0=v[:, :, 65:129],
                            in1=rd[:, 2 * bi:2 * bi + 2].unsqueeze(2).to_broadcast([128, 2, 64]), op=ALU.mult),
                            [PK[bk], "rd"], [("impn", h)])
            else:
                v = psf[banks[0]][:, 0:260].rearrange("p (q c) -> p q c", q=4)
                add(DVE, lambda e, v=v: e.tensor_tensor(out=tmpo[:], in0=v[:, :, 0:64],
                                                        in1=cf[:].unsqueeze(2).to_broadcast([128, 4, 64]), op=ALU.mult),
                    [PK[banks[0]], "cf"], ["tmpo"])
                add(DVE, lambda e: e.tensor_tensor(out=oacc[:, :, h, :], in0=oacc[:, :, h, :], in1=tmpo[:], op=ALU.add),
                    ["tmpo", ("oacc", h)], [("oacc", h)])

        def cmp_items(h):
            g = h // 4
            hp = slice(64 * g, 64 * g + 64)
            nct = 1 if I <= 3 else 2
            b0 = OBK[orr[0] % 4]
            b1 = OBK[(orr[0] + 1) % 4]
            orr[0] += 2
            banks = [b0, b1]
            first = [True, True]
            for ct in range(nct):
                M = 128

                def Afn(slot, ct=ct, M=M):
                    sb_ = SBK[slot]
                    mm(psf[sb_][0:M, 0:512], KCMPT[hp, ct * 128:ct * 128 + M], QT[hp, h % 4, I * 512:(I + 1) * 512],
                       True, True, ["KCMPT"] + qt_keys(I), [PK[sb_]])
                    add(ACT, lambda e: e.activation(out=Praw[slot][0:M, :], in_=psf[sb_][0:M, 0:512], func=AF.Exp, scale=0.125),
                        [PK[sb_]], [("Praw", slot)])
                    add(POOL, lambda e: e.affine_select(out=Pm[slot][0:M, :], in_=Praw[slot][0:M, :], pattern=[[1, 512]],
                                                        compare_op=ALU.is_ge, fill=0.0, base=512 * I - 2048 * ct - 31,
                                                        channel_multiplier=-16), [("Praw", slot)], [("Pm", slot)])

                def Bfn(slot, ct=ct, M=M):
                    for ql in range(4):
                        bk = banks[ql // 2]
                        st_ = first[ql // 2]
                        first[ql // 2] = False
                        mm(psf[bk][:, (ql % 2) * 129:(ql % 2) * 129 + 129], Pm[slot][0:M, ql * 128:(ql + 1) * 128],
                           VCA[0:M, ct, g, 0:129], st_, ct == nct - 1, [("Pm", slot), "VCA"], [PK[bk]])
                    if ct == nct - 1:
                        finish_job(0, h, banks)

                items.append((Afn, Bfn))

        def sel_items(h):
            g = h // 4
            bk = OBK[orr[0] % 4]
            orr[0] += 1
            first = [True]
            nkt = 4 * I + 4
            for kt in range(nkt):
                m = kt - 4 * I
                c0 = max(0, m) * 128

                def Afn(slot, kt=kt, m=m, c0=c0):
                    sb_ = SBK[slot]
                    mm(psf[sb_][:, c0:512], KSA[g][:, kt * 128:(kt + 1) * 128], AUG[a][:, h, c0:512], True, m < 0,
                       [("KSA", g, kt), ("KSAE", g), ("AUG", a, h)], [PK[sb_]])
                    if m >= 0:
                        mm(psf[sb_][:, c0:c0 + 128], identb[:], CBt[:], False, True, ["identb", "CB"], [PK[sb_]])
                    add(ACT, lambda e: e.activation(out=Praw[slot][:, c0:512], in_=psf[sb_][:, c0:512], func=AF.Exp, scale=0.125),
                        [PK[sb_]], [("Praw", slot)])

                def Bfn(slot, kt=kt, m=m):
                    for ql in range(max(0, m), 4):
                        st_ = first[0]
                        first[0] = False
                        mm(psf[bk][:, ql * 65:ql * 65 + 65], Praw[slot][:, ql * 128:(ql + 1) * 128], V2[:, kt, 0, g, 0:65],
                           st_, kt == 4 * I + ql, [("Praw", slot), ("V2", kt)], [PK[bk]])
                    if kt == nkt - 1:
                        finish_job(1, h, [bk])

                items.append((Afn, Bfn))

        def win_items(h):
            g = h // 4
            hp = slice(64 * g, 64 * g + 64)
            bk = OBK[orr[0] % 4]
            orr[0] += 1
            first = [True]
            kts = list(range(max(0, 4 * I - 4), 4 * I + 4))
            for kt in kts:
                q0 = max(0, kt - 4 * I)
                q1 = min(3, kt + 4 - 4 * I)
                c0, c1 = q0 * 128, (q1 + 1) * 128

                def Afn(slot, kt=kt, c0=c0, c1=c1):
                    sb_ = SBK[slot]
                    diag = kt >= 4 * I
                    old = (kt + 4 <= 4 * I + 3)
                    mm(psf[sb_][:, c0:c1], KWT[hp, kt * 128:(kt + 1) * 128], QT[hp, h % 4, I * 512 + c0:I * 512 + c1],
                       True, not (diag or old), [("KWT", kt)] + qt_keys(I), [PK[sb_]])
                    if diag:
                        qd = kt - 4 * I
                        mm(psf[sb_][:, qd * 128:(qd + 1) * 128], identb[:], CBt[:], False, not old, ["identb", "CB"], [PK[sb_]])
                    if old:
                        qo_ = kt + 4 - 4 * I
                        mm(psf[sb_][:, qo_ * 128:(qo_ + 1) * 128], identb[:], WBt[:], False, True, ["identb", "WB"], [PK[sb_]])
                    add(ACT, lambda e: e.activation(out=Praw[slot][:, c0:c1], in_=psf[sb_][:, c0:c1], func=AF.Exp, scale=0.125),
                        [PK[sb_]], [("Praw", slot)])

                def Bfn(slot, kt=kt, q0=q0, q1=q1):
                    for ql in range(q0, q1 + 1):
                        st_ = first[0]
                        first[0] = False
                        last_kt = 4 * I + ql
                        mm(psf[bk][:, ql * 65:ql * 65 + 65], Praw[slot][:, ql * 128:(ql + 1) * 128], V2[:, kt, 1, g, 0:65],
                           st_, kt == last_kt, [("Praw", slot), ("V2", kt)], [PK[bk]])
                    if kt == kts[-1]:
                        finish_job(2, h, [bk])

                items.append((Afn, Bfn))

        def run_items():
            n = len(items)
            for i in range(n + 2):
                if i < n:
                    items[i][0](i % 3)
                if i >= 2:
                    items[i - 2][1]((i - 2) % 3)
            del items[:]

        add(ACT, lambda e: e.copy(out=AUG[a][0:64, 0:4, :], in_=QT[0:64, :, I * 512:(I + 1) * 512]),
            qt_keys(I), [("AUG", a, h) for h in range(4)])
        add(ACT, lambda e: e.copy(out=AUG[a][64:128, 4:8, :], in_=QT[64:128, :, I * 512:(I + 1) * 512]),
            qt_keys(I), [("AUG", a, h) for h in range(4, 8)])
        if I < 2:
            add(POOL, lambda e: e.memset(AUG[a][64:128, 0:4, :], 0.0), [], [("AUG", a, h) for h in range(4)])
            add(POOL, lambda e: e.memset(AUG[a][0:64, 4:8, :], 0.0), [], [("AUG", a, h) for h in range(4, 8)])

        if L3 < 1:
            continue
        for h in range(int(os.environ.get("MK_NH", "8"))):
            cmp_items(h)
        run_items()
        if L3 < 2:
            continue
        if I >= 2:
            for ql in range(4):
                i = 4 * I + ql
                hi = 2 * i
                for g in range(2):
                    add(DVE, lambda e, ql=ql, g=g: e.tensor_tensor(out=IG[:], in0=impn[:, ql, 4 * g, :], in1=impn[:, ql, 4 * g + 1, :],
                                                                   op=ALU.add), [("impn", 4 * g), ("impn", 4 * g + 1)], ["IG"])
                    add(DVE, lambda e, ql=ql, g=g: e.tensor_tensor(out=IG[:], in0=IG[:], in1=impn[:, ql, 4 * g + 2, :], op=ALU.add),
                        ["IG", ("impn", 4 * g + 2)], ["IG"])
                    add(DVE, lambda e, ql=ql, g=g: e.tensor_tensor(out=IG[:], in0=IG[:], in1=impn[:, ql, 4 * g + 3, :], op=ALU.add),
                        ["IG", ("impn", 4 * g + 3)], ["IG"])
                    add(DVE, lambda e, hi=hi: e.tensor_scalar(out=IG[:, hi - 1:hi], in0=IG[:, hi - 1:hi], scalar1=notHalf[:, 0:1],
                                                              scalar2=negHalf[:, 0:1], op0=ALU.mult, op1=ALU.add), ["IG", "half"], ["IG"])
                    add(DVE, lambda e, hi=hi: e.max(out=top16[:, 0:8], in_=IG[:, 1:hi]), ["IG"], ["top16"])
                    add(DVE, lambda e, hi=hi: e.match_replace(out=IG2[:, 1:hi], in_to_replace=top16[:, 0:8], in_values=IG[:, 1:hi],
                                                              imm_value=-1e30), ["IG", "top16"], ["IG2"])
                    add(DVE, lambda e, hi=hi: e.max(out=top16[:, 8:16], in_=IG2[:, 1:hi]), ["IG2"], ["top16"])
                    add(DVE, lambda e, hi=hi, ql=ql, g=g: e.tensor_scalar(out=Mb[:, ql, g, 1:hi], in0=IG[:, 1:hi],
                                                                          scalar1=top16[:, 12:13], scalar2=NEGB, op0=ALU.is_lt,
                                                                          op1=ALU.mult), ["IG", "top16"], ["Mb"])
                    add(DVE, lambda e, hi=hi, ql=ql, g=g: e.tensor_scalar(out=Mb[:, ql, g, hi - 1:hi], in0=Mb[:, ql, g, hi - 1:hi],
                                                                          scalar1=notHalf[:, 0:1], scalar2=None, op0=ALU.mult),
                        ["Mb", "half"], ["Mb"])
        for h in range(8):
            win_items(h)
        run_items()
        if L3 < 3:
            continue
        if I >= 2:
            for ql in range(4):
                for g in range(2):
                    prt = slice(64, 128) if g == 0 else slice(0, 64)
                    add(PE, lambda e: e.transpose(out=psb[TBK][prt, ql * 128:(ql + 1) * 128], in_=Mb[:, ql, g, :],
                                                  identity=identb[:]), ["Mb", "identb"], ["T7"])
            add(ACT, lambda e: e.copy(out=AUG[a][64:128, 0:4, :],
                                      in_=psb[TBK][64:128, 0:512].unsqueeze(1).to_broadcast([64, 4, 512])),
                ["T7"], [("AUG", a, h) for h in range(4)])
            add(ACT, lambda e: e.copy(out=AUG[a][0:64, 4:8, :],
                                      in_=psb[TBK][0:64, 0:512].unsqueeze(1).to_broadcast([64, 4, 512])),
                ["T7"], [("AUG", a, h) for h in range(4, 8)])
        for h in range(8):
            sel_items(h)
        run_items()
        if L3 < 4:
            continue
        obanks = [SBK[0], SBK[1], SBK[2], TBK]
        for ql in range(4):
            add(DVE, lambda e: e.tensor_copy(out=obf[:, ql, :], in_=oacc[:, ql, :, :].rearrange("p h d -> p (h d)")),
                [("oacc", h) for h in range(8)], [("obf", ql)])
            bk_ = obanks[ql]
            kk = ["T7"] if bk_ == TBK else [PK[bk_]]
            for kc in range(4):
                add(PE, lambda e: e.transpose(out=psb[bk_][:, kc * 128:(kc + 1) * 128], in_=obf[:, ql, kc * 128:(kc + 1) * 128],
                                              identity=identb[:]), [("obf", ql), "identb"], kk)
            add(DVE, lambda e: e.tensor_copy(
                out=OT[:, :, I * 512 + ql * 128:I * 512 + (ql + 1) * 128],
                in_=psb[bk_][:, 0:512].rearrange("p (k t) -> p k t", k=4)), kk, [("OT", I)])

    if stop_after < 4:
        return finish()
    S.barrier()
    A.release(m_l2)
    W3 = A([128, 8, 2560], BF16)
    WBP = A([128, 4, D], BF16)
    WBA = A([128, 4, D], BF16)
    WOUT = A([128, 8, D], BF16)
    PW = A([128, 4, 128], BF16)
    pscl = A([128, 4], F32)
    XB = A([128, 4, D], F32)
    hb3 = A([128, D], BF16)
    hT3 = A([128, 8, 512], BF16)
    UT = A([128, 4, 528], F32)
    pt = [A([128, 528], F32), A([128, 528], F32)]
    pooled = A([128, 4, 512], BF16)
    mixed = A([128, 4, 512], BF16)
    sga = [A([128, 512], F32), A([128, 512], F32)]
    sgb = [A([128, 512], F32), A([128, 512], F32)]
    merged = A([128, 8, 512], BF16)
    st3 = A([128, 8], F32)
    XBK = [("XB", t) for t in range(4)]
    add(POOL, lambda e: e.memset(UT[:], 0.0), [], [("UT", gi) for gi in range(4)])

    wbp_v = wbp_d.rearrange("(k p) n -> p k n", p=128)
    wba_v = wba_d.rearrange("(k p) n -> p k n", p=128)
    wout_v = wout_d.rearrange("(k p) n -> p k n", p=128)
    add(PQ, lambda e: e.dma_start(out=W3[:, :, 0:512], in_=w_in_v[:, :, 1304:1816]), [], [("W3c", "u")])
    add(PQ, lambda e: e.dma_start(out=PW[:, :, :], in_=poolw_d.rearrange("(g p) n -> p g n", p=128)), [], ["PW"])

    def p3_rest_of_weights():
        for dc2 in range(4):
            add(PQ, lambda e: e.dma_start(out=W3[:, :, 512 + dc2 * 256:512 + (dc2 + 1) * 256],
                                          in_=w_in_v[:, :, 1816 + dc2 * 256:1816 + (dc2 + 1) * 256]), [], [("W3c", "a", dc2)])
            add(PQ, lambda e: e.dma_start(out=W3[:, :, 1536 + dc2 * 256:1536 + (dc2 + 1) * 256],
                                          in_=w_in_v[:, :, 2840 + dc2 * 256:2840 + (dc2 + 1) * 256]), [], [("W3c", "b", dc2)])
            if dc2 == 0:
                for k in range(4):
                    add(PQ, lambda e: e.dma_start(out=WBA[:, k, :], in_=wba_v[:, k, :]), [], ["WBA"])
                    add(PQ, lambda e: e.dma_start(out=WBP[:, k, :], in_=wbp_v[:, k, :]), [], ["WBP"])
        for k in range(8):
            add(PQ, lambda e: e.dma_start(out=WOUT[:, k, :], in_=wout_v[:, k, :]), [], ["WOUT"])

    for g_ in range(4):
        add(SP, lambda e: e.dma_start(out=pscl[:, g_:g_ + 1], in_=pscale_d[g_ * 128:(g_ + 1) * 128, 0:1]), [], ["pscl"])
    W3K = [("W3", k) for k in range(8)]

    XB2 = [XB, A([128, 4, D], F32)]
    pt2 = [A([128, 528], F32), A([128, 528], F32)]
    ptg = [pt, pt, pt2, pt2]
    hk3 = [("hT3", t) for t in range(4)]
    mk3 = [("mixed", gi) for gi in range(4)]

    def p3_load(I):
        xb = I % 2
        for tl in range(4):
            tt = 4 * I + tl
            add(SP, lambda e: e.dma_start(out=XB2[xb][:, tl, :], in_=x_d[tt * 128:(tt + 1) * 128, :]), [], [("XB", xb, tl)])

    def p3_norm_tile(I, tl):
        xb = I % 2
        rms_stats(XB2[xb][:, tl, :], ("XB", xb, tl), st3[:, 2 * tl:2 * tl + 1], st3[:, 2 * tl + 1:2 * tl + 2], ("st3", tl))
        add(DVE, lambda e: e.scalar_tensor_tensor(out=hb3[:], in0=XB2[xb][:, tl, :], scalar=st3[:, 2 * tl + 1:2 * tl + 2],
                                                  in1=gmix[:], op0=ALU.mult, op1=ALU.mult),
            [("XB", xb, tl), ("st3", tl), "gmix"], ["hb3"])
        tb = nb()
        for k in range(8):
            add(PE, lambda e: e.transpose(out=psb[tb][:, k * 128:(k + 1) * 128], in_=hb3[:, k * 128:(k + 1) * 128],
                                          identity=identb[:]), ["hb3", "identb"], [PK[tb]])
        add(ACT, lambda e: e.copy(out=hT3[:, :, tl * 128:(tl + 1) * 128],
                                  in_=psb[tb][:, :].rearrange("p (k t) -> p k t", k=8)), [PK[tb]], [("hT3", tl)])

    def p3_pool_u(I):
        for gi in range(4):
            w = 2 ** (gi + 1)
            add(POOL, lambda e: e.tensor_copy(out=UT[:, gi, 0:16], in_=UT[:, gi, 512:528]), [("UT", gi)], [("UT", gi)])
            p = nb()
            for k in range(8):
                mm(psf[p][:, 0:512], W3[:, k, gi * 128:(gi + 1) * 128], hT3[:, k, :], k == 0, k == 7, hk3 + [("W3c", "u")], [PK[p]])
            add(ACT, lambda e: e.copy(out=UT[:, gi, 16:528], in_=psf[p][:, 0:512]), [PK[p]], [("UT", gi)])
            src = UT[:, gi, :]
            srck = ("UT", gi)
            lo = 0
            step = 1
            j = 0
            while step < w:
                lo += step
                dstt = ptg[gi][j % 2]
                add(POOL if gi < 2 else DVE, lambda e: e.tensor_tensor(out=dstt[:, lo:528], in0=src[:, lo:528],
                                                                      in1=src[:, lo - step:528 - step], op=ALU.add),
                    [srck], [("pt", gi // 2, j % 2)])
                src = dstt[:, :]
                srck = ("pt", gi // 2, j % 2)
                step *= 2
                j += 1
            add(DVE, lambda e: e.scalar_tensor_tensor(out=pooled[:, gi, :], in0=src[:, 16:528], scalar=1.0 / w,
                                                      in1=UT[:, gi, 16:528], op0=ALU.mult, op1=ALU.subtract),
                [srck, ("UT", gi)], [("pooled", gi)])
            if I == 0:
                for t in range(w - 1):
                    add(DVE, lambda e: e.scalar_tensor_tensor(
                        out=pooled[:, gi, t:t + 1], in0=src[:, 16 + t:17 + t], scalar=1.0 / (t + 1), in1=UT[:, gi, 16 + t:17 + t],
                        op0=ALU.mult, op1=ALU.subtract), [srck, ("UT", gi)], [("pooled", gi)])

    def p3_pool_mix(I):
        for gi in range(4):
            p2 = nb()
            mm(psf[p2][:, 0:512], PW[:, gi, :], pooled[:, gi, :], True, True, ["PW", ("pooled", gi)], [PK[p2]])
            add(DVE, lambda e: e.tensor_scalar(out=mixed[:, gi, :], in0=psf[p2][:, 0:512], scalar1=pscl[:, gi:gi + 1],
                                               scalar2=None, op0=ALU.mult), [PK[p2], "pscl"], [("mixed", gi)])

    def p3_merge(I):
        for dc in range(8):
            j = dc % 2
            pa, pb_, pga, pgb = nb(), nb(), nb(), nb()
            dsl = slice(dc * 128, (dc + 1) * 128)
            for k in range(8):
                mm(psf[pga][:, 0:512], W3[:, k, 512 + dc * 128:512 + (dc + 1) * 128], hT3[:, k, :], k == 0, k == 7,
                   hk3 + [("W3c", "a", dc // 2)], [PK[pga]])
            for k in range(8):
                mm(psf[pgb][:, 0:512], W3[:, k, 1536 + dc * 128:1536 + (dc + 1) * 128], hT3[:, k, :], k == 0, k == 7,
                   hk3 + [("W3c", "b", dc // 2)], [PK[pgb]])
            for k in range(4):
                mm(psf[pa][:, 0:512], WBA[:, k, dsl], OT[:, k, I * 512:(I + 1) * 512], k == 0, k == 3, ["WBA", ("OT", I)], [PK[pa]])
            for k in range(4):
                mm(psf[pb_][:, 0:512], WBP[:, k, dsl], mixed[:, k, :], k == 0, k == 3, ["WBP"] + mk3, [PK[pb_]])
            add(ACT, lambda e: e.activation(out=sga[j][:], in_=psf[pga][:, 0:512], func=AF.Sigmoid), [PK[pga]], [("sga", j)])
            add(ACT, lambda e: e.activation(out=sgb[j][:], in_=psf[pgb][:, 0:512], func=AF.Sigmoid), [PK[pgb]], [("sgb", j)])
            add(DVE, lambda e: e.tensor_tensor(out=sga[j][:], in0=sga[j][:], in1=psf[pa][:, 0:512], op=ALU.mult),
                [("sga", j), PK[pa]], [("sga", j)])
            add(DVE, lambda e: e.tensor_tensor(out=sgb[j][:], in0=sgb[j][:], in1=psf[pb_][:, 0:512], op=ALU.mult),
                [("sgb", j), PK[pb_]], [("sgb", j)])
            add(POOL, lambda e: e.tensor_tensor(out=merged[:, dc, :], in0=sga[j][:], in1=sgb[j][:], op=ALU.add),
                [("sga", j), ("sgb", j)], [("merged", dc)])

    def p3_out_tile(I, tl):
        xb = I % 2
        tt = 4 * I + tl
        mgk = [("merged", dc) for dc in range(8)]
        for half in range(2):
            p = nb()
            for k in range(8):
                mm(psf[p][:, 0:512], merged[:, k, tl * 128:(tl + 1) * 128], WOUT[:, k, half * 512:(half + 1) * 512],
                   k == 0, k == 7, mgk + ["WOUT"], [PK[p]])
            add(DVE, lambda e: e.tensor_tensor(
                out=XB2[xb][:, tl, half * 512:(half + 1) * 512], in0=XB2[xb][:, tl, half * 512:(half + 1) * 512],
                in1=psf[p][:, 0:512], op=ALU.add), [("XB", xb, tl), PK[p]], [("XB", xb, tl)])
        add(SP, lambda e: e.dma_start(out=x1_d[tt * 128:(tt + 1) * 128, :], in_=XB2[xb][:, tl, :]),
            [("XB", xb, tl)], [("x1d", tt)])

    p3_load(0)
    for tl in range(4):
        p3_norm_tile(0, tl)
    p3_pool_u(0)
    p3_pool_mix(0)
    p3_rest_of_weights()
    for I in range(NB):
        p3_merge(I)
        if I + 1 < NB:
            p3_load(I + 1)
        for tl in range(4):
            if I + 1 < NB:
                p3_norm_tile(I + 1, tl)
                if tl == 3:
                    p3_pool_u(I + 1)
            p3_out_tile(I, tl)
        if I + 1 < NB:
            p3_pool_mix(I + 1)

    if stop_after < 5:
        return finish()
    S.barrier()
    A.release(m_l1)
    WF1 = A([128, 8, DFF], BF16)
    WF2 = A([128, 32, D], BF16)
    X1 = [A([128, 2, D], F32), A([128, 2, D], F32)]
    hb4 = A([128, D], BF16)
    h2T = A([128, 8, 256], BF16)
    fT = A([128, 32, 256], BF16)
    st4 = A([128, 8], F32)
    rl = [A([128, 256], F32), A([128, 256], F32)]
    wff1_v = wff1_d.rearrange("(k p) n -> p k n", p=128)
    wff2_v = wff2_d.rearrange("(k p) n -> p k n", p=128)
    for fg in range(8):
        add(PQ, lambda e: e.dma_start(out=WF1[:, :, fg * 512:(fg + 1) * 512], in_=wff1_v[:, :, fg * 512:(fg + 1) * 512]),
            [], [("WF1c", fg)])
    for k2 in range(16):
        add(PQ, lambda e: e.dma_start(out=WF2[:, 2 * k2:2 * k2 + 2, :], in_=wff2_v[:, 2 * k2:2 * k2 + 2, :]), [], [("WF2", k2)])
    WF1K = [("WF1", k) for k in range(8)]

    NB4 = S_LEN // 256
    h2Tb = [h2T, A([128, 8, 256], BF16)]
    st4b = [st4, A([128, 8], F32)]

    hb4d = [hb4, A([128, D], BF16)]

    def p4_stageA_norm(J):
        b = J % 2
        for tl in range(2):
            tt = 2 * J + tl
            add(SP, lambda e: e.dma_start(out=X1[b][:, tl, :], in_=x1_d[tt * 128:(tt + 1) * 128, :]),
                [("x1d", tt)], [("X1", b, tl)])
        for tl in range(2):
            rms_stats(X1[b][:, tl, :], ("X1", b, tl), st4b[b][:, 2 * tl:2 * tl + 1], st4b[b][:, 2 * tl + 1:2 * tl + 2], ("st4", b, tl))
            add(DVE, lambda e: e.scalar_tensor_tensor(out=hb4d[tl][:], in0=X1[b][:, tl, :], scalar=st4b[b][:, 2 * tl + 1:2 * tl + 2],
                                                      in1=gmlp[:], op0=ALU.mult, op1=ALU.mult),
                [("X1", b, tl), ("st4", b, tl), "gmlp"], [("hb4", tl)])

    def p4_stageA_tr(J):
        b = J % 2
        for tl in range(2):
            tb = nb()
            for k in range(8):
                add(PE, lambda e: e.transpose(out=psb[tb][:, k * 128:(k + 1) * 128], in_=hb4d[tl][:, k * 128:(k + 1) * 128],
                                              identity=identb[:]), [("hb4", tl), "identb"], [PK[tb]])
            add(ACT, lambda e: e.copy(out=h2Tb[b][:, :, tl * 128:(tl + 1) * 128],
                                      in_=psb[tb][:, :].rearrange("p (k t) -> p k t", k=8)), [PK[tb]], [("h2T", b, tl)])

    def p4_stageB(J):
        b = J % 2
        hk = [("h2T", b, 0), ("h2T", b, 1)]
        for fc in range(32):
            p = nb()
            for k in range(8):
                mm(psf[p][:, 0:256], WF1[:, k, fc * 128:(fc + 1) * 128], h2Tb[b][:, k, :], k == 0, k == 7, hk + [("WF1c", fc // 4)], [PK[p]])
            rj = fc % 2
            add(ACT, lambda e: e.activation(out=rl[rj][:], in_=psf[p][:, 0:256], func=AF.Relu), [PK[p]], [("rl", rj)])
            add(DVE if rj == 0 else POOL, lambda e: e.tensor_tensor(out=fT[:, fc, :], in0=rl[rj][:], in1=rl[rj][:], op=ALU.mult),
                [("rl", rj)], [("fT", fc)])
            if fc == 10 and J + 1 < NB4:
                p4_stageA_norm(J + 1)

    def p4_stageC(J):
        b = J % 2
        ftk = [("fT", fc) for fc in range(32)]
        for tl in range(2):
            tt = 2 * J + tl
            for half in range(2):
                p = nb()
                for fc in range(32):
                    mm(psf[p][:, 0:512], fT[:, fc, tl * 128:(tl + 1) * 128], WF2[:, fc, half * 512:(half + 1) * 512],
                       fc == 0, fc == 31, ftk + [("WF2", fc // 2)], [PK[p]])
                add(DVE, lambda e: e.tensor_tensor(
                    out=X1[b][:, tl, half * 512:(half + 1) * 512], in0=X1[b][:, tl, half * 512:(half + 1) * 512],
                    in1=psf[p][:, 0:512], op=ALU.add), [("X1", b, tl), PK[p]], [("X1", b, tl)])
            rms_stats(X1[b][:, tl, :], ("X1", b, tl), st4b[b][:, 4 + 2 * tl:5 + 2 * tl], st4b[b][:, 5 + 2 * tl:6 + 2 * tl], ("st4f", b, tl))
            add(DVE, lambda e: e.scalar_tensor_tensor(out=X1[b][:, tl, :], in0=X1[b][:, tl, :],
                                                      scalar=st4b[b][:, 5 + 2 * tl:6 + 2 * tl], in1=gfin[:], op0=ALU.mult,
                                                      op1=ALU.mult), [("X1", b, tl), ("st4f", b, tl), "gfin"], [("X1", b, tl)])
            add(SP, lambda e: e.dma_start(out=out_d[tt * 128:(tt + 1) * 128, :], in_=X1[b][:, tl, :]),
                [("X1", b, tl)], [("out", tt)])

    p4_stageA_norm(0)
    p4_stageA_tr(0)
    for J in range(NB4):
        p4_stageB(J)
        if J + 1 < NB4:
            p4_stageA_tr(J + 1)
        p4_stageC(J)

    return finish()


_INPUT_ORDER = ["x", "norm_mix", "w_in", "cmp_pe_k", "cmp_pe_v", "cmp_k_w1", "cmp_k_w2", "cmp_v_w1", "cmp_v_w2",
                "w_branch_attn", "pool_w", "pool_scale", "w_branch_pool", "w_out", "norm_mlp", "w_ff1", "w_ff2", "norm_final"]


def kernel(**inputs):
    f = lambda a: np.ascontiguousarray(np.asarray(a, dtype=np.float32))
    x = f(inputs["x"])
    shared = {
        "norm_mix": f(inputs["norm_mix"]).reshape(1, D),
        "w_in": f(inputs["w_in"]).reshape(D, 3864),
        "cmp_pe_k": f(inputs["cmp_pe_k"]).reshape(32, 64),
        "cmp_pe_v": f(inputs["cmp_pe_v"]).reshape(32, 64),
        "cmp_k_w1": f(inputs["cmp_k_w1"]).reshape(2048, 256),
        "cmp_k_w2": f(inputs["cmp_k_w2"]).reshape(256, 64),
        "cmp_v_w1": f(inputs["cmp_v_w1"]).reshape(2048, 256),
        "cmp_v_w2": f(inputs["cmp_v_w2"]).reshape(256, 64),
        "w_branch_attn": f(inputs["w_branch_attn"]).reshape(512, D),
        "pool_w": f(inputs["pool_w"]).reshape(512, 128),
        "pool_scale": f(inputs["pool_scale"]).reshape(512, 1),
        "w_branch_pool": f(inputs["w_branch_pool"]).reshape(512, D),
        "w_out": f(inputs["w_out"]).reshape(D, D),
        "norm_mlp": f(inputs["norm_mlp"]).reshape(1, D),
        "w_ff1": f(inputs["w_ff1"]).reshape(D, DFF),
        "w_ff2": f(inputs["w_ff2"]).reshape(DFF, D),
        "norm_final": f(inputs["norm_final"]).reshape(1, D),
    }
    nc = build_nc()
    in_maps = []
    for c in range(8):
        m = dict(shared)
        m["x"] = np.ascontiguousarray(x[c])
        in_maps.append(m)
    res = run_bass_kernel_spmd(nc, in_maps, core_ids=list(range(8)))
    return np.stack([np.asarray(r["out"], dtype=np.float32).reshape(S_LEN, D) for r in res.results], axis=0)
```

```python
import os
import numpy as np
import concourse.bass as bass
import concourse.mybir as mybir
from concourse.bass_utils import run_bass_kernel_spmd

F32 = mybir.dt.float32
BF16 = mybir.dt.bfloat16
I32 = mybir.dt.int32
ALU = mybir.AluOpType
AF = mybir.ActivationFunctionType

PE, ACT, DVE, POOL, SP, PQ = "pe", "act", "dve", "pool", "sp", "pq"
COMPUTE = (PE, ACT, DVE, POOL)
N_DMA_SEMS = 16
N_PQ_SEMS = 8
N_ALL_DSEMS = N_DMA_SEMS + N_PQ_SEMS

S_LEN = 4096
D = 1024
NT = 32
NB = 8
DFF = 4096
NEGB = -3000.0
INV_FREQ = [float(np.float32(500000.0) ** (-np.float32(2 * i) / np.float32(16))) for i in range(8)]
TWO_PI = float(2 * np.pi)


class Op:
    __slots__ = ("eng", "emit", "deps", "idx", "dsem", "dval", "dprev")

    def __init__(self, eng, emit):
        self.eng = eng
        self.emit = emit
        self.deps = set()
        self.idx = 0
        self.dsem = None
        self.dval = 0
        self.dprev = 0


class Sched:
    def __init__(self):
        self.streams = {e: [] for e in COMPUTE + (SP,)}
        self.nreal = {e: 0 for e in COMPUTE + (SP,)}
        self.lastreal = {e: None for e in COMPUTE}
        self.last_w = {}
        self.readers = {}
        self.dma_rr = 0
        self.pq_rr = 0
        self.dma_cnt = [0] * N_ALL_DSEMS
        self.dma_last = [None] * N_ALL_DSEMS

    def add(self, eng, emit, reads=(), writes=()):
        op = Op(eng, emit)
        for k in reads:
            w = self.last_w.get(k)
            if w is not None:
                op.deps.add(w)
        for k in writes:
            w = self.last_w.get(k)
            if w is not None:
                op.deps.add(w)
            for r in self.readers.get(k, ()):
                op.deps.add(r)
        for k in writes:
            self.last_w[k] = op
            self.readers[k] = []
        ws = set(writes)
        for k in reads:
            if k not in ws:
                self.readers.setdefault(k, []).append(op)
        op.deps.discard(op)
        self.streams[POOL if eng == PQ else eng].append(op)
        if eng == SP or eng == PQ:
            if eng == SP:
                s = self.dma_rr
                self.dma_rr = (self.dma_rr + 1) % N_DMA_SEMS
            else:
                s = N_DMA_SEMS + self.pq_rr
                self.pq_rr = (self.pq_rr + 1) % N_PQ_SEMS
            op.dsem = s
            op.dprev = self.dma_cnt[s] * 16
            self.dma_cnt[s] += 1
            op.dval = self.dma_cnt[s] * 16
            self.dma_last[s] = op
        else:
            self.nreal[eng] += 1
            op.idx = self.nreal[eng]
            self.lastreal[eng] = op
        return op

    def barrier(self):
        deps = set()
        for e in COMPUTE:
            if self.lastreal[e] is not None:
                deps.add(self.lastreal[e])
        for s in range(N_ALL_DSEMS):
            if self.dma_last[s] is not None:
                deps.add(self.dma_last[s])
        for e in COMPUTE + (SP,):
            op = Op(e, None)
            op.deps = set(deps)
            self.streams[e].append(op)
        self.last_w = {}
        self.readers = {}

    def emit_all(self, block, sems, dsems):
        engmap = {PE: "tensor", ACT: "scalar", DVE: "vector", POOL: "gpsimd", SP: "sync"}
        for ename in (SP, PE, ACT, DVE, POOL):
            ops = self.streams[ename]

            def body(eng, ops=ops, ename=ename):
                waited = {}
                for op in ops:
                    need = {}
                    for d in op.deps:
                        if d.eng == SP or d.eng == PQ:
                            key = ("d", d.dsem)
                            val = d.dval
                        else:
                            if d.eng == ename and ename == PE:
                                continue
                            key = ("c", d.eng)
                            val = d.idx
                        if need.get(key, 0) < val:
                            need[key] = val
                    if (op.eng == SP or op.eng == PQ) and op.emit is not None and op.dprev > 0:
                        key = ("d", op.dsem)
                        if need.get(key, 0) < op.dprev:
                            need[key] = op.dprev
                    for key, val in need.items():
                        if waited.get(key, 0) >= val:
                            continue
                        waited[key] = val
                        sem = dsems[key[1]] if key[0] == "d" else sems[key[1]]
                        eng.wait_ge(sem, val)
                    if op.emit is None:
                        continue
                    ins = op.emit(eng)
                    if op.eng == SP or op.eng == PQ:
                        ins.then_inc(dsems[op.dsem], 16)
                    else:
                        ins.then_inc(sems[op.eng], 1)
                if ename == SP:
                    for s in range(N_DMA_SEMS):
                        if self.dma_cnt[s] > 0 and waited.get(("d", s), 0) < self.dma_cnt[s] * 16:
                            eng.wait_ge(dsems[s], self.dma_cnt[s] * 16)

            getattr(block, engmap[ename])(body)


class Alloc:
    def __init__(self, nc, base=16512, top=229344):
        self.nc = nc
        self.cur = base
        self.top = top
        self.n = 0

    def mark(self):
        return self.cur

    def release(self, m):
        self.cur = m

    def __call__(self, shape, dt):
        nbytes = int(np.prod(shape[1:])) * (2 if dt == BF16 else 4)
        nbytes = (nbytes + 31) // 32 * 32
        off = self.cur
        assert off + nbytes <= self.top, ("SBUF overflow", off, nbytes, self.top)
        self.cur += nbytes
        self.n += 1
        return self.nc.alloc_sbuf_tensor_at("t%d" % self.n, list(shape), dt, offset=off)


def build_nc(stop_after=99):
    nc = bass.Bass("TRN2", target_bir_lowering=False)

    def din(name, shape):
        return nc.dram_tensor(name, list(shape), F32, kind="ExternalInput").ap()

    x_d = din("x", [S_LEN, D])
    norm_mix_d = din("norm_mix", [1, D])
    w_in_d = din("w_in", [D, 3864])
    pe_k_d = din("cmp_pe_k", [32, 64])
    pe_v_d = din("cmp_pe_v", [32, 64])
    w1k_d = din("cmp_k_w1", [2048, 256])
    w2k_d = din("cmp_k_w2", [256, 64])
    w1v_d = din("cmp_v_w1", [2048, 256])
    w2v_d = din("cmp_v_w2", [256, 64])
    wba_d = din("w_branch_attn", [512, D])
    poolw_d = din("pool_w", [512, 128])
    pscale_d = din("pool_scale", [512, 1])
    wbp_d = din("w_branch_pool", [512, D])
    wout_d = din("w_out", [D, D])
    norm_mlp_d = din("norm_mlp", [1, D])
    wff1_d = din("w_ff1", [D, DFF])
    wff2_d = din("w_ff2", [DFF, D])
    norm_fin_d = din("norm_final", [1, D])
    out_d = nc.dram_tensor("out", [S_LEN, D], F32, kind="ExternalOutput").ap()
    x1_d = nc.dram_tensor("x1_scratch", [S_LEN, D], F32).ap()

    S = Sched()
    A = Alloc(nc)

    class _Rec:
        def __getattr__(self, name):
            def f(*a, **kw):
                return (name, a, kw)
            return f

    _REC = _Rec()

    def add(eng, fn, reads=(), writes=()):
        call = fn(_REC)
        op = S.add(eng, lambda e: getattr(e, call[0])(*call[1], **call[2]), reads, writes)
        if os.environ.get("MK_DUMP"):
            S.dbg = getattr(S, "dbg", [])
            S.dbg.append((eng, op.idx, call[0], [(d.eng, d.idx, d.dval) for d in op.deps], list(reads), list(writes)))
        return op


    def finish():
        from contextlib import ExitStack
        with ExitStack() as es:
            sems = {e: es.enter_context(nc.semaphore("sem_" + e)) for e in COMPUTE}
            dsems = [es.enter_context(nc.semaphore("dsem%d" % i)) for i in range(N_ALL_DSEMS)]
            block = es.enter_context(nc.Block())
            S.emit_all(block, sems, dsems)
        return nc

    psf = [nc.alloc_psum_tensor("psb%d" % i, [128, 512], F32) for i in range(8)]
    psb = [p[:, :].bitcast(BF16) for p in psf]
    PK = [("ps", i) for i in range(8)]
    rr = [0]

    def nb():
        i = rr[0]
        rr[0] = (i + 1) % 8
        return i

    def mm(out, lhsT, rhs, start, stop, r, w):
        add(PE, lambda e: e.matmul(out, lhsT=lhsT, rhs=rhs, start=start, stop=stop, skip_group_check=True), r, w)

    identb = A([128, 128], BF16)
    onesf = A([128, 128], F32)
    zerosb = A([128, 128], BF16)
    CBt = A([128, 128], BF16)
    WBt = A([128, 128], BF16)
    gmix = A([128, D], F32)
    gmlp = A([128, D], F32)
    gfin = A([128, D], F32)
    epst = A([128, 1], F32)
    notHalf = A([128, 1], F32)
    negHalf = A([128, 1], F32)
    junk = A([128, D], BF16)

    add(POOL, lambda e: e.memset(onesf[:], 1.0), [], ["onesf"])
    add(POOL, lambda e: e.affine_select(out=notHalf[:], in_=onesf[:, 0:1], pattern=[[0, 1]], compare_op=ALU.is_ge,
                                        fill=0.0, base=-64, channel_multiplier=1), ["onesf"], ["half"])
    add(POOL, lambda e: e.tensor_scalar(out=negHalf[:], in0=notHalf[:], scalar1=-1.0, scalar2=None, op0=ALU.add), ["half"], ["half"])
    add(POOL, lambda e: e.memset(zerosb[:], 0.0), [], ["zerosb"])
    add(POOL, lambda e: e.memset(epst[:], 1e-6), [], ["eps"])
    add(POOL, lambda e: e.affine_select(out=identb[:], in_=onesf[:], pattern=[[-1, 128]], compare_op=ALU.is_equal,
                                        fill=0.0, base=0, channel_multiplier=1), ["onesf"], ["identb"])
    add(POOL, lambda e: e.affine_select(out=CBt[:], in_=zerosb[:], pattern=[[1, 128]], compare_op=ALU.is_ge,
                                        fill=NEGB, base=0, channel_multiplier=-1), ["zerosb"], ["CB"])
    add(POOL, lambda e: e.affine_select(out=WBt[:], in_=zerosb[:], pattern=[[-1, 128]], compare_op=ALU.is_ge,
                                        fill=NEGB, base=-1, channel_multiplier=1), ["zerosb"], ["WB"])
    add(SP, lambda e: e.dma_start(out=gmix[:], in_=norm_mix_d.partition_broadcast(128)), [], ["gmix"])
    add(SP, lambda e: e.dma_start(out=gmlp[:], in_=norm_mlp_d.partition_broadcast(128)), [], ["gmlp"])
    add(SP, lambda e: e.dma_start(out=gfin[:], in_=norm_fin_d.partition_broadcast(128)), [], ["gfin"])

    def rms_stats(xt_ap, kx, ss, rstd, kst):
        add(ACT, lambda e: e.activation(out=junk[:], in_=xt_ap, func=AF.Square, accum_out=ss), [kx], ["junk", kst])
        add(ACT, lambda e: e.activation(out=rstd, in_=ss, func=AF.Sqrt, bias=epst[:, 0:1], scale=1.0 / D),
            [kst, "eps"], [kst])
        add(DVE, lambda e: e.reciprocal(out=rstd, in_=rstd), [kst], [kst])

    m_l1 = A.mark()
    OT = A([128, 4, S_LEN], BF16)
    m_l2 = A.mark()
    QT = A([128, 4, S_LEN], BF16)
    KSA = [A([128, S_LEN], BF16), A([128, S_LEN], BF16)]
    KWT = A([128, S_LEN], BF16)
    V2 = A([128, NT, 2, 2, 66], BF16)
    G = A([128, NT, 24], F32)
    KCMPT = A([128, 256], BF16)
    VCA = A([128, 2, 2, 130], BF16)
    CS = A([128, NT, 16], F32)
    CSC = A([128, 2, 16], F32)
    m_p1 = A.mark()
    KCT = A([128, S_LEN], BF16)
    VCT = A([128, S_LEN], BF16)
    m_p1a = A.mark()

    posi = A([128, NT], I32)
    posf = A([128, NT], F32)
    ang = A([128, NT * 16], F32)
    angn = A([128, NT * 16], F32)
    angi = A([128, NT * 16], I32)

    def build_cs(dst, ntile, base, cm, step):
        n = ntile * 16
        add(POOL, lambda e: e.iota(posi[:, 0:ntile], pattern=[[step, ntile]], base=base, channel_multiplier=cm), [], ["posi"])
        add(POOL, lambda e: e.tensor_copy(out=posf[:, 0:ntile], in_=posi[:, 0:ntile]), ["posi"], ["posf"])
        a3 = ang[:, 0:n].rearrange("p (t f) -> p t f", f=16)
        for f in range(8):
            add(DVE, lambda e, f=f: e.tensor_scalar(out=a3[:, :, 8 + f], in0=posf[:, 0:ntile], scalar1=INV_FREQ[f],
                                                    scalar2=None, op0=ALU.mult), ["posf"], ["ang"])
        add(DVE, lambda e: e.tensor_scalar(out=a3[:, :, 0:8], in0=a3[:, :, 8:16], scalar1=float(np.pi / 2), scalar2=None,
                                           op0=ALU.add), ["ang"], ["ang"])
        av, nv, iv = ang[:, 0:n], angn[:, 0:n], angi[:, 0:n]
        add(DVE, lambda e: e.tensor_scalar(out=nv, in0=av, scalar1=1.0 / TWO_PI, scalar2=None, op0=ALU.mult), ["ang"], ["angn"])
        add(DVE, lambda e: e.tensor_copy(out=iv, in_=nv), ["angn"], ["angi"])
        add(DVE, lambda e: e.tensor_copy(out=nv, in_=iv), ["angi"], ["angn"])
        add(DVE, lambda e: e.scalar_tensor_tensor(out=av, in0=nv, scalar=-6.28125, in1=av, op0=ALU.mult, op1=ALU.add),
            ["ang", "angn"], ["ang"])
        add(DVE, lambda e: e.scalar_tensor_tensor(out=av, in0=nv, scalar=-(TWO_PI - 6.28125), in1=av, op0=ALU.mult,
                                                  op1=ALU.add), ["ang", "angn"], ["ang"])
        add(DVE, lambda e: e.tensor_scalar(out=nv, in0=av, scalar1=float(np.pi), scalar2=-TWO_PI, op0=ALU.is_gt,
                                           op1=ALU.mult), ["ang"], ["angn"])
        add(DVE, lambda e: e.tensor_tensor(out=av, in0=av, in1=nv, op=ALU.add), ["ang", "angn"], ["ang"])
        add(DVE, lambda e: e.tensor_scalar(out=nv, in0=av, scalar1=-float(np.pi), scalar2=TWO_PI, op0=ALU.is_lt,
                                           op1=ALU.mult), ["ang"], ["angn"])
        add(DVE, lambda e: e.tensor_tensor(out=av, in0=av, in1=nv, op=ALU.add), ["ang", "angn"], ["ang"])
        add(DVE, lambda e: e.tensor_scalar(out=av, in0=av, scalar1=3.141592, scalar2=-3.141592, op0=ALU.min,
                                           op1=ALU.max), ["ang"], ["ang"])
        add(ACT, lambda e: e.activation(out=dst, in_=av, func=AF.Sin), ["ang"], ["cs"])

    build_cs(CS[:, :, :].rearrange("p t f -> p (t f)"), NT, 0, 1, 128)
    build_cs(CSC[:, :, :].rearrange("p t f -> p (t f)"), 2, 31, 16, 2048)

    for g_, prt_ in ((0, slice(64, 128)), (1, slice(0, 64))):
        ev = KSA[g_][prt_, :].rearrange("p (j c) -> p j c", c=64)
        add(POOL, lambda e: e.memset(KSA[g_][prt_, :], 1.0), [], [("KSAE", g_)])
        add(POOL, lambda e: e.affine_select(out=ev, in_=ev, pattern=[[-1, 64], [0, 64]], compare_op=ALU.is_equal,
                                            fill=0.0, base=0, channel_multiplier=1), [("KSAE", g_)], [("KSAE", g_)])
    v2keys = [("V2", t) for t in range(NT)]
    add(POOL, lambda e: e.memset(V2[:], 1.0), [], v2keys)
    add(POOL, lambda e: e.memset(VCA[:], 1.0), [], ["VCA"])
    mtmp = A([128, 2, 64], F32)
    mtmp2 = A([128, 2, 64], F32)
    for ct in range(2):
        add(POOL, lambda e, ct=ct: e.affine_select(out=mtmp[:, ct, :], in_=onesf[:, 0:64], pattern=[[-4, 64]],
                                                   compare_op=ALU.is_ge, fill=0.0, base=128 * ct + 1, channel_multiplier=1),
            ["onesf"], ["mtmp"])
        add(POOL, lambda e, ct=ct: e.affine_select(out=mtmp[:, ct, :], in_=mtmp[:, ct, :], pattern=[[4, 64]],
                                                   compare_op=ALU.is_ge, fill=0.0, base=3 - 128 * ct, channel_multiplier=-1),
            ["mtmp"], ["mtmp"])
        add(POOL, lambda e, ct=ct: e.affine_select(out=mtmp2[:, ct, :], in_=onesf[:, 0:64], pattern=[[-4, 64]],
                                                   compare_op=ALU.is_ge, fill=0.0, base=128 * ct, channel_multiplier=1),
            ["onesf"], ["mtmp2"])
        add(POOL, lambda e, ct=ct: e.affine_select(out=mtmp2[:, ct, :], in_=mtmp2[:, ct, :], pattern=[[4, 64]],
                                                   compare_op=ALU.is_ge, fill=0.0, base=2 - 128 * ct, channel_multiplier=-1),
            ["mtmp2"], ["mtmp2"])
    add(POOL, lambda e: e.tensor_tensor(out=mtmp[:], in0=mtmp[:], in1=mtmp2[:], op=ALU.add), ["mtmp", "mtmp2"], ["mtmp"])
    for g in range(2):
        add(POOL, lambda e, g=g: e.tensor_copy(out=VCA[:, :, g, 65:129], in_=mtmp[:]), ["mtmp"], ["VCA"])

    if stop_after < 1:
        return finish()
    WA = A([128, 8, 1304], BF16)
    xt = [A([128, D], F32), A([128, D], F32)]
    hbf = [A([128, D], BF16), A([128, D], BF16)]
    hT = [A([128, 8, 512], BF16), A([128, 8, 512], BF16)]
    qr = [A([128, 768], BF16), A([128, 768], BF16)]
    rt = A([128, 4, 64], F32)
    st = [A([128, 2], F32), A([128, 2], F32)]

    w_in_v = w_in_d.rearrange("(k p) n -> p k n", p=128)
    cast_engs = [POOL, ACT, DVE]
    for k in range(8):
        add(PQ, lambda e: e.dma_start(out=WA[:, k, :], in_=w_in_v[:, k, 0:1304]), [], [("WA", k)])
    WAK = [("WA", k) for k in range(8)]

    def stage_norm(tt):
        b = tt % 2
        add(SP, lambda e: e.dma_start(out=xt[b][:], in_=x_d[tt * 128:(tt + 1) * 128, :]), [], [("xt", b)])
        rms_stats(xt[b][:], ("xt", b), st[b][:, 0:1], st[b][:, 1:2], ("st", b))
        add(DVE, lambda e: e.scalar_tensor_tensor(out=hbf[b][:], in0=xt[b][:], scalar=st[b][:, 1:2], in1=gmix[:],
                                                  op0=ALU.mult, op1=ALU.mult), [("xt", b), ("st", b), "gmix"], [("hbf", b)])

    def stage_proj1(tt):
        b = tt % 2
        I, tl = tt // 4, tt % 4
        bI = I % 2
        tb = nb()
        for k in range(8):
            add(PE, lambda e, k=k: e.transpose(out=psb[tb][:, k * 128:(k + 1) * 128], in_=hbf[b][:, k * 128:(k + 1) * 128],
                                               identity=identb[:]), [("hbf", b), "identb"], [PK[tb]])
        add(ACT, lambda e: e.copy(out=hT[bI][:, :, tl * 128:(tl + 1) * 128],
                                  in_=psb[tb][:, :].rearrange("p (k t) -> p k t", k=8)), [PK[tb]], [("hT", bI, tl)])

    def stage_proj(tt):
        b = tt % 2
        I, tl = tt // 4, tt % 4
        bI = I % 2
        pa, pb, pc = nb(), nb(), nb()
        for k in range(8):
            mm(psf[pa][:, 0:512], hT[bI][:, k, tl * 128:(tl + 1) * 128], WA[:, k, 0:512], k == 0, k == 7,
               [("hT", bI, tl), ("WA", k)], [PK[pa]])
        for k in range(8):
            mm(psf[pb][:, 0:512], hT[bI][:, k, tl * 128:(tl + 1) * 128], WA[:, k, 768:1280], k == 0, k == 7,
               [("hT", bI, tl), ("WA", k)], [PK[pb]])
        for k in range(8):
            mm(psf[pc][:, 0:24], hT[bI][:, k, tl * 128:(tl + 1) * 128], WA[:, k, 1280:1304], k == 0, k == 7,
               [("hT", bI, tl), ("WA", k)], [PK[pc]])
        if LVL < 4:
            return
        cosq = CS[:, tt, 0:8].unsqueeze(1).unsqueeze(1).to_broadcast([128, 2, 4, 8])
        sinq = CS[:, tt, 8:16].unsqueeze(1).unsqueeze(1).to_broadcast([128, 2, 4, 8])
        qv = psf[pa][:, 0:512].rearrange("p (g i d) -> p g i d", g=2, i=4)
        qo = qr[b][:, 0:512].rearrange("p (i g d) -> p g i d", i=4, g=2)
        kv4 = psf[pb][:, 0:512].rearrange("p (a r d) -> p a r d", a=2, r=4)
        kvw = kv4[:, :, 0:2, :]
        vvw = kv4[:, :, 2:4, :]
        ko = qr[b][:, 512:768].rearrange("p (a g d) -> p a g d", a=2, g=2)
        cosk = CS[:, tt, 0:8].unsqueeze(1).unsqueeze(1).to_broadcast([128, 2, 2, 8])
        sink = CS[:, tt, 8:16].unsqueeze(1).unsqueeze(1).to_broadcast([128, 2, 2, 8])

        def rope(src, dst, cos, sin, n, rk, wk):
            shp = dict(g=2, i=n)
            t1 = rt[:, 0, 0:16 * n].rearrange("p (g i d) -> p g i d", **shp)
            t2 = rt[:, 1, 0:16 * n].rearrange("p (g i d) -> p g i d", **shp)
            t3 = rt[:, 2, 0:16 * n].rearrange("p (g i d) -> p g i d", **shp)
            t4 = rt[:, 3, 0:16 * n].rearrange("p (g i d) -> p g i d", **shp)
            x1, x2 = src[:, :, :, 0:8], src[:, :, :, 8:16]
            add(DVE, lambda e: e.tensor_tensor(out=t1, in0=x1, in1=cos, op=ALU.mult), rk + ["cs"], ["rt1"])
            add(DVE, lambda e: e.tensor_tensor(out=t2, in0=x2, in1=sin, op=ALU.mult), rk + ["cs"], ["rt2"])
            add(DVE, lambda e: e.tensor_tensor(out=t3, in0=x2, in1=cos, op=ALU.mult), rk + ["cs"], ["rt3"])
            add(DVE, lambda e: e.tensor_tensor(out=t4, in0=x1, in1=sin, op=ALU.mult), rk + ["cs"], ["rt4"])
            add(DVE, lambda e: e.tensor_tensor(out=dst[:, :, :, 0:8], in0=t1, in1=t2, op=ALU.subtract), ["rt1", "rt2"], wk)
            add(DVE, lambda e: e.tensor_tensor(out=dst[:, :, :, 8:16], in0=t3, in1=t4, op=ALU.add), ["rt3", "rt4"], wk)
            add(ACT, lambda e: e.copy(out=dst[:, :, :, 16:64], in_=src[:, :, :, 16:64]), rk, wk)

        rope(qv, qo, cosq, sinq, 4, [PK[pa]], [("qr", b)])
        rope(kvw, ko, cosk, sink, 2, [PK[pb]], [("qr", b)])
        add(ACT, lambda e: e.copy(out=V2[:, tt, :, :, 0:64], in_=vvw), [PK[pb]], [("V2", tt)])
        add(ACT, lambda e: e.copy(out=G[:, tt, :], in_=psf[pc][:, 0:24]), [PK[pc]], [("G", tt)])

    def stage_qT(tt):
        b = tt % 2
        tb = nb()
        for m in range(6):
            add(PE, lambda e, m=m: e.transpose(out=psb[tb][:, m * 128:(m + 1) * 128], in_=qr[b][:, m * 128:(m + 1) * 128],
                                               identity=identb[:]), [("qr", b), "identb"], [PK[tb]])
        cs = slice(tt * 128, (tt + 1) * 128)
        add(ACT, lambda e: e.copy(out=QT[:, :, cs], in_=psb[tb][:, 0:512].rearrange("p (m t) -> p m t", m=4)),
            [PK[tb]], [("QT", tt)])
        if LVL < 7:
            return
        add(ACT, lambda e: e.copy(out=KSA[0][0:64, cs], in_=psb[tb][0:64, 512:640]), [PK[tb]], [("KSA", 0, tt)])
        add(ACT, lambda e: e.copy(out=KSA[1][64:128, cs], in_=psb[tb][64:128, 512:640]), [PK[tb]], [("KSA", 1, tt)])
        if LVL < 8:
            return
        add(ACT, lambda e: e.copy(out=KWT[:, cs], in_=psb[tb][:, 640:768]), [PK[tb]], [("KWT", tt)])

    def stage_fm(I):
        bI = I % 2
        for m, dst, key in ((0, KCT, "KCT"), (1, VCT, "VCT")):
            p = nb()
            for k in range(8):
                mm(psf[p][:, 0:512], WA[:, k, 512 + 128 * m:640 + 128 * m], hT[bI][:, k, :], k == 0, k == 7,
                   [("hT", bI, 0), ("hT", bI, 1), ("hT", bI, 2), ("hT", bI, 3), ("WA", k)], [PK[p]])
            if m == 0:
                add(DVE, lambda e, p=p, dst=dst: e.tensor_copy(out=dst[:, I * 512:(I + 1) * 512], in_=psf[p][:, 0:512]),
                    [PK[p]], [(key, I)])
            else:
                add(ACT, lambda e, p=p, dst=dst: e.copy(out=dst[:, I * 512:(I + 1) * 512], in_=psf[p][:, 0:512]),
                    [PK[p]], [(key, I)])

    LVL = int(os.environ.get("MK_LVL", "9"))
    NTR = int(os.environ.get("MK_NT", str(NT)))
    for it in range(NTR + 2):
        if 1 <= it <= NTR:
            stage_proj1(it - 1)
        if it >= 2:
            stage_qT(it - 2)
        if it < NTR:
            stage_norm(it)
        if 1 <= it <= NTR:
            stage_proj(it - 1)
            if (it - 1) % 4 == 3:
                stage_fm((it - 1) // 4)
    gkeys_all = [("G", t_) for t_ in range(NT)]
    add(ACT, lambda e: e.activation(out=G[:, :, :].rearrange("p t c -> p (t c)"), in_=G[:, :, :].rearrange("p t c -> p (t c)"),
                                    func=AF.Sigmoid), gkeys_all, gkeys_all)
    S.barrier()
    A.release(m_p1a)
    W1 = [A([128, 32, 256], BF16), A([128, 32, 256], BF16)]
    pass
    W2 = A([128, 2, 2, 64], BF16)
    w2st = A([128, 2, 2, 64], F32)
    pest = A([32, 2, 64], F32)
    pebf = A([32, 2, 64], BF16)
    peT = A([64, 2, 32], BF16)
    hbias = A([128, 4], F32)
    zt = A([128, 256], F32)
    z2t = A([128, 256], F32)
    sgt = A([128, 256], F32)
    gel = [[A([128, 2, 256], BF16) for g in range(2)] for xi in range(2)]
    kcr = A([128, 128], BF16)
    rt = A([128, 4, 64], F32)
    for xi_ in range(2):
        for g_ in range(2):
            add(POOL, lambda e: e.memset(gel[xi_][g_][:], 0.0), [], [("gel", xi_, g_)])

    for xi, w1d in enumerate((w1k_d, w1v_d)):
        w1v_ = w1d.rearrange("(p d) n -> d p n", d=64)
        for j in range(4):
            for dup in range(2):
                add(PQ, lambda e: e.dma_start(out=W1[xi][dup * 64:(dup + 1) * 64, 8 * j:8 * j + 8, :],
                                              in_=w1v_[:, 8 * j:8 * j + 8, :]), [], [("W1", xi)])
    for xi, (w2d, ped) in enumerate(((w2k_d, pe_k_d), (w2v_d, pe_v_d))):
        add(SP, lambda e, xi=xi, w2d=w2d: e.dma_start(out=w2st[:, xi, :, :], in_=w2d.rearrange("(c p) n -> p c n", p=128)),
            [], [("w2st", xi)])
        add(SP, lambda e, xi=xi, ped=ped: e.dma_start(out=pest[:, xi, :], in_=ped[:, :]), [], [("pest", xi)])
    add(DVE, lambda e: e.tensor_copy(out=W2[:], in_=w2st[:]), [("w2st", 0), ("w2st", 1)], ["W2"])
    add(DVE, lambda e: e.tensor_copy(out=pebf[:], in_=pest[:]), [("pest", 0), ("pest", 1)], ["pebf"])
    L2 = int(os.environ.get("MK_L2", "9"))
    if L2 < 2:
        return finish()
    tb = nb()
    for xi in range(2):
        add(PE, lambda e, xi=xi: e.transpose(out=psb[tb][0:64, xi * 32:(xi + 1) * 32], in_=pebf[0:32, xi, :],
                                             identity=identb[0:32, 0:32]), ["pebf", "identb"], [PK[tb]])
    add(ACT, lambda e: e.copy(out=peT[:], in_=psb[tb][0:64, 0:64].rearrange("p (x t) -> p x t", x=2)), [PK[tb]], ["peT"])
    pbias = nb()
    for xi in range(2):
        for c in range(2):
            col = 2 * xi + c
            for p in range(32):
                mm(psf[pbias][:, col:col + 1], W1[xi][0:64, p, c * 128:(c + 1) * 128], peT[0:64, xi, p:p + 1],
                   p == 0, p == 31, [("W1", xi), "peT"], [PK[pbias]])
    add(DVE, lambda e: e.tensor_copy(out=hbias[:], in_=psf[pbias][:, 0:4]), [PK[pbias]], ["hbias"])
    if L2 < 3:
        return finish()
    KCK = [("KCT", I) for I in range(NB)]
    VCK = [("VCT", I) for I in range(NB)]
    for xi, (XT, xkeys) in enumerate(((KCT, KCK), (VCT, VCK))):
        for g in range(2):
            for c in range(2):
                p = nb()
                for pp in range(32):
                    mm(psf[p][:, 0:255], W1[xi][64 * g:64 * g + 64, pp, c * 128:(c + 1) * 128],
                       XT[64 * g:64 * g + 64, pp:pp + 4065:16], pp == 0, pp == 31, [("W1", xi)] + xkeys, [PK[p]])
                col = 2 * xi + c
                zv, z2v, sv = zt[:, 0:255], z2t[:, 0:255], sgt[:, 0:255]
                add(ACT, lambda e, p=p, col=col: e.activation(out=zv, in_=psf[p][:, 0:255], func=AF.Identity,
                                                              bias=hbias[:, col:col + 1], scale=1.0), [PK[p], "hbias"], ["zt"])
                add(POOL, lambda e: e.tensor_tensor(out=z2v, in0=zv, in1=zv, op=ALU.mult), ["zt"], ["z2t"])
                add(POOL, lambda e: e.tensor_scalar(out=z2v, in0=z2v, scalar1=0.044715, scalar2=1.0, op0=ALU.mult,
                                                    op1=ALU.add), ["z2t"], ["z2t"])
                add(POOL, lambda e: e.tensor_tensor(out=z2v, in0=z2v, in1=zv, op=ALU.mult), ["z2t", "zt"], ["z2t"])
                add(ACT, lambda e: e.activation(out=sv, in_=z2v, func=AF.Sigmoid, scale=1.5957691216057308), ["z2t"], ["sgt"])
                add(DVE, lambda e, xi=xi, g=g, c=c: e.tensor_tensor(out=gel[xi][g][:, c, 0:255], in0=sv, in1=zv, op=ALU.mult),
                    ["sgt", "zt"], [("gel", xi, g)])
    if L2 < 4:
        return finish()
    for ct in range(2):
        M = 128
        p = nb()
        for xi in range(2):
            for g in range(2):
                off = xi * 128 + g * 64
                for c in range(2):
                    mm(psf[p][0:M, off:off + 64], gel[xi][g][:, c, ct * 128:ct * 128 + M], W2[:, xi, c, :], c == 0, c == 1,
                       [("gel", xi, g), "W2"], [PK[p]])
        if L2 < 5:
            continue
        src = psf[p][:, 0:128].rearrange("p (g d) -> p g d", g=2)
        dst = kcr[:, :].rearrange("p (g d) -> p g d", g=2)
        cosc = CSC[:, ct, 0:8].unsqueeze(1).to_broadcast([128, 2, 8])
        sinc = CSC[:, ct, 8:16].unsqueeze(1).to_broadcast([128, 2, 8])
        t = [rt[:, q, 0:16].rearrange("p (g d) -> p g d", g=2) for q in range(4)]
        x1, x2 = src[:, :, 0:8], src[:, :, 8:16]
        add(DVE, lambda e: e.tensor_tensor(out=t[0], in0=x1, in1=cosc, op=ALU.mult), [PK[p], "cs"], ["rt1"])
        add(DVE, lambda e: e.tensor_tensor(out=t[1], in0=x2, in1=sinc, op=ALU.mult), [PK[p], "cs"], ["rt2"])
        add(DVE, lambda e: e.tensor_tensor(out=t[2], in0=x2, in1=cosc, op=ALU.mult), [PK[p], "cs"], ["rt3"])
        add(DVE, lambda e: e.tensor_tensor(out=t[3], in0=x1, in1=sinc, op=ALU.mult), [PK[p], "cs"], ["rt4"])
        add(DVE, lambda e: e.tensor_tensor(out=dst[:, :, 0:8], in0=t[0], in1=t[1], op=ALU.subtract), ["rt1", "rt2"], ["kcr"])
        add(DVE, lambda e: e.tensor_tensor(out=dst[:, :, 8:16], in0=t[2], in1=t[3], op=ALU.add), ["rt3", "rt4"], ["kcr"])
        if L2 < 6:
            continue
        add(ACT, lambda e: e.copy(out=dst[:, :, 16:64], in_=src[:, :, 16:64]), [PK[p]], ["kcr"])
        add(ACT, lambda e: e.copy(out=VCA[:, ct, :, 0:64], in_=psf[p][:, 128:256].rearrange("p (g d) -> p g d", g=2)),
            [PK[p]], ["VCA"])
        if L2 < 7:
            continue
        tb = nb()
        add(PE, lambda e: e.transpose(out=psb[tb][:, 0:128], in_=kcr[:, :], identity=identb[:]), ["kcr", "identb"], [PK[tb]])
        add(ACT, lambda e: e.copy(out=KCMPT[:, ct * 128:(ct + 1) * 128], in_=psb[tb][:, 0:128]), [PK[tb]], ["KCMPT"])

    if stop_after < 3:
        return finish()
    S.barrier()
    A.release(m_p1)
    AUG = [A([128, 8, 512], BF16), A([128, 8, 512], BF16)]
    Praw = [A([128, 512], BF16) for _ in range(3)]
    Pm = [A([128, 512], BF16) for _ in range(3)]
    oacc = A([128, 4, 8, 64], F32)
    impn = A([128, 4, 8, 64], F32)
    obf = A([128, 4, 512], BF16)
    dn = A([128, 4], F32)
    rd = A([128, 4], F32)
    cf = A([128, 4], F32)
    tmpo = A([128, 4, 64], F32)
    IG = A([128, 64], F32)
    IG2 = A([128, 64], F32)
    top16 = A([128, 16], F32)
    Mb = A([128, 4, 2, 64], BF16)
    add(DVE, lambda e: e.memset(Mb[:], 0.0), [], ["Mb"])
    CMB = A([128, NB, 512], BF16)
    add(POOL, lambda e: e.memset(CMB[:], 0.0), [], [("CMB", I_) for I_ in range(NB)])
    for I_ in range(NB):
        ct_ = 0 if I_ <= 3 else 1
        add(POOL, lambda e: e.affine_select(out=CMB[:, I_, :], in_=CMB[:, I_, :], pattern=[[1, 512]], compare_op=ALU.is_ge,
                                            fill=NEGB, base=512 * I_ - 2048 * ct_ - 31, channel_multiplier=-16),
            [("CMB", I_)], [("CMB", I_)])

    SBK = [0, 1, 2]
    OBK = [3, 4, 5, 6]
    TBK = 7
    orr = [0]

    def qt_keys(I):
        return [("QT", 4 * I + q) for q in range(4)]

    L3 = int(os.environ.get("MK_L3", "9"))
    NB2 = int(os.environ.get("MK_NB2", str(NB)))
    for I in range(NB2):
        a = I % 2
        gk = [("G", 4 * I + q) for q in range(4)]
        items = []
        jobs_evac = {}

        def finish_job(br, h, banks):
            if os.environ.get("MK_NOFIN"):
                return
            Gv = G[:, 4 * I:4 * I + 4, br * 8 + h]
            if br == 0:
                for bi, bk in enumerate(banks):
                    v = psf[bk][:, 0:258].rearrange("p (q c) -> p q c", q=2)
                    add(DVE, lambda e, v=v, bi=bi: e.tensor_scalar(out=dn[:, 2 * bi:2 * bi + 2], in0=v[:, :, 64], scalar1=1e-30,
                                                                   scalar2=None, op0=ALU.max), [PK[bk]], ["dn"])
            else:
                v = psf[banks[0]][:, 0:260].rearrange("p (q c) -> p q c", q=4)
                add(DVE, lambda e, v=v: e.tensor_scalar(out=dn[:], in0=v[:, :, 64], scalar1=1e-30, scalar2=None, op0=ALU.max),
                    [PK[banks[0]]], ["dn"])
            add(DVE, lambda e: e.reciprocal(out=rd[:], in_=dn[:]), ["dn"], ["rd"])
            add(DVE, lambda e: e.tensor_tensor(out=cf[:], in0=rd[:], in1=Gv, op=ALU.mult), ["rd"] + gk, ["cf"])
            if br == 0:
                for bi, bk in enumerate(banks):
                    v = psf[bk][:, 0:258].rearrange("p (q c) -> p q c", q=2)
                    add(DVE, lambda e, v=v, bi=bi: e.tensor_tensor(
                        out=oacc[:, 2 * bi:2 * bi + 2, h, :], in0=v[:, :, 0:64],
                        in1=cf[:, 2 * bi:2 * bi + 2].unsqueeze(2).to_broadcast([128, 2, 64]), op=ALU.mult),
                        [PK[bk], "cf"], [("oacc", h)])
                    if I >= 2:
                        add(DVE, lambda e, v=v, bi=bi: e.tensor_tensor(
                            out=impn[:, 2 * bi:2 * bi + 2, h, :], in0=v[:, :, 65:129],
                            in1=rd[:, 2 * bi:2 * bi + 2].unsqueeze(2).to_broadcast([128, 2, 64]), op=ALU.mult),
                            [PK[bk], "rd"], [("impn", h)])
            else:
                v = psf[banks[0]][:, 0:260].rearrange("p (q c) -> p q c", q=4)
                add(DVE, lambda e, v=v: e.tensor_tensor(out=tmpo[:], in0=v[:, :, 0:64],
                                                        in1=cf[:].unsqueeze(2).to_broadcast([128, 4, 64]), op=ALU.mult),
                    [PK[banks[0]], "cf"], ["tmpo"])
                add(DVE, lambda e: e.tensor_tensor(out=oacc[:, :, h, :], in0=oacc[:, :, h, :], in1=tmpo[:], op=ALU.add),
                    ["tmpo", ("oacc", h)], [("oacc", h)])

        def cmp_items(h):
            g = h // 4
            hp = slice(64 * g, 64 * g + 64)
            nct = 1 if I <= 3 else 2
            b0 = OBK[orr[0] % 4]
            b1 = OBK[(orr[0] + 1) % 4]
            orr[0] += 2
            banks = [b0, b1]
            first = [True, True]
            for ct in range(nct):
                M = 128

                def Afn(slot, ct=ct, M=M):
                    sb_ = SBK[slot]
                    need_mask = (ct == 1) or (I <= 3)
                    mm(psf[sb_][0:M, 0:512], KCMPT[hp, ct * 128:ct * 128 + M], QT[hp, h % 4, I * 512:(I + 1) * 512],
                       True, not need_mask, ["KCMPT"] + qt_keys(I), [PK[sb_]])
                    if need_mask:
                        mm(psf[sb_][:, 0:512], identb[:], CMB[:, I, :], False, True, ["identb", ("CMB", I)], [PK[sb_]])
                    add(ACT, lambda e: e.activation(out=Praw[slot][0:M, :], in_=psf[sb_][0:M, 0:512], func=AF.Exp, scale=0.125),
                        [PK[sb_]], [("Praw", slot)])

                def Bfn(slot, ct=ct, M=M):
                    for ql in range(4):
                        bk = banks[ql // 2]
                        st_ = first[ql // 2]
                        first[ql // 2] = False
                        mm(psf[bk][:, (ql % 2) * 129:(ql % 2) * 129 + 129], Praw[slot][0:M, ql * 128:(ql + 1) * 128],
                           VCA[0:M, ct, g, 0:129], st_, ct == nct - 1, [("Praw", slot), "VCA"], [PK[bk]])
                    if ct == nct - 1:
                        finish_job(0, h, banks)

                items.append((Afn, Bfn))

        def sel_items(h):
            g = h // 4
            bk = OBK[orr[0] % 4]
            orr[0] += 1
            first = [True]
            nkt = 4 * I + 4
            for kt in range(nkt):
                m = kt - 4 * I
                c0 = max(0, m) * 128

                def Afn(slot, kt=kt, m=m, c0=c0):
                    sb_ = SBK[slot]
                    mm(psf[sb_][:, c0:512], KSA[g][:, kt * 128:(kt + 1) * 128], AUG[a][:, h, c0:512], True, m < 0,
                       [("KSA", g, kt), ("KSAE", g), ("AUG", a, h)], [PK[sb_]])
                    if m >= 0:
                        mm(psf[sb_][:, c0:c0 + 128], identb[:], CBt[:], False, True, ["identb", "CB"], [PK[sb_]])
                    add(ACT, lambda e: e.activation(out=Praw[slot][:, c0:512], in_=psf[sb_][:, c0:512], func=AF.Exp, scale=0.125),
                        [PK[sb_]], [("Praw", slot)])

                def Bfn(slot, kt=kt, m=m):
                    for ql in range(max(0, m), 4):
                        st_ = first[0]
                        first[0] = False
                        mm(psf[bk][:, ql * 65:ql * 65 + 65], Praw[slot][:, ql * 128:(ql + 1) * 128], V2[:, kt, 0, g, 0:65],
                           st_, kt == 4 * I + ql, [("Praw", slot), ("V2", kt)], [PK[bk]])
                    if kt == nkt - 1:
                        finish_job(1, h, [bk])

                items.append((Afn, Bfn))

        def win_items(h):
            g = h // 4
            hp = slice(64 * g, 64 * g + 64)
            bk = OBK[orr[0] % 4]
            orr[0] += 1
            first = [True]
            kts = list(range(max(0, 4 * I - 4), 4 * I + 4))
            for kt in kts:
                q0 = max(0, kt - 4 * I)
                q1 = min(3, kt + 4 - 4 * I)
                c0, c1 = q0 * 128, (q1 + 1) * 128

                def Afn(slot, kt=kt, c0=c0, c1=c1):
                    sb_ = SBK[slot]
                    diag = kt >= 4 * I
                    old = (kt + 4 <= 4 * I + 3)
                    mm(psf[sb_][:, c0:c1], KWT[hp, kt * 128:(kt + 1) * 128], QT[hp, h % 4, I * 512 + c0:I * 512 + c1],
                       True, not (diag or old), [("KWT", kt)] + qt_keys(I), [PK[sb_]])
                    if diag:
                        qd = kt - 4 * I
                        mm(psf[sb_][:, qd * 128:(qd + 1) * 128], identb[:], CBt[:], False, not old, ["identb", "CB"], [PK[sb_]])
                    if old:
                        qo_ = kt + 4 - 4 * I
                        mm(psf[sb_][:, qo_ * 128:(qo_ + 1) * 128], identb[:], WBt[:], False, True, ["identb", "WB"], [PK[sb_]])
                    add(ACT, lambda e: e.activation(out=Praw[slot][:, c0:c1], in_=psf[sb_][:, c0:c1], func=AF.Exp, scale=0.125),
                        [PK[sb_]], [("Praw", slot)])

                def Bfn(slot, kt=kt, q0=q0, q1=q1):
                    for ql in range(q0, q1 + 1):
                        st_ = first[0]
                        first[0] = False
                        last_kt = 4 * I + ql
                        mm(psf[bk][:, ql * 65:ql * 65 + 65], Praw[slot][:, ql * 128:(ql + 1) * 128], V2[:, kt, 1, g, 0:65],
                           st_, kt == last_kt, [("Praw", slot), ("V2", kt)], [PK[bk]])
                    if kt == kts[-1]:
                        finish_job(2, h, [bk])

                items.append((Afn, Bfn))

        def run_items():
            n = len(items)
            for i in range(n + 2):
                if i < n:
                    items[i][0](i % 3)
                if i >= 2:
                    items[i - 2][1]((i - 2) % 3)
            del items[:]

        add(ACT, lambda e: e.copy(out=AUG[a][0:64, 0:4, :], in_=QT[0:64, :, I * 512:(I + 1) * 512]),
            qt_keys(I), [("AUG", a, h) for h in range(4)])
        add(ACT, lambda e: e.copy(out=AUG[a][64:128, 4:8, :], in_=QT[64:128, :, I * 512:(I + 1) * 512]),
            qt_keys(I), [("AUG", a, h) for h in range(4, 8)])
        if I < 2:
            add(POOL, lambda e: e.memset(AUG[a][64:128, 0:4, :], 0.0), [], [("AUG", a, h) for h in range(4)])
            add(POOL, lambda e: e.memset(AUG[a][0:64, 4:8, :], 0.0), [], [("AUG", a, h) for h in range(4, 8)])

        if L3 < 1:
            continue
        for h in range(int(os.environ.get("MK_NH", "8"))):
            cmp_items(h)
        run_items()
        if L3 < 2:
            continue
        if I >= 2:
            for ql in range(4):
                i = 4 * I + ql
                hi = 2 * i
                for g in range(2):
                    add(DVE, lambda e, ql=ql, g=g: e.tensor_tensor(out=IG[:], in0=impn[:, ql, 4 * g, :], in1=impn[:, ql, 4 * g + 1, :],
                                                                   op=ALU.add), [("impn", 4 * g), ("impn", 4 * g + 1)], ["IG"])
                    add(DVE, lambda e, ql=ql, g=g: e.tensor_tensor(out=IG[:], in0=IG[:], in1=impn[:, ql, 4 * g + 2, :], op=ALU.add),
                        ["IG", ("impn", 4 * g + 2)], ["IG"])
                    add(DVE, lambda e, ql=ql, g=g: e.tensor_tensor(out=IG[:], in0=IG[:], in1=impn[:, ql, 4 * g + 3, :], op=ALU.add),
                        ["IG", ("impn", 4 * g + 3)], ["IG"])
                    add(DVE, lambda e, hi=hi: e.tensor_scalar(out=IG[:, hi - 1:hi], in0=IG[:, hi - 1:hi], scalar1=notHalf[:, 0:1],
                                                              scalar2=negHalf[:, 0:1], op0=ALU.mult, op1=ALU.add), ["IG", "half"], ["IG"])
                    add(DVE, lambda e, hi=hi: e.max(out=top16[:, 0:8], in_=IG[:, 1:hi]), ["IG"], ["top16"])
                    add(DVE, lambda e, hi=hi: e.match_replace(out=IG2[:, 1:hi], in_to_replace=top16[:, 0:8], in_values=IG[:, 1:hi],
                                                              imm_value=-1e30), ["IG", "top16"], ["IG2"])
                    add(DVE, lambda e, hi=hi: e.max(out=top16[:, 8:16], in_=IG2[:, 1:hi]), ["IG2"], ["top16"])
                    add(DVE, lambda e, hi=hi, ql=ql, g=g: e.tensor_scalar(out=Mb[:, ql, g, 1:hi], in0=IG[:, 1:hi],
                                                                          scalar1=top16[:, 12:13], scalar2=NEGB, op0=ALU.is_lt,
                                                                          op1=ALU.mult), ["IG", "top16"], ["Mb"])
                    add(DVE, lambda e, hi=hi, ql=ql, g=g: e.tensor_scalar(out=Mb[:, ql, g, hi - 1:hi], in0=Mb[:, ql, g, hi - 1:hi],
                                                                          scalar1=notHalf[:, 0:1], scalar2=None, op0=ALU.mult),
                        ["Mb", "half"], ["Mb"])
        for h in range(8):
            win_items(h)
        run_items()
        if L3 < 3:
            continue
        if I >= 2:
            for ql in range(4):
                for g in range(2):
                    prt = slice(64, 128) if g == 0 else slice(0, 64)
                    add(PE, lambda e: e.transpose(out=psb[TBK][prt, ql * 128:(ql + 1) * 128], in_=Mb[:, ql, g, :],
                                                  identity=identb[:]), ["Mb", "identb"], ["T7"])
            add(ACT, lambda e: e.copy(out=AUG[a][64:128, 0:4, :],
                                      in_=psb[TBK][64:128, 0:512].unsqueeze(1).to_broadcast([64, 4, 512])),
                ["T7"], [("AUG", a, h) for h in range(4)])
            add(ACT, lambda e: e.copy(out=AUG[a][0:64, 4:8, :],
                                      in_=psb[TBK][0:64, 0:512].unsqueeze(1).to_broadcast([64, 4, 512])),
                ["T7"], [("AUG", a, h) for h in range(4, 8)])
        for h in range(8):
            sel_items(h)
        run_items()
        if L3 < 4:
            continue
        obanks = [SBK[0], SBK[1], SBK[2], TBK]
        for ql in range(4):
            add(DVE, lambda e: e.tensor_copy(out=obf[:, ql, :], in_=oacc[:, ql, :, :].rearrange("p h d -> p (h d)")),
                [("oacc", h) for h in range(8)], [("obf", ql)])
            bk_ = obanks[ql]
            kk = ["T7"] if bk_ == TBK else [PK[bk_]]
            for kc in range(4):
                add(PE, lambda e: e.transpose(out=psb[bk_][:, kc * 128:(kc + 1) * 128], in_=obf[:, ql, kc * 128:(kc + 1) * 128],
                                              identity=identb[:]), [("obf", ql), "identb"], kk)
            add(DVE, lambda e: e.tensor_copy(
                out=OT[:, :, I * 512 + ql * 128:I * 512 + (ql + 1) * 128],
                in_=psb[bk_][:, 0:512].rearrange("p (k t) -> p k t", k=4)), kk, [("OT", I)])

    if stop_after < 4:
        return finish()
    S.barrier()
    A.release(m_l2)
    W3 = A([128, 8, 2560], BF16)
    WBP = A([128, 4, D], BF16)
    WBA = A([128, 4, D], BF16)
    WOUT = A([128, 8, D], BF16)
    PW = A([128, 4, 128], BF16)
    pscl = A([128, 4], F32)
    XB = A([128, 4, D], F32)
    hb3 = A([128, D], BF16)
    hT3 = A([128, 8, 512], BF16)
    UT = A([128, 4, 528], F32)
    pt = [A([128, 528], F32), A([128, 528], F32)]
    pooled = A([128, 4, 512], BF16)
    mixed = A([128, 4, 512], BF16)
    sga = [A([128, 512], F32), A([128, 512], F32)]
    sgb = [A([128, 512], F32), A([128, 512], F32)]
    merged = A([128, 8, 512], BF16)
    st3 = A([128, 8], F32)
    XBK = [("XB", t) for t in range(4)]
    add(POOL, lambda e: e.memset(UT[:], 0.0), [], [("UT", gi) for gi in range(4)])

    wbp_v = wbp_d.rearrange("(k p) n -> p k n", p=128)
    wba_v = wba_d.rearrange("(k p) n -> p k n", p=128)
    wout_v = wout_d.rearrange("(k p) n -> p k n", p=128)
    add(PQ, lambda e: e.dma_start(out=W3[:, :, 0:512], in_=w_in_v[:, :, 1304:1816]), [], [("W3c", "u")])
    add(PQ, lambda e: e.dma_start(out=PW[:, :, :], in_=poolw_d.rearrange("(g p) n -> p g n", p=128)), [], ["PW"])

    def p3_rest_of_weights():
        for dc2 in range(4):
            add(PQ, lambda e: e.dma_start(out=W3[:, :, 512 + dc2 * 256:512 + (dc2 + 1) * 256],
                                          in_=w_in_v[:, :, 1816 + dc2 * 256:1816 + (dc2 + 1) * 256]), [], [("W3c", "a", dc2)])
            add(PQ, lambda e: e.dma_start(out=W3[:, :, 1536 + dc2 * 256:1536 + (dc2 + 1) * 256],
                                          in_=w_in_v[:, :, 2840 + dc2 * 256:2840 + (dc2 + 1) * 256]), [], [("W3c", "b", dc2)])
            if dc2 == 0:
                for k in range(4):
                    add(PQ, lambda e: e.dma_start(out=WBA[:, k, :], in_=wba_v[:, k, :]), [], ["WBA"])
                    add(PQ, lambda e: e.dma_start(out=WBP[:, k, :], in_=wbp_v[:, k, :]), [], ["WBP"])
        for k in range(8):
            add(PQ, lambda e: e.dma_start(out=WOUT[:, k, :], in_=wout_v[:, k, :]), [], ["WOUT"])

    for g_ in range(4):
        add(SP, lambda e: e.dma_start(out=pscl[:, g_:g_ + 1], in_=pscale_d[g_ * 128:(g_ + 1) * 128, 0:1]), [], ["pscl"])
    W3K = [("W3", k) for k in range(8)]

    XB2 = [XB, A([128, 4, D], F32)]
    pt2 = [A([128, 528], F32), A([128, 528], F32)]
    ptg = [pt, pt, pt2, pt2]
    hk3 = [("hT3", t) for t in range(4)]
    mk3 = [("mixed", gi) for gi in range(4)]

    def p3_load(I):
        xb = I % 2
        for tl in range(4):
            tt = 4 * I + tl
            add(SP, lambda e: e.dma_start(out=XB2[xb][:, tl, :], in_=x_d[tt * 128:(tt + 1) * 128, :]), [], [("XB", xb, tl)])

    def p3_norm_tile(I, tl):
        xb = I % 2
        rms_stats(XB2[xb][:, tl, :], ("XB", xb, tl), st3[:, 2 * tl:2 * tl + 1], st3[:, 2 * tl + 1:2 * tl + 2], ("st3", tl))
        add(DVE, lambda e: e.scalar_tensor_tensor(out=hb3[:], in0=XB2[xb][:, tl, :], scalar=st3[:, 2 * tl + 1:2 * tl + 2],
                                                  in1=gmix[:], op0=ALU.mult, op1=ALU.mult),
            [("XB", xb, tl), ("st3", tl), "gmix"], ["hb3"])
        tb = nb()
        for k in range(8):
            add(PE, lambda e: e.transpose(out=psb[tb][:, k * 128:(k + 1) * 128], in_=hb3[:, k * 128:(k + 1) * 128],
                                          identity=identb[:]), ["hb3", "identb"], [PK[tb]])
        add(ACT, lambda e: e.copy(out=hT3[:, :, tl * 128:(tl + 1) * 128],
                                  in_=psb[tb][:, :].rearrange("p (k t) -> p k t", k=8)), [PK[tb]], [("hT3", tl)])

    def p3_pool_u(I):
        for gi in range(4):
            w = 2 ** (gi + 1)
            add(POOL, lambda e: e.tensor_copy(out=UT[:, gi, 0:16], in_=UT[:, gi, 512:528]), [("UT", gi)], [("UT", gi)])
            p = nb()
            for k in range(8):
                mm(psf[p][:, 0:512], W3[:, k, gi * 128:(gi + 1) * 128], hT3[:, k, :], k == 0, k == 7, hk3 + [("W3c", "u")], [PK[p]])
            add(ACT, lambda e: e.copy(out=UT[:, gi, 16:528], in_=psf[p][:, 0:512]), [PK[p]], [("UT", gi)])
            src = UT[:, gi, :]
            srck = ("UT", gi)
            lo = 0
            step = 1
            j = 0
            while step < w:
                lo += step
                dstt = ptg[gi][j % 2]
                add(POOL if gi < 2 else DVE, lambda e: e.tensor_tensor(out=dstt[:, lo:528], in0=src[:, lo:528],
                                                                      in1=src[:, lo - step:528 - step], op=ALU.add),
                    [srck], [("pt", gi // 2, j % 2)])
                src = dstt[:, :]
                srck = ("pt", gi // 2, j % 2)
                step *= 2
                j += 1
            add(DVE, lambda e: e.scalar_tensor_tensor(out=pooled[:, gi, :], in0=src[:, 16:528], scalar=1.0 / w,
                                                      in1=UT[:, gi, 16:528], op0=ALU.mult, op1=ALU.subtract),
                [srck, ("UT", gi)], [("pooled", gi)])
            if I == 0:
                for t in range(w - 1):
                    add(DVE, lambda e: e.scalar_tensor_tensor(
                        out=pooled[:, gi, t:t + 1], in0=src[:, 16 + t:17 + t], scalar=1.0 / (t + 1), in1=UT[:, gi, 16 + t:17 + t],
                        op0=ALU.mult, op1=ALU.subtract), [srck, ("UT", gi)], [("pooled", gi)])

    def p3_pool_mix(I):
        for gi in range(4):
            p2 = nb()
            mm(psf[p2][:, 0:512], PW[:, gi, :], pooled[:, gi, :], True, True, ["PW", ("pooled", gi)], [PK[p2]])
            add(DVE, lambda e: e.tensor_scalar(out=mixed[:, gi, :], in0=psf[p2][:, 0:512], scalar1=pscl[:, gi:gi + 1],
                                               scalar2=None, op0=ALU.mult), [PK[p2], "pscl"], [("mixed", gi)])

    def p3_merge(I):
        for dc in range(8):
            j = dc % 2
            pa, pb_, pga, pgb = nb(), nb(), nb(), nb()
            dsl = slice(dc * 128, (dc + 1) * 128)
            for k in range(8):
                mm(psf[pga][:, 0:512], W3[:, k, 512 + dc * 128:512 + (dc + 1) * 128], hT3[:, k, :], k == 0, k == 7,
                   hk3 + [("W3c", "a", dc // 2)], [PK[pga]])
            for k in range(8):
                mm(psf[pgb][:, 0:512], W3[:, k, 1536 + dc * 128:1536 + (dc + 1) * 128], hT3[:, k, :], k == 0, k == 7,
                   hk3 + [("W3c", "b", dc // 2)], [PK[pgb]])
            for k in range(4):
                mm(psf[pa][:, 0:512], WBA[:, k, dsl], OT[:, k, I * 512:(I + 1) * 512], k == 0, k == 3, ["WBA", ("OT", I)], [PK[pa]])
            for k in range(4):
                mm(psf[pb_][:, 0:512], WBP[:, k, dsl], mixed[:, k, :], k == 0, k == 3, ["WBP"] + mk3, [PK[pb_]])
            add(ACT, lambda e: e.activation(out=sga[j][:], in_=psf[pga][:, 0:512], func=AF.Sigmoid), [PK[pga]], [("sga", j)])
            add(ACT, lambda e: e.activation(out=sgb[j][:], in_=psf[pgb][:, 0:512], func=AF.Sigmoid), [PK[pgb]], [("sgb", j)])
            add(DVE, lambda e: e.tensor_tensor(out=sga[j][:], in0=sga[j][:], in1=psf[pa][:, 0:512], op=ALU.mult),
                [("sga", j), PK[pa]], [("sga", j)])
            add(DVE, lambda e: e.tensor_tensor(out=sgb[j][:], in0=sgb[j][:], in1=psf[pb_][:, 0:512], op=ALU.mult),
                [("sgb", j), PK[pb_]], [("sgb", j)])
            add(POOL, lambda e: e.tensor_tensor(out=merged[:, dc, :], in0=sga[j][:], in1=sgb[j][:], op=ALU.add),
                [("sga", j), ("sgb", j)], [("merged", dc)])

    def p3_out_tile(I, tl):
        xb = I % 2
        tt = 4 * I + tl
        mgk = [("merged", dc) for dc in range(8)]
        for half in range(2):
            p = nb()
            for k in range(8):
                mm(psf[p][:, 0:512], merged[:, k, tl * 128:(tl + 1) * 128], WOUT[:, k, half * 512:(half + 1) * 512],
                   k == 0, k == 7, mgk + ["WOUT"], [PK[p]])
            add(DVE, lambda e: e.tensor_tensor(
                out=XB2[xb][:, tl, half * 512:(half + 1) * 512], in0=XB2[xb][:, tl, half * 512:(half + 1) * 512],
                in1=psf[p][:, 0:512], op=ALU.add), [("XB", xb, tl), PK[p]], [("XB", xb, tl)])
        add(SP, lambda e: e.dma_start(out=x1_d[tt * 128:(tt + 1) * 128, :], in_=XB2[xb][:, tl, :]),
            [("XB", xb, tl)], [("x1d", tt)])

    p3_load(0)
    for tl in range(4):
        p3_norm_tile(0, tl)
    p3_pool_u(0)
    p3_pool_mix(0)
    p3_rest_of_weights()
    for I in range(NB):
        p3_merge(I)
        if I + 1 < NB:
            p3_load(I + 1)
        for tl in range(4):
            if I + 1 < NB:
                p3_norm_tile(I + 1, tl)
                if tl == 3:
                    p3_pool_u(I + 1)
            p3_out_tile(I, tl)
        if I + 1 < NB:
            p3_pool_mix(I + 1)

    if stop_after < 5:
        return finish()
    S.barrier()
    A.release(m_l1)
    WF1 = A([128, 8, DFF], BF16)
    WF2 = A([128, 32, D], BF16)
    X1 = [A([128, 2, D], F32), A([128, 2, D], F32)]
    hb4 = A([128, D], BF16)
    h2T = A([128, 8, 256], BF16)
    fT = A([128, 32, 256], BF16)
    st4 = A([128, 8], F32)
    rl = [A([128, 256], F32), A([128, 256], F32)]
    wff1_v = wff1_d.rearrange("(k p) n -> p k n", p=128)
    wff2_v = wff2_d.rearrange("(k p) n -> p k n", p=128)
    for fg in range(8):
        add(PQ, lambda e: e.dma_start(out=WF1[:, :, fg * 512:(fg + 1) * 512], in_=wff1_v[:, :, fg * 512:(fg + 1) * 512]),
            [], [("WF1c", fg)])
    for k2 in range(16):
        add(PQ, lambda e: e.dma_start(out=WF2[:, 2 * k2:2 * k2 + 2, :], in_=wff2_v[:, 2 * k2:2 * k2 + 2, :]), [], [("WF2", k2)])
    WF1K = [("WF1", k) for k in range(8)]

    NB4 = S_LEN // 256
    h2Tb = [h2T, A([128, 8, 256], BF16)]
    st4b = [st4, A([128, 8], F32)]

    hb4d = [hb4, A([128, D], BF16)]

    def p4_stageA_norm(J):
        b = J % 2
        for tl in range(2):
            tt = 2 * J + tl
            add(SP, lambda e: e.dma_start(out=X1[b][:, tl, :], in_=x1_d[tt * 128:(tt + 1) * 128, :]),
                [("x1d", tt)], [("X1", b, tl)])
        for tl in range(2):
            rms_stats(X1[b][:, tl, :], ("X1", b, tl), st4b[b][:, 2 * tl:2 * tl + 1], st4b[b][:, 2 * tl + 1:2 * tl + 2], ("st4", b, tl))
            add(DVE, lambda e: e.scalar_tensor_tensor(out=hb4d[tl][:], in0=X1[b][:, tl, :], scalar=st4b[b][:, 2 * tl + 1:2 * tl + 2],
                                                      in1=gmlp[:], op0=ALU.mult, op1=ALU.mult),
                [("X1", b, tl), ("st4", b, tl), "gmlp"], [("hb4", tl)])

    def p4_stageA_tr(J):
        b = J % 2
        for tl in range(2):
            tb = nb()
            for k in range(8):
                add(PE, lambda e: e.transpose(out=psb[tb][:, k * 128:(k + 1) * 128], in_=hb4d[tl][:, k * 128:(k + 1) * 128],
                                              identity=identb[:]), [("hb4", tl), "identb"], [PK[tb]])
            add(ACT, lambda e: e.copy(out=h2Tb[b][:, :, tl * 128:(tl + 1) * 128],
                                      in_=psb[tb][:, :].rearrange("p (k t) -> p k t", k=8)), [PK[tb]], [("h2T", b, tl)])

    def p4_stageB(J):
        b = J % 2
        hk = [("h2T", b, 0), ("h2T", b, 1)]
        for fc in range(32):
            p = nb()
            for k in range(8):
                mm(psf[p][:, 0:256], WF1[:, k, fc * 128:(fc + 1) * 128], h2Tb[b][:, k, :], k == 0, k == 7, hk + [("WF1c", fc // 4)], [PK[p]])
            rj = fc % 2
            add(ACT, lambda e: e.activation(out=rl[rj][:], in_=psf[p][:, 0:256], func=AF.Relu), [PK[p]], [("rl", rj)])
            add(DVE if rj == 0 else POOL, lambda e: e.tensor_tensor(out=fT[:, fc, :], in0=rl[rj][:], in1=rl[rj][:], op=ALU.mult),
                [("rl", rj)], [("fT", fc)])
            if fc == 10 and J + 1 < NB4:
                p4_stageA_norm(J + 1)

    def p4_stageC(J):
        b = J % 2
        ftk = [("fT", fc) for fc in range(32)]
        for tl in range(2):
            tt = 2 * J + tl
            for half in range(2):
                p = nb()
                for fc in range(32):
                    mm(psf[p][:, 0:512], fT[:, fc, tl * 128:(tl + 1) * 128], WF2[:, fc, half * 512:(half + 1) * 512],
                       fc == 0, fc == 31, ftk + [("WF2", fc // 2)], [PK[p]])
                add(DVE, lambda e: e.tensor_tensor(
                    out=X1[b][:, tl, half * 512:(half + 1) * 512], in0=X1[b][:, tl, half * 512:(half + 1) * 512],
                    in1=psf[p][:, 0:512], op=ALU.add), [("X1", b, tl), PK[p]], [("X1", b, tl)])
            rms_stats(X1[b][:, tl, :], ("X1", b, tl), st4b[b][:, 4 + 2 * tl:5 + 2 * tl], st4b[b][:, 5 + 2 * tl:6 + 2 * tl], ("st4f", b, tl))
            add(DVE, lambda e: e.scalar_tensor_tensor(out=X1[b][:, tl, :], in0=X1[b][:, tl, :],
                                                      scalar=st4b[b][:, 5 + 2 * tl:6 + 2 * tl], in1=gfin[:], op0=ALU.mult,
                                                      op1=ALU.mult), [("X1", b, tl), ("st4f", b, tl), "gfin"], [("X1", b, tl)])
            add(SP, lambda e: e.dma_start(out=out_d[tt * 128:(tt + 1) * 128, :], in_=X1[b][:, tl, :]),
                [("X1", b, tl)], [("out", tt)])

    p4_stageA_norm(0)
    p4_stageA_tr(0)
    for J in range(NB4):
        p4_stageB(J)
        if J + 1 < NB4:
            p4_stageA_tr(J + 1)
        p4_stageC(J)

    return finish()


_INPUT_ORDER = ["x", "norm_mix", "w_in", "cmp_pe_k", "cmp_pe_v", "cmp_k_w1", "cmp_k_w2", "cmp_v_w1", "cmp_v_w2",
                "w_branch_attn", "pool_w", "pool_scale", "w_branch_pool", "w_out", "norm_mlp", "w_ff1", "w_ff2", "norm_final"]


def kernel(**inputs):
    f = lambda a: np.ascontiguousarray(np.asarray(a, dtype=np.float32))
    x = f(inputs["x"])
    shared = {
        "norm_mix": f(inputs["norm_mix"]).reshape(1, D),
        "w_in": f(inputs["w_in"]).reshape(D, 3864),
        "cmp_pe_k": f(inputs["cmp_pe_k"]).reshape(32, 64),
        "cmp_pe_v": f(inputs["cmp_pe_v"]).reshape(32, 64),
        "cmp_k_w1": f(inputs["cmp_k_w1"]).reshape(2048, 256),
        "cmp_k_w2": f(inputs["cmp_k_w2"]).reshape(256, 64),
        "cmp_v_w1": f(inputs["cmp_v_w1"]).reshape(2048, 256),
        "cmp_v_w2": f(inputs["cmp_v_w2"]).reshape(256, 64),
        "w_branch_attn": f(inputs["w_branch_attn"]).reshape(512, D),
        "pool_w": f(inputs["pool_w"]).reshape(512, 128),
        "pool_scale": f(inputs["pool_scale"]).reshape(512, 1),
        "w_branch_pool": f(inputs["w_branch_pool"]).reshape(512, D),
        "w_out": f(inputs["w_out"]).reshape(D, D),
        "norm_mlp": f(inputs["norm_mlp"]).reshape(1, D),
        "w_ff1": f(inputs["w_ff1"]).reshape(D, DFF),
        "w_ff2": f(inputs["w_ff2"]).reshape(DFF, D),
        "norm_final": f(inputs["norm_final"]).reshape(1, D),
    }
    nc = build_nc()
    in_maps = []
    for c in range(8):
        m = dict(shared)
        m["x"] = np.ascontiguousarray(x[c])
        in_maps.append(m)
    res = run_bass_kernel_spmd(nc, in_maps, core_ids=list(range(8)))
    return np.stack([np.asarray(r["out"], dtype=np.float32).reshape(S_LEN, D) for r in res.results], axis=0)
```
